# Optimizing a Trainium2 kernel written in Bass

```python
import math
import jax, jax.numpy as jnp
from jax import lax
import numpy as np

D_MODEL = 2048
BATCH = 4
SEQ = 2048
DEPTH = 2
DEC_BATCH = 32
DEC_SEQ = 32
PAST_LEN = 4096

CHUNK = 64
N_MIXERS = 2
N_A_LAYERS = (DEPTH + 1) // 2
N_B_LAYERS = DEPTH // 2

GMLP_CHUNK = 128
GMLP_HALF = D_MODEL
GMLP_GROUPS = 8
GMLP_GROUP_DIM = GMLP_HALF // GMLP_GROUPS

RET_HEADS = 8
RET_DK = D_MODEL // RET_HEADS
RET_DV = 2 * RET_DK
RET_QK = RET_HEADS * RET_DK
RET_V = RET_HEADS * RET_DV
ROPE_BASE = 10000.0

FFN_HIDDEN = ((8 * D_MODEL + 3 * 256 - 1) // (3 * 256)) * 256

ALPHA = (2 * DEPTH) ** 0.25
BETA = (8 * DEPTH) ** -0.25
LN_EPS = 1e-5
GN_EPS = 1e-6

kernel_name = "hybrid_gmlp_retention_stream_step"


def layer_norm(x, g, b):
    xf = x.astype(jnp.float32)
    mu = jnp.mean(xf, axis=-1, keepdims=True)
    var = jnp.mean(jnp.square(xf - mu), axis=-1, keepdims=True)
    y = (xf - mu) * lax.rsqrt(var + LN_EPS) * g.astype(jnp.float32) + b.astype(jnp.float32)
    return y.astype(x.dtype)


def gmlp_mix(x, w_in, ln_g, ln_b, w_s, b_s, w_out):
    bsz, t, _ = x.shape
    seg = min(t, GMLP_CHUNK)
    n = t // seg
    z = jax.nn.gelu(x @ w_in, approximate=False)
    u, v = jnp.split(z, 2, axis=-1)
    v = layer_norm(v, ln_g, ln_b)
    mask = jnp.tril(jnp.ones((seg, seg), dtype=bool))
    w = jnp.where(mask[None], w_s[:, :seg, :seg], 0.0).astype(x.dtype)
    vg = v.reshape(bsz, n, seg, GMLP_GROUPS, GMLP_GROUP_DIM)
    bias = jnp.swapaxes(b_s[:, :seg], 0, 1)[None, None, :, :, None].astype(x.dtype)
    mixed = jnp.einsum('gts,bnsgc->bntgc', w, vg) + bias
    s = u * mixed.reshape(bsz, t, GMLP_HALF)
    return s @ w_out, v


def rope(x, pos):
    half = RET_DK // 2
    inv = jnp.power(ROPE_BASE, -jnp.arange(half, dtype=jnp.float32) / half)
    ang = pos.astype(jnp.float32)[:, None] * inv[None, :]
    cos, sin = jnp.cos(ang), jnp.sin(ang)
    x1, x2 = x[..., :half], x[..., half:]
    return jnp.concatenate([x1 * cos - x2 * sin, x1 * sin + x2 * cos], axis=-1)


def retention_block(q, k, v, s0, decay, inner, zeta, blk_decay):
    scores = jnp.einsum('bhqd,bhkd->bhqk', q, k) * decay[None]
    intra = jnp.einsum('bhqk,bhkv->bhqv', scores, v)
    cross = jnp.einsum('bhqd,bhdv->bhqv', q, s0) * inner[None, :, :, None]
    s_new = blk_decay[None, :, None, None] * s0 + jnp.einsum(
        'bhkd,bhkv->bhdv', k * zeta[None, :, :, None], v)
    return intra + cross, s_new


def retention_mix(x, pos, s0, w_in, w_out):
    bsz, t, _ = x.shape
    proj = x @ w_in
    q, k, v, g = jnp.split(proj, [RET_QK, 2 * RET_QK, 2 * RET_QK + RET_V], axis=-1)

    def heads(a, d):
        return a.reshape(bsz, t, RET_HEADS, d).transpose(0, 2, 1, 3).astype(jnp.float32)

    q = rope(heads(q, RET_DK), pos) * (RET_DK ** -0.5)
    k = rope(heads(k, RET_DK), pos)
    v = heads(v, RET_DV)

    blk = min(t, CHUNK)
    n = t // blk
    log_gamma = jnp.log1p(-jnp.power(2.0, -5.0 - jnp.arange(RET_HEADS, dtype=jnp.float32)))
    idx = jnp.arange(blk, dtype=jnp.float32)
    diff = idx[:, None] - idx[None, :]
    decay = jnp.where(diff[None] >= 0,
                      jnp.exp(log_gamma[:, None, None] * jnp.maximum(diff, 0.0)[None]), 0.0)
    inner = jnp.exp(log_gamma[:, None] * (idx + 1.0)[None])
    zeta = jnp.exp(log_gamma[:, None] * (blk - 1.0 - idx)[None])
    blk_decay = jnp.exp(log_gamma * blk)

    def to_blocks(a):
        return a.reshape(bsz, RET_HEADS, n, blk, a.shape[-1]).transpose(2, 0, 1, 3, 4)

    def step(s, qkv):
        o, s = retention_block(qkv[0], qkv[1], qkv[2], s, decay, inner, zeta, blk_decay)
        return s, o

    s_final, o = lax.scan(step, s0.astype(jnp.float32), (to_blocks(q), to_blocks(k), to_blocks(v)))
    o = o.transpose(1, 2, 0, 3, 4).reshape(bsz, RET_HEADS, t, RET_DV)
    mu = jnp.mean(o, axis=-1, keepdims=True)
    var = jnp.mean(jnp.square(o - mu), axis=-1, keepdims=True)
    o = (o - mu) * lax.rsqrt(var + GN_EPS)
    o = o.transpose(0, 2, 1, 3).reshape(bsz, t, RET_V)
    y = (jax.nn.silu(g.astype(jnp.float32)) * o).astype(x.dtype) @ w_out
    return y, s_final.astype(x.dtype)


def swiglu(x, w_in, w_out):
    gate, up = jnp.split(x @ w_in, 2, axis=-1)
    return (jax.nn.silu(gate) * up) @ w_out


def trunk(x, pos, states, w_a_in, a_ln_g, a_ln_b, a_ws, a_bs, w_a_out, w_b_in, w_b_out,
          w_ffn_in, w_ffn_out, ln_mix_g, ln_mix_b, ln_ffn_g, ln_ffn_b):
    new_states, v_rows = [], []
    for i in range(DEPTH):
        j = i // N_MIXERS
        if i % N_MIXERS == 0:
            h, v = gmlp_mix(x, w_a_in[j], a_ln_g[j], a_ln_b[j], a_ws[j], a_bs[j], w_a_out[j])
            v_rows.append(v)
        else:
            h, s = retention_mix(x, pos, states[j], w_b_in[j], w_b_out[j])
            new_states.append(s)
        x = layer_norm(ALPHA * x + h, ln_mix_g[i], ln_mix_b[i])
        x = layer_norm(ALPHA * x + swiglu(x, w_ffn_in[i], w_ffn_out[i]), ln_ffn_g[i], ln_ffn_b[i])
    return x, jnp.stack(new_states), jnp.stack(v_rows)


def setup_inputs(seed: int = 0) -> dict:
    key = jax.random.key(seed)
    ks = jax.random.split(key, 20)
    f32 = jnp.float32
    nrm = lambda k, shape, scale: jax.random.normal(k, shape, f32) * scale
    return {
        "x_prompt": nrm(ks[0], (BATCH, SEQ, D_MODEL), 1.0),
        "x_sample": nrm(ks[1], (DEC_BATCH, DEC_SEQ, D_MODEL), 1.0),
        "state_ret": nrm(ks[2], (N_B_LAYERS, DEC_BATCH, RET_HEADS, RET_DK, RET_DV), 1.0),
        "w_a_in": nrm(ks[3], (N_A_LAYERS, D_MODEL, 2 * GMLP_HALF), D_MODEL ** -0.5),
        "a_ln_g": 1.0 + nrm(ks[4], (N_A_LAYERS, GMLP_HALF), 0.02),
        "a_ln_b": nrm(ks[5], (N_A_LAYERS, GMLP_HALF), 0.02),
        "a_ws": nrm(ks[6], (N_A_LAYERS, GMLP_GROUPS, GMLP_CHUNK, GMLP_CHUNK), GMLP_CHUNK ** -0.5),
        "a_bs": 1.0 + nrm(ks[7], (N_A_LAYERS, GMLP_GROUPS, GMLP_CHUNK), 0.1),
        "w_a_out": nrm(ks[8], (N_A_LAYERS, GMLP_HALF, D_MODEL), BETA * GMLP_HALF ** -0.5),
        "w_b_in": nrm(ks[9], (N_B_LAYERS, D_MODEL, 2 * RET_QK + 2 * RET_V), D_MODEL ** -0.5),
        "w_b_out": nrm(ks[10], (N_B_LAYERS, RET_V, D_MODEL), BETA * RET_V ** -0.5),
        "w_ffn_in": nrm(ks[11], (DEPTH, D_MODEL, 2 * FFN_HIDDEN), D_MODEL ** -0.5),
        "w_ffn_out": nrm(ks[12], (DEPTH, FFN_HIDDEN, D_MODEL), BETA * FFN_HIDDEN ** -0.5),
        "ln_mix_g": 1.0 + nrm(ks[13], (DEPTH, D_MODEL), 0.02),
        "ln_mix_b": nrm(ks[14], (DEPTH, D_MODEL), 0.02),
        "ln_ffn_g": 1.0 + nrm(ks[15], (DEPTH, D_MODEL), 0.02),
        "ln_ffn_b": nrm(ks[16], (DEPTH, D_MODEL), 0.02),
    }


def reference(x_prompt, x_sample, state_ret, w_a_in, a_ln_g, a_ln_b, a_ws, a_bs, w_a_out,
              w_b_in, w_b_out, w_ffn_in, w_ffn_out, ln_mix_g, ln_mix_b, ln_ffn_g, ln_ffn_b):
    pos_prompt = jnp.arange(x_prompt.shape[1], dtype=jnp.int32)
    pos_sample = PAST_LEN + jnp.arange(x_sample.shape[1], dtype=jnp.int32)
    zero_state = jnp.zeros((N_B_LAYERS, x_prompt.shape[0], RET_HEADS, RET_DK, RET_DV), x_prompt.dtype)
    y_prompt, ret_state_prompt, _ = trunk(
        x_prompt, pos_prompt, zero_state, w_a_in, a_ln_g, a_ln_b, a_ws, a_bs, w_a_out,
        w_b_in, w_b_out, w_ffn_in, w_ffn_out, ln_mix_g, ln_mix_b, ln_ffn_g, ln_ffn_b)
    y_sample, ret_state_sample, gmlp_v_sample = trunk(
        x_sample, pos_sample, state_ret, w_a_in, a_ln_g, a_ln_b, a_ws, a_bs, w_a_out,
        w_b_in, w_b_out, w_ffn_in, w_ffn_out, ln_mix_g, ln_mix_b, ln_ffn_g, ln_ffn_b)
    return (y_prompt, y_sample, ret_state_prompt, ret_state_sample, gmlp_v_sample)
```

```python
import numpy as np
import concourse.bass as bass
import concourse.mybir as mybir
from concourse.bass_utils import run_bass_kernel_spmd

F32 = mybir.dt.float32
BF16 = mybir.dt.bfloat16
AF = mybir.ActivationFunctionType
ALU = mybir.AluOpType
P = 128
D = 2048
NTL = 9
NPT = 8
NT = NTL * P
FH = 5632
ALPHA = 4.0 ** 0.25
LN_EPS = 1e-5
GN_EPS = 1e-6
NSLOT = 3
GAM = [1.0 - 2.0 ** (-5.0 - h) for h in range(8)]


class Buf:
    __slots__ = ("name", "w", "r")

    def __init__(self, name):
        self.name = name
        self.w = {}
        self.r = {}


class Sched:
    ENG = ("pe", "act", "dve", "pool", "sp")

    def __init__(self, nc):
        self.nc = nc
        self.ops = []
        self.semh = {}
        self.semc = {}
        self.cur = {e: 0 for e in self.ENG}

    def _sem(self, key):
        if key not in self.semh:
            self.semh[key] = self.nc.alloc_semaphore(key)
            self.semc[key] = 0
        return key

    def op(self, eng, fn, reads=(), writes=(), dkey=None):
        waits = {}

        def add(evs):
            for s, v in evs.items():
                if waits.get(s, 0) < v:
                    waits[s] = v

        for b in reads:
            add(b.w)
        for b in writes:
            add(b.r)
            add(b.w)
        if dkey is not None:
            key = self._sem("d_" + dkey)
            inc = 16
        else:
            key = self._sem(f"{eng}{self.cur[eng]}")
            if self.semc[key] >= 2000:
                self.cur[eng] += 1
                key = self._sem(f"{eng}{self.cur[eng]}")
            inc = 1
        self.semc[key] += inc
        v = self.semc[key]
        if eng == "pe":
            waits = {s: x for s, x in waits.items() if not s.startswith("pe")}
        self.ops.append((eng, fn, waits, key, inc))
        for b in reads:
            if b not in writes:
                if b.r.get(key, 0) < v:
                    b.r[key] = v
        for b in writes:
            if b.r:
                b.w = {key: v}
                b.r = {}
            else:
                b.w[key] = v
        return (key, v)

    def emit(self, block):
        engs = {"pe": block.tensor, "act": block.scalar, "dve": block.vector, "pool": block.gpsimd, "sp": block.sync}
        for ename, deco in engs.items():
            myops = [o for o in self.ops if o[0] == ename]

            def body(eng, myops=myops):
                seen = {}
                for (_, fn, waits, key, inc) in myops:
                    for s, v in waits.items():
                        if seen.get(s, 0) >= v:
                            continue
                        seen[s] = v
                        eng.wait_ge(self.semh[s], v)
                    ins = fn(eng)
                    ins.then_inc(self.semh[key], inc)

            deco(body)


def build_nc(exchange=True, debug=False):
    nc = bass.Bass("TRN2", target_bir_lowering=False)
    S = Sched(nc)

    def din(name, shape, dt=F32):
        return nc.dram_tensor(name, list(shape), dt, kind="ExternalInput").ap()

    def dout(name, shape, dt=F32):
        return nc.dram_tensor(name, list(shape), dt, kind="ExternalOutput").ap()

    def dscr(name, shape, dt=F32):
        if debug and name in ("xs0", "xs1", "xs2", "qd", "kd", "vd", "gd", "zscr"):
            return nc.dram_tensor(name, list(shape), dt, kind="ExternalOutput").ap()
        return nc.dram_tensor(name, list(shape), dt).ap()

    xin = din("xin", [NT, D])
    s0in = din("s0in", [4 * 8 * 256, 512])
    w_a_in = din("w_a_in", [D, 4096])
    w_a_out = din("w_a_out", [D, D])
    w_b_in = din("w_b_in", [D, 12288])
    w_b_out = din("w_b_out", [4096, D])
    w_ffn_in = [din(f"w_ffn_in{l}", [D, 2 * FH]) for l in range(2)]
    w_ffn_out = [din(f"w_ffn_out{l}", [FH, D]) for l in range(2)]
    a_ws = din("a_ws", [8 * 128, 128])
    a_bs = din("a_bs", [8, 128])
    lnv = din("lnv", [10, D])
    c_ident = din("c_ident", [P, P])
    c_cs = din("c_cs", [P, NTL * 256])
    c_dqk = din("c_dqk", [P, 32])
    c_mask = din("c_mask", [P, 256])
    c_rowm = din("c_rowm", [P, 4])
    c_flag = din("c_flag", [P, 1])

    yout = dout("yout", [NT, D])
    sout_s = dout("sout_s", [4 * 8 * 256, 512])
    sout_p = dout("sout_p", [8 * 256, 512])
    vout = dout("vout", [P, D])

    zscr = dscr("zscr", [NT, 4096])
    xs = [dscr(f"xs{i}", [NT, D]) for i in range(3)]
    Rs = [dscr(f"R{i}", [NT, D]) for i in range(3)]
    qd = dscr("qd", [NT, D], BF16)
    kd = dscr("kd", [NT, D], BF16)
    vd = dscr("vd", [NT, 4096], BF16)
    gd = dscr("gd", [NT, 4096], BF16)
    sxh = [dscr(f"sx{h}", [256, 512]) for h in range(8)]
    gxh = [dscr(f"gx{h}", [512, 512]) for h in range(8)]

    off = [16512]

    def sb(name, shape, dt, at=None):
        nbytes = int(np.prod(shape[1:])) * (2 if dt == BF16 else 4)
        if at is None:
            o = off[0]
            off[0] += (nbytes + 31) // 32 * 32
        else:
            o = at
        return nc.alloc_sbuf_tensor_at(name, list(shape), dt, offset=o)

    ident = sb("ident", [P, P], BF16)
    identf = sb("identf", [P, P], F32)
    mask = sb("mask", [P, 2, P], F32)
    cs = sb("cs", [P, NTL, 256], F32)
    dqk = sb("dqk", [P, 32], F32)
    rowm = sb("rowm", [P, 4], F32)
    flag = sb("flag", [P, 1], F32)
    WgT = sb("WgT", [P, 2, 8, P], BF16)
    bcol = sb("bcol", [P, 2, 8], F32)
    st6 = sb("st6", [P, 4, 6], F32)
    mv = sb("mv", [P, 2], F32)
    rstd = sb("rstd", [P, 1], F32)
    nmr = sb("nmr", [P, 1], F32)
    sc = sb("sc", [P, P], BF16)
    xT = sb("xT", [P, 16, NT], BF16)
    XT_OFF = off[0] - 16 * NT * 2
    slots = [sb(f"slot{i}", [P, 16, 512], BF16) for i in range(NSLOT)]
    A = sb("A", [P, 16, NT], BF16)
    stg = [sb(f"stg{i}", [P, 512], F32) for i in range(4)]
    ug = [sb(f"ug{i}", [P, 3, 384], F32) for i in range(2)]
    xb2 = sb("xb2", [P, D], BF16)
    C_OFF = off[0]
    C_SIZE = 49152
    off[0] += C_SIZE
    assert off[0] <= 229344, off[0]
    xa = [sb(f"xa{i}", [P, D], F32, at=C_OFF + i * 8192) for i in range(2)]
    rb = sb("rb", [P, D], F32, at=C_OFF + 16384)
    xb = [sb("xb0", [P, D], BF16, at=C_OFF + 24576), sb("xb1", [P, D], BF16, at=C_OFF + 45056), xb2]
    gbc = sb("gbc", [P, D], F32, at=C_OFF + 28672)
    bbc = sb("bbc", [P, D], F32, at=C_OFF + 36864)
    gu = xa[0]
    gv = xa[1]
    gvb = sb("gvb", [P, D], BF16, at=C_OFF + 16384)
    gs = sb("gs", [P, D], BF16, at=C_OFF + 24576)
    rq = [sb(f"rq{i}", [P, 1024], BF16, at=C_OFF + i * 12288) for i in range(2)]
    rk = [sb(f"rk{i}", [P, 1024], BF16, at=C_OFF + i * 12288 + 2048) for i in range(2)]
    rv = [sb(f"rv{i}", [P, 2048], BF16, at=C_OFF + i * 12288 + 4096) for i in range(2)]
    rg = [sb(f"rg{i}", [P, 2048], BF16, at=C_OFF + i * 12288 + 8192) for i in range(2)]
    qT = sb("qT", [P, 8, P], BF16, at=C_OFF + 24576)
    kT = sb("kT", [P, 8, P], BF16, at=C_OFF + 26624)
    qTz = sb("qTz", [P, 4, 8, P], BF16, at=C_OFF + 28672)
    ktz = sb("ktz", [P, 1024], BF16, at=C_OFF + 36864)
    yin = sb("yin", [P, 2048], BF16, at=C_OFF + 38912)
    onr = sb("onr", [P, 512], F32, at=C_OFF + 43008)
    Sf = sb("Sf", [P, 8, 512], F32, at=XT_OFF)
    Sb = sb("Sb", [P, 8, 512], BF16, at=XT_OFF + 16384)
    osb = sb("osb", [P, 4, 512], F32, at=XT_OFF + 24576)

    pbank = [nc.alloc_psum_tensor(f"pb{i}", [P, 512], F32) for i in range(8)]
    bankB = [Buf(f"bank{i}") for i in range(8)]
    bank_i = [0]

    def next_bank():
        i = bank_i[0] % 8
        bank_i[0] += 1
        return pbank[i], bankB[i]

    slotB = [Buf(f"slot{i}") for i in range(NSLOT)]
    slot_i = [0]

    def next_slot():
        i = slot_i[0] % NSLOT
        slot_i[0] += 1
        return i

    xTB = [Buf(f"xT{t}") for t in range(NTL)]
    AB = [Buf(f"A{j}") for j in range(16)]
    stgB = [Buf(f"stg{i}") for i in range(4)]
    stg_i = [0]
    ugB = [Buf(f"ug{i}") for i in range(2)]
    ug_i = [0]
    B = {}

    def bf(name):
        if name not in B:
            B[name] = Buf(name)
        return B[name]

    cbufs = [bf(n) for n in ("xa0", "xa1", "rb", "xb0", "xb1", "gbc", "bbc")]
    const_b = bf("const")
    stat_b = bf("stat")

    deferred = []

    DEPTH = 2

    def flush(n=1):
        while len(deferred) > DEPTH:
            deferred.pop(0)()

    def flush_all():
        while deferred:
            deferred.pop(0)()

    def wblock_src(w, r0, kc, c0):
        return w[r0:r0 + kc * P, c0:c0 + 512].rearrange("(kc p) n -> p kc n", p=P)

    def load_slot(w, r0, kc, c0):
        si = next_slot()
        src = wblock_src(w, r0, kc, c0)
        S.op("pool", lambda e: e.dma_start(out=slots[si][:, 0:kc, :], in_=src), writes=[slotB[si]], dkey=f"slot{si}")
        return si

    def ld(dst_ap, src_ap, b, key, eng="sp", **kw):
        S.op(eng, lambda e: e.dma_start(out=dst_ap, in_=src_ap, **kw), writes=[b], dkey=key)

    ld(identf[:], c_ident, const_b, "c0")
    ld(mask[:].rearrange("p a b -> p (a b)"), c_mask, const_b, "c1")
    ld(cs[:].rearrange("p a b -> p (a b)"), c_cs, const_b, "c2")
    ld(dqk[:], c_dqk, const_b, "c3")
    ld(rowm[:], c_rowm, const_b, "c4")
    ld(flag[:], c_flag, const_b, "c5")
    S.op("act", lambda e: e.copy(ident[:], identf[:]), reads=[const_b], writes=[bf("ident")])
    identB = bf("ident")

    def transposes(src_fn, n, dst_fn, src_bufs, dst_bufs, evac="dve", post=None):
        for g0 in range(0, n, 8):
            cnt = min(8, n - g0)
            bk, bb = next_bank()
            bkv = bk[:].bitcast(BF16)

            def pe(e, g0=g0, cnt=cnt, bkv=bkv):
                ins = None
                for j in range(cnt):
                    ins = e.transpose(bkv[:, j * P:(j + 1) * P], src_fn(g0 + j), ident[:])
                return ins

            S.op("pe", pe, reads=list(src_bufs) + [identB], writes=[bb])
            dst = dst_fn(g0, cnt) if dst_fn is not None else None
            src = bkv[:, 0:cnt * P].rearrange("p (a b) -> p a b", b=P)
            if post is not None:
                post(g0, cnt, src, bb)
            elif evac == "dve":
                S.op("dve", lambda e, dst=dst, src=src: e.tensor_copy(dst, src), reads=[bb], writes=list(dst_bufs))
            else:
                S.op("act", lambda e, dst=dst, src=src: e.copy(dst, src), reads=[bb], writes=list(dst_bufs))

    wtmp = gu
    wtb = gvb
    gB = [bf("xa0"), bf("rb")]
    for var in range(2):
        if var == 0:
            S.op("sp", lambda e: e.dma_start(out=wtmp[:, 0:1024].rearrange("p (g s) -> p g s", s=P),
                                             in_=a_ws.rearrange("(g t) s -> t g s", t=P)), writes=[gB[0]], dkey="xa0")
        else:
            S.op("dve", lambda e: e.memset(wtmp[:, 0:1024], 0.0), writes=[gB[0]])
            for i in range(4):
                S.op("sp", lambda e, i=i: e.dma_start(
                    out=wtmp[32 * i:32 * i + 32, 0:1024].rearrange("p (g s) -> p g s", s=P)[:, :, 32 * i:32 * i + 32],
                    in_=a_ws.rearrange("(g t) s -> t g s", t=P)[0:32, :, 0:32]), writes=[gB[0]], dkey=f"wd{i}")
        S.op("act", lambda e: e.copy(wtb[:, 0:1024], wtmp[:, 0:1024]), reads=[gB[0]], writes=[gB[1]])

        def post(g0, cnt, src, bb, var=var):
            S.op("dve", lambda e: e.tensor_tensor(WgT[:, var, g0:g0 + cnt, :], src,
                                                  mask[:, var:var + 1, :].to_broadcast([P, cnt, P]), ALU.mult),
                 reads=[bb, const_b], writes=[bf("WgT")])

        transposes(lambda j: wtb[:, j * P:(j + 1) * P], 8, None, [gB[1]], [], post=post)
    ld(bcol[:, 0, :], a_bs.rearrange("g p -> p g"), bf("bcol"), "c6", allow_slow_non_contiguous=True)
    for i in range(4):
        ld(bcol[32 * i:32 * i + 32, 1, :], a_bs.rearrange("g p -> p g")[0:32, :], bf("bcol"), f"c7{i}", allow_slow_non_contiguous=True)

    def matmul_group(bk, stat_fn, kc_n, si):
        def pe(e):
            ins = None
            for kc in range(kc_n):
                ins = e.matmul(bk[:], stat_fn(kc), slots[si][:, kc, :], start=(kc == 0), stop=(kc == kc_n - 1))
            return ins
        return pe

    def proj_tok(stat_fn, stat_bufs_fn, kc_n, w, r0, col0, nblocks, epi, last_cb=None):
        for b in range(nblocks):
            si = load_slot(w, r0, kc_n, col0 + 512 * b)
            for t in range(NTL):
                bk, bb = next_bank()
                S.op("pe", matmul_group(bk, lambda kc, t=t: stat_fn(kc, t), kc_n, si),
                     reads=[slotB[si]] + stat_bufs_fn(t), writes=[bb])
                epi(b, t, bk, bb)
                if last_cb is not None and b == nblocks - 1:
                    last_cb(t)
                flush(1)

    def next_stg():
        i = stg_i[0] % 4
        stg_i[0] += 1
        return i

    def epi_store(dst, col0, func=None, eng="act"):
        def epi(b, t, bk, bb):
            i = next_stg()
            if eng == "act":
                S.op("act", lambda e: e.activation(stg[i][:], bk[:], func if func is not None else AF.Copy),
                     reads=[bb], writes=[stgB[i]])
            else:
                S.op("dve", lambda e: e.tensor_copy(stg[i][:], bk[:]), reads=[bb], writes=[stgB[i]])
            c = col0 + 512 * b
            S.op("sp", lambda e: e.dma_start(out=dst[t * P:(t + 1) * P, c:c + 512], in_=stg[i][:]),
                 reads=[stgB[i]], writes=[bf(f"{dst.name}_{t}")], dkey=f"stg{i}")
        return epi

    def epi_store_bf(dst, col0, func=None):
        def epi(b, t, bk, bb):
            i = next_stg()
            sv = stg[i][:].bitcast(BF16)[:, 0:512]
            S.op("act", lambda e: e.activation(sv, bk[:], func if func is not None else AF.Copy),
                 reads=[bb], writes=[stgB[i]])
            c = col0 + 512 * b
            S.op("sp", lambda e: e.dma_start(out=dst[t * P:(t + 1) * P, c:c + 512], in_=sv),
                 reads=[stgB[i]], writes=[bf(f"{dst.name}_{t}")], dkey=f"stg{i}")
        return epi

    def load_bc(row_g, row_b):
        S.op("sp", lambda e: e.dma_start(out=gbc[:], in_=lnv[row_g:row_g + 1, :].partition_broadcast(P)),
             writes=[bf("gbc")], dkey="gbc")
        S.op("sp", lambda e: e.dma_start(out=bbc[:], in_=lnv[row_b:row_b + 1, :].partition_broadcast(P)),
             writes=[bf("bbc")], dkey="bbc")

    def stats(src_ap, nchunk, eps, src_bufs):
        def f(e):
            ins = None
            for c in range(nchunk):
                ins = e.bn_stats(st6[:, c, :], src_ap[:, c * 512:(c + 1) * 512])
            return ins
        S.op("dve", f, reads=src_bufs, writes=[stat_b])
        S.op("dve", lambda e: e.bn_aggr(mv[:], st6[:, 0:nchunk, :].rearrange("p a b -> p (a b)")), reads=[stat_b], writes=[stat_b])
        S.op("dve", lambda e: e.tensor_scalar_add(rstd[:], mv[:, 1:2], eps), reads=[stat_b], writes=[stat_b])
        S.op("act", lambda e: e.activation(rstd[:], rstd[:], AF.Sqrt), reads=[stat_b], writes=[stat_b])
        S.op("dve", lambda e: e.reciprocal(rstd[:], rstd[:]), reads=[stat_b], writes=[stat_b])
        S.op("dve", lambda e: e.scalar_tensor_tensor(nmr[:], mv[:, 0:1], -1.0, rstd[:], ALU.mult, ALU.mult),
             reads=[stat_b], writes=[stat_b])

    def make_xT(t, src_bf_ap, src_buf):
        def go():
            transposes(lambda j: src_bf_ap[:, j * P:(j + 1) * P], 16,
                       lambda g0, cnt: xT[:, g0:g0 + cnt, t * P:(t + 1) * P], [src_buf], [xTB[t]])
        deferred.append(go)

    ln_i = [0]
    xb_i = [0]

    def ln_tile(t, x_src, R_list, x_dst, want_xT=True):
        i = ln_i[0] % 2
        ln_i[0] += 1
        xab = bf(f"xa{i}")
        rows = slice(t * P, (t + 1) * P)
        S.op("sp", lambda e: e.dma_start(out=xa[i][:], in_=x_src[rows, :]), reads=[bf(f"{x_src.name}_{t}")], writes=[xab], dkey=f"xa{i}")
        for p, R in enumerate(R_list):
            S.op("sp", lambda e, R=R: e.dma_start(out=rb[:], in_=R[rows, :]), reads=[bf(f"{R.name}_{t}")],
                 writes=[bf("rb")], dkey="rb")
            if p == 0:
                S.op("dve", lambda e: e.scalar_tensor_tensor(xa[i][:], xa[i][:], ALPHA, rb[:], ALU.mult, ALU.add),
                     reads=[bf("rb")], writes=[xab])
            else:
                S.op("dve", lambda e: e.tensor_tensor(xa[i][:], xa[i][:], rb[:], ALU.add), reads=[bf("rb")], writes=[xab])
        stats(xa[i], 4, LN_EPS, [xab])
        S.op("act", lambda e: e.activation(xa[i][:], xa[i][:], AF.Identity, bias=nmr[:], scale=rstd[:]),
             reads=[stat_b], writes=[xab])
        S.op("dve", lambda e: e.tensor_tensor(xa[i][:], xa[i][:], gbc[:], ALU.mult), reads=[bf("gbc")], writes=[xab])
        S.op("dve", lambda e: e.tensor_tensor(xa[i][:], xa[i][:], bbc[:], ALU.add), reads=[bf("bbc")], writes=[xab])
        S.op("sp", lambda e: e.dma_start(out=x_dst[rows, :], in_=xa[i][:]), reads=[xab],
             writes=[bf(f"{x_dst.name}_{t}")], dkey=f"xas{i}")
        if want_xT:
            xi = xb_i[0] % 3
            xb_i[0] += 1
            S.op("act", lambda e: e.copy(xb[xi][:], xa[i][:]), reads=[xab], writes=[bf(f"xb{xi}")])
            make_xT(t, xb[xi], bf(f"xb{xi}"))

    def region_c_barrier(from_names, to_names):
        S.op("dve", lambda e: e.memset(sc[:, 0:2], 0.0),
             writes=[bf(n) for n in from_names] + [bf(n) for n in to_names] + [bf("sc")])

    LN_NAMES = ["xa0", "xa1", "rb", "xb0", "xb1", "gbc", "bbc"]
    RET_NAMES = ["rq0", "rk0", "rv0", "rg0", "rq1", "rk1", "rv1", "rg1", "qT", "kT", "qTz", "ktz", "yin", "onr"]

    def ffn(l, x_src, g_row, x_dst, final):
        parts = [(0, 3), (3, 7), (7, 11)]
        for pi, (b0, b1) in enumerate(parts):
            for b in range(b0, b1):
                sg_ = load_slot(w_ffn_in[l], 0, 16, 512 * b)
                su_ = load_slot(w_ffn_in[l], 0, 16, FH + 512 * b)
                for cc in range(4):
                    jl = (b - b0) * 4 + cc
                    banks = [next_bank() for _ in range(6)]

                    def pe(e, si, bks, cc=cc):
                        ins = None
                        for kc in range(16):
                            for tg in range(3):
                                ins = e.matmul(bks[tg][0][:, 0:384], slots[si][:, kc, cc * P:(cc + 1) * P],
                                               xT[:, kc, tg * 384:(tg + 1) * 384], start=(kc == 0), stop=(kc == 15))
                        return ins

                    S.op("pe", lambda e, pe=pe, si=sg_, bks=banks[0:3]: pe(e, si, bks), reads=[slotB[sg_]] + xTB,
                         writes=[b_[1] for b_ in banks[0:3]])
                    S.op("pe", lambda e, pe=pe, si=su_, bks=banks[3:6]: pe(e, si, bks), reads=[slotB[su_]] + xTB,
                         writes=[b_[1] for b_ in banks[3:6]])
                    ui = ug_i[0] % 2
                    ug_i[0] += 1
                    for tg in range(3):
                        S.op("act", lambda e, tg=tg, ui=ui, bks=banks: e.activation(ug[ui][:, tg, :], bks[tg][0][:, 0:384], AF.Silu),
                             reads=[banks[tg][1]], writes=[ugB[ui]])
                    for tg in range(3):
                        S.op("dve", lambda e, tg=tg, ui=ui, bks=banks, jl=jl: e.tensor_tensor(
                            A[:, jl, tg * 384:(tg + 1) * 384], ug[ui][:, tg, :], bks[3 + tg][0][:, 0:384], ALU.mult),
                            reads=[ugB[ui], banks[3 + tg][1]], writes=[AB[jl]])
                    flush(1)
            kc_n = 4 * (b1 - b0)
            last = (pi == len(parts) - 1)
            if last:
                load_bc(g_row, g_row + 1)
            proj_tok(lambda kc, t: A[:, kc, t * P:(t + 1) * P], lambda t, kc_n=kc_n: AB[0:kc_n], kc_n,
                     w_ffn_out[l], b0 * 512, 0, 4, epi_store(Rs[pi], 0, eng="dve"),
                     last_cb=(lambda t: ln_tile(t, x_src, Rs, x_dst, want_xT=not final)) if last else None)
        flush_all()

    for t in range(NTL):
        i = t % 2
        S.op("sp", lambda e, t=t, i=i: e.dma_start(out=xa[i][:], in_=xin[t * P:(t + 1) * P, :]), writes=[bf(f"xa{i}")], dkey=f"xa{i}")
        S.op("act", lambda e, i=i: e.copy(xb[0][:], xa[i][:]), reads=[bf(f"xa{i}")], writes=[bf("xb0")])
        make_xT(t, xb[0], bf("xb0"))
        flush_all()

    load_bc(0, 1)

    def gmlp_tile(t):
        var = 1 if t == NTL - 1 else 0
        rows = slice(t * P, (t + 1) * P)
        S.op("sp", lambda e: e.dma_start(out=gu[:], in_=zscr[rows, 0:2048]), reads=[bf(f"zscr_{t}")], writes=[bf("xa0")], dkey="xa0")
        S.op("sp", lambda e: e.dma_start(out=gv[:], in_=zscr[rows, 2048:4096]), reads=[bf(f"zscr_{t}")], writes=[bf("xa1")], dkey="xa1")
        stats(gv, 4, LN_EPS, [bf("xa1")])
        S.op("act", lambda e: e.activation(gv[:], gv[:], AF.Identity, bias=nmr[:], scale=rstd[:]), reads=[stat_b], writes=[bf("xa1")])
        S.op("dve", lambda e: e.tensor_tensor(gv[:], gv[:], gbc[:], ALU.mult), reads=[bf("gbc")], writes=[bf("xa1")])
        S.op("dve", lambda e: e.tensor_tensor(gv[:], gv[:], bbc[:], ALU.add), reads=[bf("bbc")], writes=[bf("xa1")])
        if var == 1:
            S.op("sp", lambda e: e.dma_start(out=vout, in_=gv[:]), reads=[bf("xa1")], writes=[bf("vout")], dkey="vout")
        S.op("act", lambda e: e.copy(gvb[:], gv[:]), reads=[bf("xa1")], writes=[bf("rb")])
        gmlp_tile_pe(t, var)

    def gmlp_tile_pe(t, var):
        for q4 in range(4):
            bk, bb = next_bank()

            def pe(e, q4=q4, bk=bk):
                ins = None
                for gg in range(2):
                    g = q4 * 2 + gg
                    ins = e.matmul(bk[:, gg * 256:(gg + 1) * 256], WgT[:, var, g, :], gvb[:, g * 256:(g + 1) * 256],
                                   start=True, stop=True)
                return ins

            S.op("pe", pe, reads=[bf("WgT"), bf("rb")], writes=[bb])
            for gg in range(2):
                g = q4 * 2 + gg
                S.op("dve", lambda e, g=g, gg=gg, bk=bk: e.scalar_tensor_tensor(
                    gs[:, g * 256:(g + 1) * 256], bk[:, gg * 256:(gg + 1) * 256], bcol[:, var, g:g + 1],
                    gu[:, g * 256:(g + 1) * 256], ALU.add, ALU.mult), reads=[bb, bf("xa0"), bf("bcol")], writes=[bf("xb0")])

        transposes(lambda j: gs[:, j * P:(j + 1) * P], 16,
                   lambda g0, cnt: A[:, g0:g0 + cnt, t * P:(t + 1) * P], [bf("xb0")], [bf(f"sT{t}")])

    proj_tok(lambda kc, t: xT[:, kc, t * P:(t + 1) * P], lambda t: [xTB[t]], 16, w_a_in, 0, 0, 8,
             epi_store(zscr, 0, AF.Gelu), last_cb=gmlp_tile)
    flush_all()
    load_bc(2, 3)
    proj_tok(lambda kc, t: A[:, kc, t * P:(t + 1) * P], lambda t: [bf(f"sT{t}")], 16, w_a_out, 0, 0, 4,
             epi_store(Rs[0], 0, eng="dve"), last_cb=lambda t: ln_tile(t, xin, [Rs[0]], xs[0]))
    flush_all()
    ffn(0, xs[0], 4, xs[1], final=False)

    def epi_rope(dst, col0, dcol0):
        def epi(b, t, bk, bb):
            i = next_stg()
            i2 = next_stg()
            typ = 1 if t == NTL - 1 else 0
            for hh in range(2):
                h = 2 * b + hh
                dc = typ * 16 + dcol0 + h
                S.op("act", lambda e, hh=hh, dc=dc: e.activation(stg[i][:, hh * 256:(hh + 1) * 256], bk[:, hh * 256:(hh + 1) * 256],
                                                                  AF.Identity, scale=dqk[:, dc:dc + 1]),
                     reads=[bb, const_b], writes=[stgB[i]])
            cosb = cs[:, t:t + 1, 0:128].to_broadcast([P, 2, P])
            sinb = cs[:, t:t + 1, 128:256].to_broadcast([P, 2, P])
            xv = stg[i][:].rearrange("p (h two f) -> p h two f", two=2, f=P)
            tv = stg[i2][:].rearrange("p (h two f) -> p h two f", two=2, f=P)
            ov = stg[i2][:].bitcast(BF16)[:, 0:512].rearrange("p (h two f) -> p h two f", two=2, f=P)
            x1, x2 = xv[:, :, 0, :], xv[:, :, 1, :]
            t1, t2 = tv[:, :, 0, :], tv[:, :, 1, :]

            def f(e):
                e.tensor_tensor(t1, x1, cosb, ALU.mult)
                e.tensor_tensor(t2, x2, sinb, ALU.mult)
                return e.tensor_tensor(t1, t1, t2, ALU.subtract)

            def f2(e):
                e.tensor_tensor(t2, x1, sinb, ALU.mult)
                e.tensor_tensor(x1, x2, cosb, ALU.mult)
                return e.tensor_tensor(t2, t2, x1, ALU.add)

            S.op("dve", lambda e: e.tensor_tensor(t1, x1, cosb, ALU.mult), reads=[stgB[i], const_b], writes=[stgB[i2]])
            S.op("dve", lambda e: e.tensor_tensor(t2, x2, sinb, ALU.mult), reads=[stgB[i], const_b], writes=[stgB[i2]])
            S.op("dve", lambda e: e.tensor_tensor(t1, t1, t2, ALU.subtract), reads=[stgB[i2]], writes=[stgB[i2]])
            S.op("dve", lambda e: e.tensor_tensor(t2, x1, sinb, ALU.mult), reads=[stgB[i], const_b], writes=[stgB[i2]])
            S.op("dve", lambda e: e.tensor_tensor(x1, x2, cosb, ALU.mult), reads=[stgB[i], const_b], writes=[stgB[i]])
            S.op("dve", lambda e: e.tensor_tensor(t2, t2, x1, ALU.add), reads=[stgB[i], stgB[i2]], writes=[stgB[i2]])
            sv = stg[i][:].bitcast(BF16)[:, 0:512]
            S.op("act", lambda e: e.copy(sv, stg[i2][:]), reads=[stgB[i2]], writes=[stgB[i]])
            c = col0 + 512 * b
            S.op("sp", lambda e: e.dma_start(out=dst[t * P:(t + 1) * P, c:c + 512], in_=sv),
                 reads=[stgB[i]], writes=[bf(f"{dst.name}_{t}")], dkey=f"stg{i}")
        return epi

    xTf = lambda kc, t: xT[:, kc, t * P:(t + 1) * P]
    xTb = lambda t: [xTB[t]]
    proj_tok(xTf, xTb, 16, w_b_in, 0, 2048, 4, epi_rope(kd, 0, 8))
    proj_tok(xTf, xTb, 16, w_b_in, 0, 4096, 8, epi_store_bf(vd, 0))
    proj_tok(xTf, xTb, 16, w_b_in, 0, 0, 4, epi_rope(qd, 0, 0))
    proj_tok(xTf, xTb, 16, w_b_in, 0, 8192, 8, epi_store_bf(gd, 0, AF.Silu))
    flush_all()

    SB_ = bf("S")

    def ret_tile(t, hg, ri, mode):
        rows = slice(t * P, (t + 1) * P)
        typ = 1 if t == NTL - 1 else 0
        full = (mode == "full")
        S.op("sp", lambda e: e.dma_start(out=rk[ri][:], in_=kd[rows, hg * 1024:(hg + 1) * 1024]), reads=[bf(f"kd_{t}")], writes=[bf(f"rk{ri}")], dkey=f"rk{ri}")
        S.op("sp", lambda e: e.dma_start(out=rv[ri][:], in_=vd[rows, hg * 2048:(hg + 1) * 2048]), reads=[bf(f"vd_{t}")], writes=[bf(f"rv{ri}")], dkey=f"rv{ri}")
        if full:
            S.op("sp", lambda e: e.dma_start(out=rq[ri][:], in_=qd[rows, hg * 1024:(hg + 1) * 1024]), reads=[bf(f"qd_{t}")], writes=[bf(f"rq{ri}")], dkey=f"rq{ri}")
            S.op("sp", lambda e: e.dma_start(out=rg[ri][:], in_=gd[rows, hg * 2048:(hg + 1) * 2048]), reads=[bf(f"gd_{t}")], writes=[bf(f"rg{ri}")], dkey=f"rg{ri}")
            transposes(lambda j: rq[ri][:, j * P:(j + 1) * P], 8, lambda g0, cnt: qT[:, g0:g0 + cnt, :], [bf(f"rq{ri}")], [bf("qT")], evac="act")
            transposes(lambda j: rk[ri][:, j * P:(j + 1) * P], 8, lambda g0, cnt: kT[:, g0:g0 + cnt, :], [bf(f"rk{ri}")], [bf("kT")], evac="act")
        if typ == 0:
            for hl in range(4):
                h = hg * 4 + hl
                gL = GAM[h] ** 128
                if full:
                    bs, bsb = next_bank()

                    def pe_sc(e, hl=hl, bs=bs):
                        e.matmul(bs[:, 0:P], kT[:, 2 * hl, :], qT[:, 2 * hl, :], start=True, stop=False)
                        return e.matmul(bs[:, 0:P], kT[:, 2 * hl + 1, :], qT[:, 2 * hl + 1, :], start=False, stop=True)

                    S.op("pe", pe_sc, reads=[bf("qT"), bf("kT")], writes=[bsb])
                    S.op("dve", lambda e, bs=bs: e.tensor_tensor(sc[:], bs[:, 0:P], mask[:, 0, :], ALU.mult), reads=[bsb, const_b], writes=[bf("sc")])
                    bo, bob = next_bank()

                    def pe_o(e, hl=hl, bo=bo):
                        e.matmul(bo[:], sc[:], rv[ri][:, hl * 512:(hl + 1) * 512], start=True, stop=False)
                        e.matmul(bo[:], qT[:, 2 * hl, :], Sb[:, 2 * hl, :], start=False, stop=False)
                        return e.matmul(bo[:], qT[:, 2 * hl + 1, :], Sb[:, 2 * hl + 1, :], start=False, stop=True)

                    S.op("pe", pe_o, reads=[bf("sc"), bf(f"rv{ri}"), bf("qT"), bf("Sb")], writes=[bob])
                    gn_head(bo, bob, hl, ri)
                for dc in range(2):
                    bd, bdb = next_bank()
                    S.op("pe", lambda e, hl=hl, dc=dc, bd=bd: e.matmul(bd[:], rk[ri][:, hl * 256 + dc * P: hl * 256 + (dc + 1) * P],
                                                                     rv[ri][:, hl * 512:(hl + 1) * 512], start=True, stop=True),
                         reads=[bf(f"rk{ri}"), bf(f"rv{ri}")], writes=[bdb])
                    j = 2 * hl + dc
                    S.op("dve", lambda e, j=j, bd=bd, gL=gL: e.scalar_tensor_tensor(Sf[:, j, :], Sf[:, j, :], 1.0, bd[:], ALU.mult, ALU.add),
                         reads=[bdb], writes=[SB_])
                    S.op("dve", lambda e, j=j, gL=gL: e.tensor_scalar_mul(Sf[:, j, :], Sf[:, j, :], float(gL)), reads=[SB_], writes=[SB_])
                    S.op("act", lambda e, j=j: e.copy(Sb[:, j, :], Sf[:, j, :]), reads=[SB_], writes=[bf("Sb")])
        else:
            assert full
            S.op("dve", lambda e: e.memset(qTz[:].rearrange("p a b c -> p (a b c)"), 0.0), writes=[bf("qTz")])
            for i in range(4):
                S.op("dve", lambda e, i=i: e.tensor_copy(qTz[:, i, :, 32 * i:32 * i + 32], qT[:, :, 32 * i:32 * i + 32]),
                     reads=[bf("qT")], writes=[bf("qTz")])
            for hl in range(4):
                bs, bsb = next_bank()

                def pe_sc(e, hl=hl, bs=bs):
                    e.matmul(bs[:, 0:P], kT[:, 2 * hl, :], qT[:, 2 * hl, :], start=True, stop=False)
                    return e.matmul(bs[:, 0:P], kT[:, 2 * hl + 1, :], qT[:, 2 * hl + 1, :], start=False, stop=True)

                S.op("pe", pe_sc, reads=[bf("qT"), bf("kT")], writes=[bsb])
                S.op("dve", lambda e, bs=bs: e.tensor_tensor(sc[:], bs[:, 0:P], mask[:, 1, :], ALU.mult), reads=[bsb, const_b], writes=[bf("sc")])
                bo, bob = next_bank()
                S.op("pe", lambda e, hl=hl, bo=bo: e.matmul(bo[:], sc[:], rv[ri][:, hl * 512:(hl + 1) * 512], start=True, stop=True),
                     reads=[bf("sc"), bf(f"rv{ri}")], writes=[bob])
                S.op("act", lambda e, hl=hl, bo=bo: e.copy(osb[:, hl, :], bo[:]), reads=[bob], writes=[bf("osb")])
            for i in range(4):
                r0 = (i * 8 + hg * 4) * 256
                S.op("sp", lambda e, r0=r0: e.dma_start(out=Sf[:], in_=s0in[r0:r0 + 1024, :].rearrange("(j p) v -> p j v", p=P)),
                     writes=[SB_], dkey="Sf")
                S.op("act", lambda e: e.copy(Sb[:], Sf[:]), reads=[SB_], writes=[bf("Sb")])
                S.op("dve", lambda e, i=i: e.tensor_scalar_mul(ktz[:], rk[ri][:], rowm[:, i:i + 1]), reads=[bf(f"rk{ri}"), const_b], writes=[bf("ktz")])
                for hl in range(4):
                    h = hg * 4 + hl
                    gL = GAM[h] ** 32
                    bo, bob = next_bank()

                    def pe_c(e, hl=hl, bo=bo, i=i):
                        e.matmul(bo[:], qTz[:, i, 2 * hl, :], Sb[:, 2 * hl, :], start=True, stop=False)
                        return e.matmul(bo[:], qTz[:, i, 2 * hl + 1, :], Sb[:, 2 * hl + 1, :], start=False, stop=True)

                    S.op("pe", pe_c, reads=[bf("qTz"), bf("Sb")], writes=[bob])
                    S.op("dve", lambda e, hl=hl, bo=bo: e.tensor_tensor(osb[:, hl, :], osb[:, hl, :], bo[:], ALU.add), reads=[bob], writes=[bf("osb")])
                    for dc in range(2):
                        bd, bdb = next_bank()
                        S.op("pe", lambda e, hl=hl, dc=dc, bd=bd: e.matmul(bd[:], ktz[:, hl * 256 + dc * P: hl * 256 + (dc + 1) * P],
                                                                         rv[ri][:, hl * 512:(hl + 1) * 512], start=True, stop=True),
                             reads=[bf("ktz"), bf(f"rv{ri}")], writes=[bdb])
                        j = 2 * hl + dc
                        S.op("dve", lambda e, j=j, bd=bd: e.tensor_tensor(Sf[:, j, :], Sf[:, j, :], bd[:], ALU.add), reads=[bdb], writes=[SB_])
                        S.op("dve", lambda e, j=j, gL=gL: e.tensor_scalar_mul(Sf[:, j, :], Sf[:, j, :], float(gL)), reads=[SB_], writes=[SB_])
                S.op("sp", lambda e, r0=r0: e.dma_start(out=sout_s[r0:r0 + 1024, :].rearrange("(j p) v -> p j v", p=P), in_=Sf[:]),
                     reads=[SB_], writes=[bf("sout_s")], dkey=f"Sfo{i}")
            for hl in range(4):
                gn_head(None, None, hl, ri)
        if full:
            def go():
                transposes(lambda j: yin[:, j * P:(j + 1) * P], 16,
                           lambda g0, cnt: A[:, g0:g0 + cnt, t * P:(t + 1) * P], [bf("yin")], [bf(f"sT{t}")])
            go()

    def gn_head(bo, bob, hl, ri):
        if bo is not None:
            src, sb_ = bo, [bob]
        else:
            src, sb_ = osb[:, hl, :], [bf("osb")]
        stats(src, 1, GN_EPS, sb_)
        S.op("act", lambda e: e.activation(onr[:], src[:] if bo is not None else src, AF.Identity, bias=nmr[:], scale=rstd[:]),
             reads=sb_ + [stat_b], writes=[bf("onr")])
        S.op("dve", lambda e: e.tensor_tensor(yin[:, hl * 512:(hl + 1) * 512], onr[:], rg[ri][:, hl * 512:(hl + 1) * 512], ALU.mult),
             reads=[bf("onr"), bf(f"rg{ri}")], writes=[bf("yin")])

    region_c_barrier(LN_NAMES, RET_NAMES)
    S.op("dve", lambda e: e.memset(sc[:, 0:2], 0.0), writes=xTB + [SB_, bf("Sb"), bf("osb"), bf("sc")])

    if exchange:
        for hg in range(2):
            S.op("dve", lambda e: e.memset(Sf[:].rearrange("p a b -> p (a b)"), 0.0), writes=[SB_])
            for t in range(NPT):
                ret_tile(t, hg, t % 2, "state")
            for hl in range(4):
                h = hg * 4 + hl
                S.op("sp", lambda e, h=h, hl=hl: e.dma_start(out=sxh[h].rearrange("(j p) v -> p j v", p=P), in_=Sf[:, 2 * hl:2 * hl + 2, :]),
                     reads=[SB_], writes=[bf(f"sx{h}")], dkey=f"sx{h}")
                S.op("pool", lambda e, h=h: e.collective_compute("AllGather", ALU.bypass, replica_groups=[[0, 1], [2, 3], [4, 5], [6, 7]],
                                                                 ins=[sxh[h]], outs=[gxh[h]]), reads=[bf(f"sx{h}")], writes=[bf(f"gx{h}")])

    for hg in range(2):
        if exchange:
            for hl in range(4):
                h = hg * 4 + hl
                S.op("sp", lambda e, h=h, hl=hl: e.dma_start(out=Sf[:, 2 * hl:2 * hl + 2, :], in_=gxh[h][0:256, :].rearrange("(j p) v -> p j v", p=P)),
                     reads=[bf(f"gx{h}")], writes=[SB_], dkey="Sf")
            S.op("dve", lambda e: e.tensor_scalar_mul(Sf[:].rearrange("p a b -> p (a b)"), Sf[:].rearrange("p a b -> p (a b)"), flag[:, 0:1]),
                 reads=[const_b], writes=[SB_])
        else:
            S.op("dve", lambda e: e.memset(Sf[:].rearrange("p a b -> p (a b)"), 0.0), writes=[SB_])
        S.op("act", lambda e: e.copy(Sb[:].rearrange("p a b -> p (a b)"), Sf[:].rearrange("p a b -> p (a b)")), reads=[SB_], writes=[bf("Sb")])
        for t in range(NTL):
            if t == NPT:
                S.op("sp", lambda e, hg=hg: e.dma_start(out=sout_p[hg * 1024:(hg + 1) * 1024, :].rearrange("(j p) v -> p j v", p=P), in_=Sf[:]),
                     reads=[SB_], writes=[bf("sout_p")], dkey=f"sop{hg}")
            ret_tile(t, hg, t % 2, "full")
        last = (hg == 1)
        if last:
            region_c_barrier(RET_NAMES, LN_NAMES)
            S.op("dve", lambda e: e.memset(sc[:, 0:2], 0.0), writes=[SB_, bf("Sb"), bf("osb")] + xTB + [bf("sc")])
            load_bc(6, 7)
        proj_tok(lambda kc, t: A[:, kc, t * P:(t + 1) * P], lambda t: [bf(f"sT{t}")], 16, w_b_out, hg * 2048, 0, 4,
                 epi_store(Rs[hg], 0, eng="dve"),
                 last_cb=(lambda t: ln_tile(t, xs[1], [Rs[0], Rs[1]], xs[2])) if last else None)
    flush_all()
    ffn(1, xs[2], 8, yout, final=True)

    outs = [bf(f"yout_{t}") for t in range(NTL)] + [bf("sout_s"), bf("sout_p"), bf("vout")]
    S.op("sp", lambda e: e.nop(), reads=outs)
    with nc.Block() as block:
        S.emit(block)
    return nc


def _consts(half):
    c = {}
    c["c_ident"] = np.eye(P, dtype=np.float32)
    half_d = 128
    inv = np.power(np.float32(10000.0), -np.arange(half_d, dtype=np.float32) / np.float32(half_d)).astype(np.float32)
    cs = np.zeros((P, NTL, 256), np.float32)
    for t in range(NTL):
        if t < NPT:
            pos = (half * 1024 + t * P + np.arange(P)).astype(np.float32)
        else:
            pos = (4096 + (np.arange(P) % 32)).astype(np.float32)
        ang = (pos[:, None] * inv[None, :]).astype(np.float32)
        cs[:, t, 0:128] = np.cos(ang)
        cs[:, t, 128:256] = np.sin(ang)
    c["c_cs"] = cs.reshape(P, NTL * 256)
    dqk = np.zeros((P, 32), np.float64)
    for typ in range(2):
        tl = np.arange(P) if typ == 0 else (np.arange(P) % 32)
        for h in range(8):
            lg = np.log1p(-2.0 ** (-5.0 - h))
            dqk[:, typ * 16 + h] = np.exp(lg * (tl + 1.0)) * (256.0 ** -0.5)
            dqk[:, typ * 16 + 8 + h] = np.exp(-lg * (tl + 1.0))
    c["c_dqk"] = dqk.astype(np.float32)
    s = np.arange(P)[:, None]
    t = np.arange(P)[None, :]
    m0 = (s <= t).astype(np.float32)
    m1 = ((s <= t) & ((s // 32) == (t // 32))).astype(np.float32)
    c["c_mask"] = np.concatenate([m0, m1], axis=1)
    rowm = np.zeros((P, 4), np.float32)
    for i in range(4):
        rowm[32 * i:32 * i + 32, i] = 1.0
    c["c_rowm"] = rowm
    c["c_flag"] = np.full((P, 1), float(half), np.float32)
    return c


_NC_CACHE = {}


def kernel(x_prompt, x_sample, state_ret, w_a_in, a_ln_g, a_ln_b, a_ws, a_bs, w_a_out,
           w_b_in, w_b_out, w_ffn_in, w_ffn_out, ln_mix_g, ln_mix_b, ln_ffn_g, ln_ffn_b):
    f = lambda a: np.ascontiguousarray(np.asarray(a, dtype=np.float32))
    x_prompt, x_sample, state_ret = f(x_prompt), f(x_sample), f(state_ret)
    lnv = np.stack([f(a_ln_g)[0], f(a_ln_b)[0], f(ln_mix_g)[0], f(ln_mix_b)[0], f(ln_ffn_g)[0], f(ln_ffn_b)[0],
                    f(ln_mix_g)[1], f(ln_mix_b)[1], f(ln_ffn_g)[1], f(ln_ffn_b)[1]], axis=0)
    shared = {
        "w_a_in": f(w_a_in)[0], "w_a_out": f(w_a_out)[0], "w_b_in": f(w_b_in)[0], "w_b_out": f(w_b_out)[0],
        "w_ffn_in0": f(w_ffn_in)[0], "w_ffn_in1": f(w_ffn_in)[1], "w_ffn_out0": f(w_ffn_out)[0], "w_ffn_out1": f(w_ffn_out)[1],
        "a_ws": f(a_ws)[0].reshape(8 * 128, 128), "a_bs": f(a_bs)[0], "lnv": lnv,
    }
    in_maps = []
    for c in range(8):
        b, half = c // 2, c % 2
        m = dict(shared)
        m["xin"] = np.concatenate([x_prompt[b, half * 1024:(half + 1) * 1024], x_sample[4 * c:4 * c + 4].reshape(128, D)], axis=0)
        m["s0in"] = state_ret[0, 4 * c:4 * c + 4].reshape(4 * 8 * 256, 512)
        m.update(_consts(half))
        in_maps.append(m)
    if "nc" not in _NC_CACHE:
        _NC_CACHE["nc"] = build_nc()
    res = run_bass_kernel_spmd(_NC_CACHE["nc"], in_maps, core_ids=list(range(8)))
    r = res.results
    y_prompt = np.zeros((4, 2048, D), np.float32)
    y_sample = np.zeros((32, 32, D), np.float32)
    rsp = np.zeros((1, 4, 8, 256, 512), np.float32)
    rss = np.zeros((1, 32, 8, 256, 512), np.float32)
    gv = np.zeros((1, 32, 32, D), np.float32)
    for c in range(8):
        b, half = c // 2, c % 2
        y = np.asarray(r[c]["yout"])
        y_prompt[b, half * 1024:(half + 1) * 1024] = y[:1024]
        y_sample[4 * c:4 * c + 4] = y[1024:].reshape(4, 32, D)
        if half == 1:
            rsp[0, b] = np.asarray(r[c]["sout_p"]).reshape(8, 256, 512)
        rss[0, 4 * c:4 * c + 4] = np.asarray(r[c]["sout_s"]).reshape(4, 8, 256, 512)
        gv[0, 4 * c:4 * c + 4] = np.asarray(r[c]["vout"]).reshape(4, 32, D)
    return (y_prompt, y_sample, rsp, rss, gv)
```

```python
import numpy as np
import concourse.bass as bass
import concourse.mybir as mybir
from concourse.bass_utils import run_bass_kernel_spmd

F32 = mybir.dt.float32
BF16 = mybir.dt.bfloat16
AF = mybir.ActivationFunctionType
ALU = mybir.AluOpType
P = 128
D = 2048
NTL = 9
NPT = 8
NT = NTL * P
FH = 5632
ALPHA = 4.0 ** 0.25
LN_EPS = 1e-5
GN_EPS = 1e-6
NSLOT = 3
GAM = [1.0 - 2.0 ** (-5.0 - h) for h in range(8)]


class Buf:
    __slots__ = ("name", "w", "r")

    def __init__(self, name):
        self.name = name
        self.w = {}
        self.r = {}


class Sched:
    ENG = ("pe", "act", "dve", "pool", "sp")

    def __init__(self, nc):
        self.nc = nc
        self.ops = []
        self.semh = {}
        self.semc = {}
        self.cur = {e: 0 for e in self.ENG}

    def _sem(self, key):
        if key not in self.semh:
            self.semh[key] = self.nc.alloc_semaphore(key)
            self.semc[key] = 0
        return key

    def op(self, eng, fn, reads=(), writes=(), dkey=None):
        waits = {}

        def add(evs):
            for s, v in evs.items():
                if waits.get(s, 0) < v:
                    waits[s] = v

        for b in reads:
            add(b.w)
        for b in writes:
            add(b.r)
            add(b.w)
        if dkey is not None:
            key = self._sem("d_" + dkey)
            inc = 16
        else:
            key = self._sem(f"{eng}{self.cur[eng]}")
            if self.semc[key] >= 2000:
                self.cur[eng] += 1
                key = self._sem(f"{eng}{self.cur[eng]}")
            inc = 1
        self.semc[key] += inc
        v = self.semc[key]
        if eng == "pe":
            waits = {s: x for s, x in waits.items() if not s.startswith("pe")}
        self.ops.append((eng, fn, waits, key, inc))
        for b in reads:
            if b not in writes:
                if b.r.get(key, 0) < v:
                    b.r[key] = v
        for b in writes:
            if b.r:
                b.w = {key: v}
                b.r = {}
            else:
                b.w[key] = v
        return (key, v)

    def emit(self, block):
        engs = {"pe": block.tensor, "act": block.scalar, "dve": block.vector, "pool": block.gpsimd, "sp": block.sync}
        for ename, deco in engs.items():
            myops = [o for o in self.ops if o[0] == ename]

            def body(eng, myops=myops):
                seen = {}
                for (_, fn, waits, key, inc) in myops:
                    for s, v in waits.items():
                        if seen.get(s, 0) >= v:
                            continue
                        seen[s] = v
                        eng.wait_ge(self.semh[s], v)
                    ins = fn(eng)
                    ins.then_inc(self.semh[key], inc)

            deco(body)


def build_nc(exchange=True, debug=False):
    nc = bass.Bass("TRN2", target_bir_lowering=False)
    S = Sched(nc)

    def din(name, shape, dt=F32):
        return nc.dram_tensor(name, list(shape), dt, kind="ExternalInput").ap()

    def dout(name, shape, dt=F32):
        return nc.dram_tensor(name, list(shape), dt, kind="ExternalOutput").ap()

    def dscr(name, shape, dt=F32):
        if debug and name in ("xs0", "xs1", "xs2", "qd", "kd", "vd", "gd", "zscr"):
            return nc.dram_tensor(name, list(shape), dt, kind="ExternalOutput").ap()
        return nc.dram_tensor(name, list(shape), dt).ap()

    xin = din("xin", [NT, D])
    s0in = din("s0in", [4 * 8 * 256, 512])
    w_a_in = din("w_a_in", [D, 4096])
    w_a_out = din("w_a_out", [D, D])
    w_b_in = din("w_b_in", [D, 12288])
    w_b_out = din("w_b_out", [4096, D])
    w_ffn_in = [din(f"w_ffn_in{l}", [D, 2 * FH]) for l in range(2)]
    w_ffn_out = [din(f"w_ffn_out{l}", [FH, D]) for l in range(2)]
    a_ws = din("a_ws", [8 * 128, 128])
    a_bs = din("a_bs", [8, 128])
    lnv = din("lnv", [10, D])
    c_ident = din("c_ident", [P, P])
    c_cs = din("c_cs", [P, NTL * 256])
    c_dqk = din("c_dqk", [P, 32])
    c_mask = din("c_mask", [P, 256])
    c_rowm = din("c_rowm", [P, 4])
    c_flag = din("c_flag", [P, 1])

    yout = dout("yout", [NT, D])
    sout_s = dout("sout_s", [4 * 8 * 256, 512])
    sout_p = dout("sout_p", [8 * 256, 512])
    vout = dout("vout", [P, D])

    zscr = dscr("zscr", [NT, 4096])
    xs = [dscr(f"xs{i}", [NT, D]) for i in range(3)]
    Rs = [dscr(f"R{i}", [NT, D]) for i in range(3)]
    qd = dscr("qd", [NT, D], BF16)
    kd = dscr("kd", [NT, D], BF16)
    vd = dscr("vd", [NT, 4096], BF16)
    gd = dscr("gd", [NT, 4096], BF16)
    sxh = [dscr(f"sx{h}", [256, 512]) for h in range(8)]
    gxh = [dscr(f"gx{h}", [512, 512]) for h in range(8)]

    off = [16512]

    def sb(name, shape, dt, at=None):
        nbytes = int(np.prod(shape[1:])) * (2 if dt == BF16 else 4)
        if at is None:
            o = off[0]
            off[0] += (nbytes + 31) // 32 * 32
        else:
            o = at
        return nc.alloc_sbuf_tensor_at(name, list(shape), dt, offset=o)

    ident = sb("ident", [P, P], BF16)
    identf = sb("identf", [P, P], F32)
    mask = sb("mask", [P, 2, P], F32)
    cs = sb("cs", [P, NTL, 256], F32)
    dqk = sb("dqk", [P, 32], F32)
    rowm = sb("rowm", [P, 4], F32)
    flag = sb("flag", [P, 1], F32)
    WgT = sb("WgT", [P, 2, 8, P], BF16)
    bcol = sb("bcol", [P, 2, 8], F32)
    st6 = sb("st6", [P, 4, 6], F32)
    mv = sb("mv", [P, 2], F32)
    rstd = sb("rstd", [P, 1], F32)
    nmr = sb("nmr", [P, 1], F32)
    sc = sb("sc", [P, P], BF16)
    xT = sb("xT", [P, 16, NT], BF16)
    XT_OFF = off[0] - 16 * NT * 2
    slots = [sb(f"slot{i}", [P, 16, 512], BF16) for i in range(NSLOT)]
    A = sb("A", [P, 16, NT], BF16)
    stg = [sb(f"stg{i}", [P, 512], F32) for i in range(4)]
    UG_OFF = off[0]
    ug = [sb(f"ug{i}", [P, 3, 384], F32) for i in range(2)]
    onr4 = sb("onr4", [P, 4, 512], F32, at=UG_OFF)
    sc4 = sb("sc4", [P, 4, P], BF16)
    mv4 = sb("mv4", [P, 4, 2], F32)
    rstd4 = sb("rstd4", [P, 4], F32)
    nmr4 = sb("nmr4", [P, 4], F32)
    xb2 = sb("xb2", [P, D], BF16)
    C_OFF = off[0]
    C_SIZE = 49152
    off[0] += C_SIZE
    assert off[0] <= 229344, off[0]
    xa = [sb(f"xa{i}", [P, D], F32, at=C_OFF + i * 8192) for i in range(2)]
    rb = sb("rb", [P, D], F32, at=C_OFF + 16384)
    xb = [sb("xb0", [P, D], BF16, at=C_OFF + 24576), sb("xb1", [P, D], BF16, at=C_OFF + 45056), xb2]
    gbc = sb("gbc", [P, D], F32, at=C_OFF + 28672)
    bbc = sb("bbc", [P, D], F32, at=C_OFF + 36864)
    gu = xa[0]
    gv = xa[1]
    gvb = sb("gvb", [P, D], BF16, at=C_OFF + 16384)
    gs = sb("gs", [P, D], BF16, at=C_OFF + 24576)
    rq = [sb(f"rq{i}", [P, 1024], BF16, at=C_OFF + i * 12288) for i in range(2)]
    rk = [sb(f"rk{i}", [P, 1024], BF16, at=C_OFF + i * 12288 + 2048) for i in range(2)]
    rv = [sb(f"rv{i}", [P, 2048], BF16, at=C_OFF + i * 12288 + 4096) for i in range(2)]
    rg = [sb(f"rg{i}", [P, 2048], BF16, at=C_OFF + i * 12288 + 8192) for i in range(2)]
    qT = sb("qT", [P, 8, P], BF16, at=C_OFF + 24576)
    kT = sb("kT", [P, 8, P], BF16, at=C_OFF + 26624)
    qTz = sb("qTz", [P, 4, 8, P], BF16, at=C_OFF + 28672)
    ktz = sb("ktz", [P, 1024], BF16, at=C_OFF + 36864)
    yin = sb("yin", [P, 2048], BF16, at=C_OFF + 38912)
    onr = sb("onr", [P, 512], F32, at=C_OFF + 43008)
    Sf = sb("Sf", [P, 8, 512], F32, at=XT_OFF)
    Sb = sb("Sb", [P, 8, 512], BF16, at=XT_OFF + 16384)
    osb = sb("osb", [P, 4, 512], F32, at=XT_OFF + 24576)

    pbank = [nc.alloc_psum_tensor(f"pb{i}", [P, 512], F32) for i in range(8)]
    bankB = [Buf(f"bank{i}") for i in range(8)]
    bank_i = [0]

    def next_bank():
        i = bank_i[0] % 8
        bank_i[0] += 1
        return pbank[i], bankB[i]

    slotB = [Buf(f"slot{i}") for i in range(NSLOT)]
    slot_i = [0]

    def next_slot():
        i = slot_i[0] % NSLOT
        slot_i[0] += 1
        return i

    xTB = [Buf(f"xT{t}") for t in range(NTL)]
    AB = [Buf(f"A{j}") for j in range(16)]
    stgB = [Buf(f"stg{i}") for i in range(4)]
    stg_i = [0]
    ugB = [Buf(f"ug{i}") for i in range(2)]
    ug_i = [0]
    B = {}

    def bf(name):
        if name not in B:
            B[name] = Buf(name)
        return B[name]

    cbufs = [bf(n) for n in ("xa0", "xa1", "rb", "xb0", "xb1", "gbc", "bbc")]
    const_b = bf("const")
    stat_b = bf("stat")

    deferred = []

    DEPTH = 2

    def flush(n=1):
        while len(deferred) > DEPTH:
            deferred.pop(0)()

    def flush_all():
        while deferred:
            deferred.pop(0)()

    def wblock_src(w, r0, kc, c0):
        return w[r0:r0 + kc * P, c0:c0 + 512].rearrange("(kc p) n -> p kc n", p=P)

    def load_slot(w, r0, kc, c0):
        si = next_slot()
        src = wblock_src(w, r0, kc, c0)
        S.op("pool", lambda e: e.dma_start(out=slots[si][:, 0:kc, :], in_=src), writes=[slotB[si]], dkey=f"slot{si}")
        return si

    def ld(dst_ap, src_ap, b, key, eng="sp", **kw):
        S.op(eng, lambda e: e.dma_start(out=dst_ap, in_=src_ap, **kw), writes=[b], dkey=key)

    ld(identf[:], c_ident, const_b, "c0")
    ld(mask[:].rearrange("p a b -> p (a b)"), c_mask, const_b, "c1")
    ld(cs[:].rearrange("p a b -> p (a b)"), c_cs, const_b, "c2")
    ld(dqk[:], c_dqk, const_b, "c3")
    ld(rowm[:], c_rowm, const_b, "c4")
    ld(flag[:], c_flag, const_b, "c5")
    S.op("act", lambda e: e.copy(ident[:], identf[:]), reads=[const_b], writes=[bf("ident")])
    identB = bf("ident")

    def transposes(src_fn, n, dst_fn, src_bufs, dst_bufs, evac="dve", post=None):
        for g0 in range(0, n, 8):
            cnt = min(8, n - g0)
            bk, bb = next_bank()
            bkv = bk[:].bitcast(BF16)

            def pe(e, g0=g0, cnt=cnt, bkv=bkv):
                ins = None
                for j in range(cnt):
                    ins = e.transpose(bkv[:, j * P:(j + 1) * P], src_fn(g0 + j), ident[:])
                return ins

            S.op("pe", pe, reads=list(src_bufs) + [identB], writes=[bb])
            dst = dst_fn(g0, cnt) if dst_fn is not None else None
            src = bkv[:, 0:cnt * P].rearrange("p (a b) -> p a b", b=P)
            if post is not None:
                post(g0, cnt, src, bb)
            elif evac == "dve":
                S.op("dve", lambda e, dst=dst, src=src: e.tensor_copy(dst, src), reads=[bb], writes=list(dst_bufs))
            else:
                S.op("act", lambda e, dst=dst, src=src: e.copy(dst, src), reads=[bb], writes=list(dst_bufs))

    wtmp = gu
    wtb = gvb
    gB = [bf("xa0"), bf("rb")]
    for var in range(2):
        if var == 0:
            S.op("sp", lambda e: e.dma_start(out=wtmp[:, 0:1024].rearrange("p (g s) -> p g s", s=P),
                                             in_=a_ws.rearrange("(g t) s -> t g s", t=P)), writes=[gB[0]], dkey="xa0")
        else:
            S.op("dve", lambda e: e.memset(wtmp[:, 0:1024], 0.0), writes=[gB[0]])
            for i in range(4):
                S.op("sp", lambda e, i=i: e.dma_start(
                    out=wtmp[32 * i:32 * i + 32, 0:1024].rearrange("p (g s) -> p g s", s=P)[:, :, 32 * i:32 * i + 32],
                    in_=a_ws.rearrange("(g t) s -> t g s", t=P)[0:32, :, 0:32]), writes=[gB[0]], dkey=f"wd{i}")
        S.op("act", lambda e: e.copy(wtb[:, 0:1024], wtmp[:, 0:1024]), reads=[gB[0]], writes=[gB[1]])

        def post(g0, cnt, src, bb, var=var):
            S.op("dve", lambda e: e.tensor_tensor(WgT[:, var, g0:g0 + cnt, :], src,
                                                  mask[:, var:var + 1, :].to_broadcast([P, cnt, P]), ALU.mult),
                 reads=[bb, const_b], writes=[bf("WgT")])

        transposes(lambda j: wtb[:, j * P:(j + 1) * P], 8, None, [gB[1]], [], post=post)
    ld(bcol[:, 0, :], a_bs.rearrange("g p -> p g"), bf("bcol"), "c6", allow_slow_non_contiguous=True)
    for i in range(4):
        ld(bcol[32 * i:32 * i + 32, 1, :], a_bs.rearrange("g p -> p g")[0:32, :], bf("bcol"), f"c7{i}", allow_slow_non_contiguous=True)

    def matmul_group(bk, stat_fn, kc_n, si):
        def pe(e):
            ins = None
            for kc in range(kc_n):
                ins = e.matmul(bk[:], stat_fn(kc), slots[si][:, kc, :], start=(kc == 0), stop=(kc == kc_n - 1))
            return ins
        return pe

    def proj_tok(stat_fn, stat_bufs_fn, kc_n, w, r0, col0, nblocks, epi, last_cb=None):
        for b in range(nblocks):
            si = load_slot(w, r0, kc_n, col0 + 512 * b)
            for t in range(NTL):
                bk, bb = next_bank()
                S.op("pe", matmul_group(bk, lambda kc, t=t: stat_fn(kc, t), kc_n, si),
                     reads=[slotB[si]] + stat_bufs_fn(t), writes=[bb])
                epi(b, t, bk, bb)
                if last_cb is not None and b == nblocks - 1:
                    last_cb(t)
                flush(1)

    def next_stg():
        i = stg_i[0] % 4
        stg_i[0] += 1
        return i

    def epi_store(dst, col0, func=None, eng="act"):
        def epi(b, t, bk, bb):
            i = next_stg()
            if eng == "act":
                S.op("act", lambda e: e.activation(stg[i][:], bk[:], func if func is not None else AF.Copy),
                     reads=[bb], writes=[stgB[i]])
            else:
                S.op("dve", lambda e: e.tensor_copy(stg[i][:], bk[:]), reads=[bb], writes=[stgB[i]])
            c = col0 + 512 * b
            S.op("sp", lambda e: e.dma_start(out=dst[t * P:(t + 1) * P, c:c + 512], in_=stg[i][:]),
                 reads=[stgB[i]], writes=[bf(f"{dst.name}_{t}")], dkey=f"stg{i}")
        return epi

    def epi_store_bf(dst, col0, func=None):
        def epi(b, t, bk, bb):
            i = next_stg()
            sv = stg[i][:].bitcast(BF16)[:, 0:512]
            S.op("act", lambda e: e.activation(sv, bk[:], func if func is not None else AF.Copy),
                 reads=[bb], writes=[stgB[i]])
            c = col0 + 512 * b
            S.op("sp", lambda e: e.dma_start(out=dst[t * P:(t + 1) * P, c:c + 512], in_=sv),
                 reads=[stgB[i]], writes=[bf(f"{dst.name}_{t}")], dkey=f"stg{i}")
        return epi

    def load_bc(row_g, row_b):
        S.op("sp", lambda e: e.dma_start(out=gbc[:], in_=lnv[row_g:row_g + 1, :].partition_broadcast(P)),
             writes=[bf("gbc")], dkey="gbc")
        S.op("sp", lambda e: e.dma_start(out=bbc[:], in_=lnv[row_b:row_b + 1, :].partition_broadcast(P)),
             writes=[bf("bbc")], dkey="bbc")

    def stats(src_ap, nchunk, eps, src_bufs):
        def f(e):
            ins = None
            for c in range(nchunk):
                ins = e.bn_stats(st6[:, c, :], src_ap[:, c * 512:(c + 1) * 512])
            return ins
        S.op("dve", f, reads=src_bufs, writes=[stat_b])
        S.op("dve", lambda e: e.bn_aggr(mv[:], st6[:, 0:nchunk, :].rearrange("p a b -> p (a b)")), reads=[stat_b], writes=[stat_b])
        S.op("dve", lambda e: e.tensor_scalar_add(rstd[:], mv[:, 1:2], eps), reads=[stat_b], writes=[stat_b])
        S.op("act", lambda e: e.activation(rstd[:], rstd[:], AF.Sqrt), reads=[stat_b], writes=[stat_b])
        S.op("dve", lambda e: e.reciprocal(rstd[:], rstd[:]), reads=[stat_b], writes=[stat_b])
        S.op("dve", lambda e: e.scalar_tensor_tensor(nmr[:], mv[:, 0:1], -1.0, rstd[:], ALU.mult, ALU.mult),
             reads=[stat_b], writes=[stat_b])

    def make_xT(t, src_bf_ap, src_buf):
        def go():
            transposes(lambda j: src_bf_ap[:, j * P:(j + 1) * P], 16,
                       lambda g0, cnt: xT[:, g0:g0 + cnt, t * P:(t + 1) * P], [src_buf], [xTB[t]])
        deferred.append(go)

    ln_i = [0]
    xb_i = [0]

    def ln_tile(t, x_src, R_list, x_dst, want_xT=True):
        i = ln_i[0] % 2
        ln_i[0] += 1
        xab = bf(f"xa{i}")
        rows = slice(t * P, (t + 1) * P)
        S.op("sp", lambda e: e.dma_start(out=xa[i][:], in_=x_src[rows, :]), reads=[bf(f"{x_src.name}_{t}")], writes=[xab], dkey=f"xa{i}")
        for p, R in enumerate(R_list):
            S.op("sp", lambda e, R=R: e.dma_start(out=rb[:], in_=R[rows, :]), reads=[bf(f"{R.name}_{t}")],
                 writes=[bf("rb")], dkey="rb")
            if p == 0:
                S.op("dve", lambda e: e.scalar_tensor_tensor(xa[i][:], xa[i][:], ALPHA, rb[:], ALU.mult, ALU.add),
                     reads=[bf("rb")], writes=[xab])
            else:
                S.op("dve", lambda e: e.tensor_tensor(xa[i][:], xa[i][:], rb[:], ALU.add), reads=[bf("rb")], writes=[xab])
        stats(xa[i], 4, LN_EPS, [xab])
        S.op("act", lambda e: e.activation(xa[i][:], xa[i][:], AF.Identity, bias=nmr[:], scale=rstd[:]),
             reads=[stat_b], writes=[xab])
        S.op("dve", lambda e: e.tensor_tensor(xa[i][:], xa[i][:], gbc[:], ALU.mult), reads=[bf("gbc")], writes=[xab])
        S.op("dve", lambda e: e.tensor_tensor(xa[i][:], xa[i][:], bbc[:], ALU.add), reads=[bf("bbc")], writes=[xab])
        S.op("sp", lambda e: e.dma_start(out=x_dst[rows, :], in_=xa[i][:]), reads=[xab],
             writes=[bf(f"{x_dst.name}_{t}")], dkey=f"xas{i}")
        if want_xT:
            xi = xb_i[0] % 3
            xb_i[0] += 1
            S.op("act", lambda e: e.copy(xb[xi][:], xa[i][:]), reads=[xab], writes=[bf(f"xb{xi}")])
            make_xT(t, xb[xi], bf(f"xb{xi}"))

    def region_c_barrier(from_names, to_names):
        S.op("dve", lambda e: e.memset(sc[:, 0:2], 0.0),
             writes=[bf(n) for n in from_names] + [bf(n) for n in to_names] + [bf("sc")])

    LN_NAMES = ["xa0", "xa1", "rb", "xb0", "xb1", "gbc", "bbc"]
    RET_NAMES = ["rq0", "rk0", "rv0", "rg0", "rq1", "rk1", "rv1", "rg1", "qT", "kT", "qTz", "ktz", "yin", "onr"]

    def ffn(l, x_src, g_row, x_dst, final):
        parts = [(0, 3), (3, 7), (7, 11)]
        for pi, (b0, b1) in enumerate(parts):
            for b in range(b0, b1):
                sg_ = load_slot(w_ffn_in[l], 0, 16, 512 * b)
                su_ = load_slot(w_ffn_in[l], 0, 16, FH + 512 * b)
                for cc in range(4):
                    jl = (b - b0) * 4 + cc
                    banks = [next_bank() for _ in range(6)]

                    def pe(e, si, bks, cc=cc):
                        ins = None
                        for kc in range(16):
                            for tg in range(3):
                                ins = e.matmul(bks[tg][0][:, 0:384], slots[si][:, kc, cc * P:(cc + 1) * P],
                                               xT[:, kc, tg * 384:(tg + 1) * 384], start=(kc == 0), stop=(kc == 15))
                        return ins

                    S.op("pe", lambda e, pe=pe, si=sg_, bks=banks[0:3]: pe(e, si, bks), reads=[slotB[sg_]] + xTB,
                         writes=[b_[1] for b_ in banks[0:3]])
                    S.op("pe", lambda e, pe=pe, si=su_, bks=banks[3:6]: pe(e, si, bks), reads=[slotB[su_]] + xTB,
                         writes=[b_[1] for b_ in banks[3:6]])
                    ui = ug_i[0] % 2
                    ug_i[0] += 1
                    for tg in range(3):
                        S.op("act", lambda e, tg=tg, ui=ui, bks=banks: e.activation(ug[ui][:, tg, :], bks[tg][0][:, 0:384], AF.Silu),
                             reads=[banks[tg][1]], writes=[ugB[ui]])
                    for tg in range(3):
                        S.op("dve", lambda e, tg=tg, ui=ui, bks=banks, jl=jl: e.tensor_tensor(
                            A[:, jl, tg * 384:(tg + 1) * 384], ug[ui][:, tg, :], bks[3 + tg][0][:, 0:384], ALU.mult),
                            reads=[ugB[ui], banks[3 + tg][1]], writes=[AB[jl]])
                    flush(1)
            kc_n = 4 * (b1 - b0)
            last = (pi == len(parts) - 1)
            if last:
                load_bc(g_row, g_row + 1)
            proj_tok(lambda kc, t: A[:, kc, t * P:(t + 1) * P], lambda t, kc_n=kc_n: AB[0:kc_n], kc_n,
                     w_ffn_out[l], b0 * 512, 0, 4, epi_store(Rs[pi], 0, eng="dve"),
                     last_cb=(lambda t: ln_tile(t, x_src, Rs, x_dst, want_xT=not final)) if last else None)
        flush_all()

    for t in range(NTL):
        i = t % 2
        S.op("sp", lambda e, t=t, i=i: e.dma_start(out=xa[i][:], in_=xin[t * P:(t + 1) * P, :]), writes=[bf(f"xa{i}")], dkey=f"xa{i}")
        S.op("act", lambda e, i=i: e.copy(xb[0][:], xa[i][:]), reads=[bf(f"xa{i}")], writes=[bf("xb0")])
        make_xT(t, xb[0], bf("xb0"))
        flush_all()

    load_bc(0, 1)

    def gmlp_tile(t):
        var = 1 if t == NTL - 1 else 0
        rows = slice(t * P, (t + 1) * P)
        S.op("sp", lambda e: e.dma_start(out=gu[:], in_=zscr[rows, 0:2048]), reads=[bf(f"zscr_{t}")], writes=[bf("xa0")], dkey="xa0")
        S.op("sp", lambda e: e.dma_start(out=gv[:], in_=zscr[rows, 2048:4096]), reads=[bf(f"zscr_{t}")], writes=[bf("xa1")], dkey="xa1")
        stats(gv, 4, LN_EPS, [bf("xa1")])
        S.op("act", lambda e: e.activation(gv[:], gv[:], AF.Identity, bias=nmr[:], scale=rstd[:]), reads=[stat_b], writes=[bf("xa1")])
        S.op("dve", lambda e: e.tensor_tensor(gv[:], gv[:], gbc[:], ALU.mult), reads=[bf("gbc")], writes=[bf("xa1")])
        S.op("dve", lambda e: e.tensor_tensor(gv[:], gv[:], bbc[:], ALU.add), reads=[bf("bbc")], writes=[bf("xa1")])
        if var == 1:
            S.op("sp", lambda e: e.dma_start(out=vout, in_=gv[:]), reads=[bf("xa1")], writes=[bf("vout")], dkey="vout")
        S.op("act", lambda e: e.copy(gvb[:], gv[:]), reads=[bf("xa1")], writes=[bf("rb")])
        gmlp_tile_pe(t, var)

    def gmlp_tile_pe(t, var):
        for q4 in range(4):
            bk, bb = next_bank()

            def pe(e, q4=q4, bk=bk):
                ins = None
                for gg in range(2):
                    g = q4 * 2 + gg
                    ins = e.matmul(bk[:, gg * 256:(gg + 1) * 256], WgT[:, var, g, :], gvb[:, g * 256:(g + 1) * 256],
                                   start=True, stop=True)
                return ins

            S.op("pe", pe, reads=[bf("WgT"), bf("rb")], writes=[bb])
            for gg in range(2):
                g = q4 * 2 + gg
                S.op("dve", lambda e, g=g, gg=gg, bk=bk: e.scalar_tensor_tensor(
                    gs[:, g * 256:(g + 1) * 256], bk[:, gg * 256:(gg + 1) * 256], bcol[:, var, g:g + 1],
                    gu[:, g * 256:(g + 1) * 256], ALU.add, ALU.mult), reads=[bb, bf("xa0"), bf("bcol")], writes=[bf("xb0")])

        transposes(lambda j: gs[:, j * P:(j + 1) * P], 16,
                   lambda g0, cnt: A[:, g0:g0 + cnt, t * P:(t + 1) * P], [bf("xb0")], [bf(f"sT{t}")])

    proj_tok(lambda kc, t: xT[:, kc, t * P:(t + 1) * P], lambda t: [xTB[t]], 16, w_a_in, 0, 0, 8,
             epi_store(zscr, 0, AF.Gelu), last_cb=gmlp_tile)
    flush_all()
    load_bc(2, 3)
    proj_tok(lambda kc, t: A[:, kc, t * P:(t + 1) * P], lambda t: [bf(f"sT{t}")], 16, w_a_out, 0, 0, 4,
             epi_store(Rs[0], 0, eng="dve"), last_cb=lambda t: ln_tile(t, xin, [Rs[0]], xs[0]))
    flush_all()
    ffn(0, xs[0], 4, xs[1], final=False)

    def epi_rope(dst, col0, dcol0):
        def epi(b, t, bk, bb):
            i = next_stg()
            i2 = next_stg()
            typ = 1 if t == NTL - 1 else 0
            for hh in range(2):
                h = 2 * b + hh
                dc = typ * 16 + dcol0 + h
                S.op("act", lambda e, hh=hh, dc=dc: e.activation(stg[i][:, hh * 256:(hh + 1) * 256], bk[:, hh * 256:(hh + 1) * 256],
                                                                  AF.Identity, scale=dqk[:, dc:dc + 1]),
                     reads=[bb, const_b], writes=[stgB[i]])
            cosb = cs[:, t:t + 1, 0:128].to_broadcast([P, 2, P])
            sinb = cs[:, t:t + 1, 128:256].to_broadcast([P, 2, P])
            xv = stg[i][:].rearrange("p (h two f) -> p h two f", two=2, f=P)
            tv = stg[i2][:].rearrange("p (h two f) -> p h two f", two=2, f=P)
            ov = stg[i2][:].bitcast(BF16)[:, 0:512].rearrange("p (h two f) -> p h two f", two=2, f=P)
            x1, x2 = xv[:, :, 0, :], xv[:, :, 1, :]
            t1, t2 = tv[:, :, 0, :], tv[:, :, 1, :]

            def f(e):
                e.tensor_tensor(t1, x1, cosb, ALU.mult)
                e.tensor_tensor(t2, x2, sinb, ALU.mult)
                return e.tensor_tensor(t1, t1, t2, ALU.subtract)

            def f2(e):
                e.tensor_tensor(t2, x1, sinb, ALU.mult)
                e.tensor_tensor(x1, x2, cosb, ALU.mult)
                return e.tensor_tensor(t2, t2, x1, ALU.add)

            S.op("dve", lambda e: e.tensor_tensor(t1, x1, cosb, ALU.mult), reads=[stgB[i], const_b], writes=[stgB[i2]])
            S.op("dve", lambda e: e.tensor_tensor(t2, x2, sinb, ALU.mult), reads=[stgB[i], const_b], writes=[stgB[i2]])
            S.op("dve", lambda e: e.tensor_tensor(t1, t1, t2, ALU.subtract), reads=[stgB[i2]], writes=[stgB[i2]])
            S.op("dve", lambda e: e.tensor_tensor(t2, x1, sinb, ALU.mult), reads=[stgB[i], const_b], writes=[stgB[i2]])
            S.op("dve", lambda e: e.tensor_tensor(x1, x2, cosb, ALU.mult), reads=[stgB[i], const_b], writes=[stgB[i]])
            S.op("dve", lambda e: e.tensor_tensor(t2, t2, x1, ALU.add), reads=[stgB[i], stgB[i2]], writes=[stgB[i2]])
            sv = stg[i][:].bitcast(BF16)[:, 0:512]
            S.op("act", lambda e: e.copy(sv, stg[i2][:]), reads=[stgB[i2]], writes=[stgB[i]])
            c = col0 + 512 * b
            S.op("sp", lambda e: e.dma_start(out=dst[t * P:(t + 1) * P, c:c + 512], in_=sv),
                 reads=[stgB[i]], writes=[bf(f"{dst.name}_{t}")], dkey=f"stg{i}")
        return epi

    xTf = lambda kc, t: xT[:, kc, t * P:(t + 1) * P]
    xTb = lambda t: [xTB[t]]
    proj_tok(xTf, xTb, 16, w_b_in, 0, 2048, 4, epi_rope(kd, 0, 8))
    proj_tok(xTf, xTb, 16, w_b_in, 0, 4096, 8, epi_store_bf(vd, 0))
    proj_tok(xTf, xTb, 16, w_b_in, 0, 0, 4, epi_rope(qd, 0, 0))
    proj_tok(xTf, xTb, 16, w_b_in, 0, 8192, 8, epi_store_bf(gd, 0, AF.Silu))
    flush_all()

    SBh = [bf(f"S{hl}") for hl in range(4)]

    def ret_tile(t, hg, ri, mode):
        rows = slice(t * P, (t + 1) * P)
        typ = 1 if t == NTL - 1 else 0
        full = (mode == "full")
        S.op("sp", lambda e: e.dma_start(out=rk[ri][:], in_=kd[rows, hg * 1024:(hg + 1) * 1024]), reads=[bf(f"kd_{t}")], writes=[bf(f"rk{ri}")], dkey=f"rk{ri}")
        S.op("sp", lambda e: e.dma_start(out=rv[ri][:], in_=vd[rows, hg * 2048:(hg + 1) * 2048]), reads=[bf(f"vd_{t}")], writes=[bf(f"rv{ri}")], dkey=f"rv{ri}")
        if full:
            S.op("sp", lambda e: e.dma_start(out=rq[ri][:], in_=qd[rows, hg * 1024:(hg + 1) * 1024]), reads=[bf(f"qd_{t}")], writes=[bf(f"rq{ri}")], dkey=f"rq{ri}")
            S.op("sp", lambda e: e.dma_start(out=rg[ri][:], in_=gd[rows, hg * 2048:(hg + 1) * 2048]), reads=[bf(f"gd_{t}")], writes=[bf(f"rg{ri}")], dkey=f"rg{ri}")
            transposes(lambda j: rq[ri][:, j * P:(j + 1) * P], 8, lambda g0, cnt: qT[:, g0:g0 + cnt, :], [bf(f"rq{ri}")], [bf("qT")], evac="act")
            transposes(lambda j: rk[ri][:, j * P:(j + 1) * P], 8, lambda g0, cnt: kT[:, g0:g0 + cnt, :], [bf(f"rk{ri}")], [bf("kT")], evac="act")
        if typ == 0:
            if full:
                bs, bsb = next_bank()

                def pe_sc(e, bs=bs):
                    ins = None
                    for hl in range(4):
                        e.matmul(bs[:, hl * P:(hl + 1) * P], kT[:, 2 * hl, :], qT[:, 2 * hl, :], start=True, stop=False)
                        ins = e.matmul(bs[:, hl * P:(hl + 1) * P], kT[:, 2 * hl + 1, :], qT[:, 2 * hl + 1, :], start=False, stop=True)
                    return ins

                S.op("pe", pe_sc, reads=[bf("qT"), bf("kT")], writes=[bsb])
                S.op("dve", lambda e, bs=bs: e.tensor_tensor(sc4[:], bs[:].rearrange("p (h t) -> p h t", t=P),
                                                             mask[:, 0:1, :].to_broadcast([P, 4, P]), ALU.mult),
                     reads=[bsb, const_b], writes=[bf("sc4")])
                obanks = [next_bank() for _ in range(4)]
                for hl in range(4):
                    bo = obanks[hl][0]

                    def pe_o(e, hl=hl, bo=bo):
                        e.matmul(bo[:], sc4[:, hl, :], rv[ri][:, hl * 512:(hl + 1) * 512], start=True, stop=False)
                        e.matmul(bo[:], qT[:, 2 * hl, :], Sb[:, 2 * hl, :], start=False, stop=False)
                        return e.matmul(bo[:], qT[:, 2 * hl + 1, :], Sb[:, 2 * hl + 1, :], start=False, stop=True)

                    S.op("pe", pe_o, reads=[bf("sc4"), bf(f"rv{ri}"), bf("qT"), bf("Sb")], writes=[obanks[hl][1]])

                def fstats(e):
                    ins = None
                    for hl in range(4):
                        ins = e.bn_stats(st6[:, hl, :], obanks[hl][0][:])
                    return ins

                def faggr(e):
                    ins = None
                    for hl in range(4):
                        ins = e.bn_aggr(mv4[:, hl, :], st6[:, hl, :])
                    return ins

                S.op("dve", fstats, reads=[ob[1] for ob in obanks], writes=[stat_b])
                S.op("dve", faggr, reads=[stat_b], writes=[stat_b])
                S.op("dve", lambda e: e.tensor_scalar_add(rstd4[:], mv4[:, :, 1], GN_EPS), reads=[stat_b], writes=[stat_b])
                S.op("act", lambda e: e.activation(rstd4[:], rstd4[:], AF.Sqrt), reads=[stat_b], writes=[stat_b])
                S.op("dve", lambda e: e.reciprocal(rstd4[:], rstd4[:]), reads=[stat_b], writes=[stat_b])
                S.op("dve", lambda e: e.scalar_tensor_tensor(nmr4[:], mv4[:, :, 0], -1.0, rstd4[:], ALU.mult, ALU.mult),
                     reads=[stat_b], writes=[stat_b])
                for hl in range(4):
                    S.op("act", lambda e, hl=hl: e.activation(onr4[:, hl, :], obanks[hl][0][:], AF.Identity,
                                                              bias=nmr4[:, hl:hl + 1], scale=rstd4[:, hl:hl + 1]),
                         reads=[obanks[hl][1], stat_b], writes=[ugB[0], ugB[1]])
                S.op("dve", lambda e: e.tensor_tensor(yin[:], onr4[:].rearrange("p a b -> p (a b)"), rg[ri][:], ALU.mult),
                     reads=[ugB[0], ugB[1], bf(f"rg{ri}")], writes=[bf("yin")])
            for hl in range(4):
                for dc in range(2):
                    bd, bdb = next_bank()
                    S.op("pe", lambda e, hl=hl, dc=dc, bd=bd: e.matmul(bd[:], rk[ri][:, hl * 256 + dc * P: hl * 256 + (dc + 1) * P],
                                                                     rv[ri][:, hl * 512:(hl + 1) * 512], start=True, stop=True),
                         reads=[bf(f"rk{ri}"), bf(f"rv{ri}")], writes=[bdb])
                    j = 2 * hl + dc
                    S.op("dve", lambda e, j=j, bd=bd: e.tensor_tensor(Sf[:, j, :], Sf[:, j, :], bd[:], ALU.add),
                         reads=[bdb], writes=[SBh[hl]])
                gL = GAM[hg * 4 + hl] ** 128
                S.op("dve", lambda e, hl=hl, gL=gL: e.tensor_scalar_mul(Sf[:, 2 * hl:2 * hl + 2, :], Sf[:, 2 * hl:2 * hl + 2, :], float(gL)),
                     writes=[SBh[hl]])
            if full:
                S.op("act", lambda e: e.copy(Sb[:].rearrange("p a b -> p (a b)"), Sf[:].rearrange("p a b -> p (a b)")),
                     reads=SBh, writes=[bf("Sb")])
        else:
            assert full
            S.op("dve", lambda e: e.memset(qTz[:].rearrange("p a b c -> p (a b c)"), 0.0), writes=[bf("qTz")])
            for i in range(4):
                S.op("dve", lambda e, i=i: e.tensor_copy(qTz[:, i, :, 32 * i:32 * i + 32], qT[:, :, 32 * i:32 * i + 32]),
                     reads=[bf("qT")], writes=[bf("qTz")])
            for hl in range(4):
                bs, bsb = next_bank()

                def pe_sc(e, hl=hl, bs=bs):
                    e.matmul(bs[:, 0:P], kT[:, 2 * hl, :], qT[:, 2 * hl, :], start=True, stop=False)
                    return e.matmul(bs[:, 0:P], kT[:, 2 * hl + 1, :], qT[:, 2 * hl + 1, :], start=False, stop=True)

                S.op("pe", pe_sc, reads=[bf("qT"), bf("kT")], writes=[bsb])
                S.op("dve", lambda e, bs=bs: e.tensor_tensor(sc[:], bs[:, 0:P], mask[:, 1, :], ALU.mult), reads=[bsb, const_b], writes=[bf("sc")])
                bo, bob = next_bank()
                S.op("pe", lambda e, hl=hl, bo=bo: e.matmul(bo[:], sc[:], rv[ri][:, hl * 512:(hl + 1) * 512], start=True, stop=True),
                     reads=[bf("sc"), bf(f"rv{ri}")], writes=[bob])
                S.op("act", lambda e, hl=hl, bo=bo: e.copy(osb[:, hl, :], bo[:]), reads=[bob], writes=[bf("osb")])
            for i in range(4):
                r0 = (i * 8 + hg * 4) * 256
                S.op("sp", lambda e, r0=r0: e.dma_start(out=Sf[:], in_=s0in[r0:r0 + 1024, :].rearrange("(j p) v -> p j v", p=P)),
                     writes=SBh, dkey="Sf")
                S.op("act", lambda e: e.copy(Sb[:], Sf[:]), reads=SBh, writes=[bf("Sb")])
                S.op("dve", lambda e, i=i: e.tensor_scalar_mul(ktz[:], rk[ri][:], rowm[:, i:i + 1]), reads=[bf(f"rk{ri}"), const_b], writes=[bf("ktz")])
                for hl in range(4):
                    h = hg * 4 + hl
                    gL = GAM[h] ** 32
                    bo, bob = next_bank()

                    def pe_c(e, hl=hl, bo=bo, i=i):
                        e.matmul(bo[:], qTz[:, i, 2 * hl, :], Sb[:, 2 * hl, :], start=True, stop=False)
                        return e.matmul(bo[:], qTz[:, i, 2 * hl + 1, :], Sb[:, 2 * hl + 1, :], start=False, stop=True)

                    S.op("pe", pe_c, reads=[bf("qTz"), bf("Sb")], writes=[bob])
                    S.op("dve", lambda e, hl=hl, bo=bo: e.tensor_tensor(osb[:, hl, :], osb[:, hl, :], bo[:], ALU.add), reads=[bob], writes=[bf("osb")])
                    for dc in range(2):
                        bd, bdb = next_bank()
                        S.op("pe", lambda e, hl=hl, dc=dc, bd=bd: e.matmul(bd[:], ktz[:, hl * 256 + dc * P: hl * 256 + (dc + 1) * P],
                                                                         rv[ri][:, hl * 512:(hl + 1) * 512], start=True, stop=True),
                             reads=[bf("ktz"), bf(f"rv{ri}")], writes=[bdb])
                        j = 2 * hl + dc
                        S.op("dve", lambda e, j=j, bd=bd: e.tensor_tensor(Sf[:, j, :], Sf[:, j, :], bd[:], ALU.add), reads=[bdb], writes=SBh)
                        S.op("dve", lambda e, j=j, gL=gL: e.tensor_scalar_mul(Sf[:, j, :], Sf[:, j, :], float(gL)), reads=SBh, writes=SBh)
                S.op("sp", lambda e, r0=r0: e.dma_start(out=sout_s[r0:r0 + 1024, :].rearrange("(j p) v -> p j v", p=P), in_=Sf[:]),
                     reads=SBh, writes=[bf("sout_s")], dkey=f"Sfo{i}")
            for hl in range(4):
                gn_head(None, None, hl, ri)
        if full:
            def go():
                transposes(lambda j: yin[:, j * P:(j + 1) * P], 16,
                           lambda g0, cnt: A[:, g0:g0 + cnt, t * P:(t + 1) * P], [bf("yin")], [bf(f"sT{t}")])
            go()

    def gn_head(bo, bob, hl, ri):
        if bo is not None:
            src, sb_ = bo, [bob]
        else:
            src, sb_ = osb[:, hl, :], [bf("osb")]
        stats(src, 1, GN_EPS, sb_)
        S.op("act", lambda e: e.activation(onr[:], src[:] if bo is not None else src, AF.Identity, bias=nmr[:], scale=rstd[:]),
             reads=sb_ + [stat_b], writes=[bf("onr")])
        S.op("dve", lambda e: e.tensor_tensor(yin[:, hl * 512:(hl + 1) * 512], onr[:], rg[ri][:, hl * 512:(hl + 1) * 512], ALU.mult),
             reads=[bf("onr"), bf(f"rg{ri}")], writes=[bf("yin")])

    region_c_barrier(LN_NAMES, RET_NAMES)
    S.op("dve", lambda e: e.memset(sc[:, 0:2], 0.0), writes=xTB + SBh + [bf("Sb"), bf("osb"), bf("sc")])

    if exchange:
        for hg in range(2):
            S.op("dve", lambda e: e.memset(Sf[:].rearrange("p a b -> p (a b)"), 0.0), writes=SBh)
            for t in range(NPT):
                ret_tile(t, hg, t % 2, "state")
            for hl in range(4):
                h = hg * 4 + hl
                S.op("sp", lambda e, h=h, hl=hl: e.dma_start(out=sxh[h].rearrange("(j p) v -> p j v", p=P), in_=Sf[:, 2 * hl:2 * hl + 2, :]),
                     reads=SBh, writes=[bf(f"sx{h}")], dkey=f"sx{h}")
                S.op("pool", lambda e, h=h: e.collective_compute("AllGather", ALU.bypass, replica_groups=[[0, 1], [2, 3], [4, 5], [6, 7]],
                                                                 ins=[sxh[h]], outs=[gxh[h]]), reads=[bf(f"sx{h}")], writes=[bf(f"gx{h}")])

    for hg in range(2):
        if exchange:
            for hl in range(4):
                h = hg * 4 + hl
                S.op("sp", lambda e, h=h, hl=hl: e.dma_start(out=Sf[:, 2 * hl:2 * hl + 2, :], in_=gxh[h][0:256, :].rearrange("(j p) v -> p j v", p=P)),
                     reads=[bf(f"gx{h}")], writes=SBh, dkey="Sf")
            S.op("dve", lambda e: e.tensor_scalar_mul(Sf[:].rearrange("p a b -> p (a b)"), Sf[:].rearrange("p a b -> p (a b)"), flag[:, 0:1]),
                 reads=[const_b], writes=SBh)
        else:
            S.op("dve", lambda e: e.memset(Sf[:].rearrange("p a b -> p (a b)"), 0.0), writes=SBh)
        S.op("act", lambda e: e.copy(Sb[:].rearrange("p a b -> p (a b)"), Sf[:].rearrange("p a b -> p (a b)")), reads=SBh, writes=[bf("Sb")])
        for t in range(NTL):
            if t == NPT:
                S.op("sp", lambda e, hg=hg: e.dma_start(out=sout_p[hg * 1024:(hg + 1) * 1024, :].rearrange("(j p) v -> p j v", p=P), in_=Sf[:]),
                     reads=SBh, writes=[bf("sout_p")], dkey=f"sop{hg}")
            ret_tile(t, hg, t % 2, "full")
        last = (hg == 1)
        if last:
            region_c_barrier(RET_NAMES, LN_NAMES)
            S.op("dve", lambda e: e.memset(sc[:, 0:2], 0.0), writes=SBh + [bf("Sb"), bf("osb")] + xTB + [bf("sc")])
            load_bc(6, 7)
        proj_tok(lambda kc, t: A[:, kc, t * P:(t + 1) * P], lambda t: [bf(f"sT{t}")], 16, w_b_out, hg * 2048, 0, 4,
                 epi_store(Rs[hg], 0, eng="dve"),
                 last_cb=(lambda t: ln_tile(t, xs[1], [Rs[0], Rs[1]], xs[2])) if last else None)
    flush_all()
    ffn(1, xs[2], 8, yout, final=True)

    outs = [bf(f"yout_{t}") for t in range(NTL)] + [bf("sout_s"), bf("sout_p"), bf("vout")]
    S.op("sp", lambda e: e.nop(), reads=outs)
    with nc.Block() as block:
        S.emit(block)
    return nc


def _consts(half):
    c = {}
    c["c_ident"] = np.eye(P, dtype=np.float32)
    half_d = 128
    inv = np.power(np.float32(10000.0), -np.arange(half_d, dtype=np.float32) / np.float32(half_d)).astype(np.float32)
    cs = np.zeros((P, NTL, 256), np.float32)
    for t in range(NTL):
        if t < NPT:
            pos = (half * 1024 + t * P + np.arange(P)).astype(np.float32)
        else:
            pos = (4096 + (np.arange(P) % 32)).astype(np.float32)
        ang = (pos[:, None] * inv[None, :]).astype(np.float32)
        cs[:, t, 0:128] = np.cos(ang)
        cs[:, t, 128:256] = np.sin(ang)
    c["c_cs"] = cs.reshape(P, NTL * 256)
    dqk = np.zeros((P, 32), np.float64)
    for typ in range(2):
        tl = np.arange(P) if typ == 0 else (np.arange(P) % 32)
        for h in range(8):
            lg = np.log1p(-2.0 ** (-5.0 - h))
            dqk[:, typ * 16 + h] = np.exp(lg * (tl + 1.0)) * (256.0 ** -0.5)
            dqk[:, typ * 16 + 8 + h] = np.exp(-lg * (tl + 1.0))
    c["c_dqk"] = dqk.astype(np.float32)
    s = np.arange(P)[:, None]
    t = np.arange(P)[None, :]
    m0 = (s <= t).astype(np.float32)
    m1 = ((s <= t) & ((s // 32) == (t // 32))).astype(np.float32)
    c["c_mask"] = np.concatenate([m0, m1], axis=1)
    rowm = np.zeros((P, 4), np.float32)
    for i in range(4):
        rowm[32 * i:32 * i + 32, i] = 1.0
    c["c_rowm"] = rowm
    c["c_flag"] = np.full((P, 1), float(half), np.float32)
    return c


_NC_CACHE = {}


def kernel(x_prompt, x_sample, state_ret, w_a_in, a_ln_g, a_ln_b, a_ws, a_bs, w_a_out,
           w_b_in, w_b_out, w_ffn_in, w_ffn_out, ln_mix_g, ln_mix_b, ln_ffn_g, ln_ffn_b):
    f = lambda a: np.ascontiguousarray(np.asarray(a, dtype=np.float32))
    x_prompt, x_sample, state_ret = f(x_prompt), f(x_sample), f(state_ret)
    lnv = np.stack([f(a_ln_g)[0], f(a_ln_b)[0], f(ln_mix_g)[0], f(ln_mix_b)[0], f(ln_ffn_g)[0], f(ln_ffn_b)[0],
                    f(ln_mix_g)[1], f(ln_mix_b)[1], f(ln_ffn_g)[1], f(ln_ffn_b)[1]], axis=0)
    shared = {
        "w_a_in": f(w_a_in)[0], "w_a_out": f(w_a_out)[0], "w_b_in": f(w_b_in)[0], "w_b_out": f(w_b_out)[0],
        "w_ffn_in0": f(w_ffn_in)[0], "w_ffn_in1": f(w_ffn_in)[1], "w_ffn_out0": f(w_ffn_out)[0], "w_ffn_out1": f(w_ffn_out)[1],
        "a_ws": f(a_ws)[0].reshape(8 * 128, 128), "a_bs": f(a_bs)[0], "lnv": lnv,
    }
    in_maps = []
    for c in range(8):
        b, half = c // 2, c % 2
        m = dict(shared)
        m["xin"] = np.concatenate([x_prompt[b, half * 1024:(half + 1) * 1024], x_sample[4 * c:4 * c + 4].reshape(128, D)], axis=0)
        m["s0in"] = state_ret[0, 4 * c:4 * c + 4].reshape(4 * 8 * 256, 512)
        m.update(_consts(half))
        in_maps.append(m)
    if "nc" not in _NC_CACHE:
        _NC_CACHE["nc"] = build_nc()
    res = run_bass_kernel_spmd(_NC_CACHE["nc"], in_maps, core_ids=list(range(8)))
    r = res.results
    y_prompt = np.zeros((4, 2048, D), np.float32)
    y_sample = np.zeros((32, 32, D), np.float32)
    rsp = np.zeros((1, 4, 8, 256, 512), np.float32)
    rss = np.zeros((1, 32, 8, 256, 512), np.float32)
    gv = np.zeros((1, 32, 32, D), np.float32)
    for c in range(8):
        b, half = c // 2, c % 2
        y = np.asarray(r[c]["yout"])
        y_prompt[b, half * 1024:(half + 1) * 1024] = y[:1024]
        y_sample[4 * c:4 * c + 4] = y[1024:].reshape(4, 32, D)
        if half == 1:
            rsp[0, b] = np.asarray(r[c]["sout_p"]).reshape(8, 256, 512)
        rss[0, 4 * c:4 * c + 4] = np.asarray(r[c]["sout_s"]).reshape(4, 8, 256, 512)
        gv[0, 4 * c:4 * c + 4] = np.asarray(r[c]["vout"]).reshape(4, 32, D)
    return (y_prompt, y_sample, rsp, rss, gv)
```

```python
import numpy as np
import concourse.bass as bass
import concourse.mybir as mybir
from concourse.bass_utils import run_bass_kernel_spmd

F32 = mybir.dt.float32
BF16 = mybir.dt.bfloat16
AF = mybir.ActivationFunctionType
ALU = mybir.AluOpType
P = 128
D = 2048
NTL = 9
NPT = 8
NT = NTL * P
FH = 5632
ALPHA = 4.0 ** 0.25
LN_EPS = 1e-5
GN_EPS = 1e-6
NSLOT = 3
GAM = [1.0 - 2.0 ** (-5.0 - h) for h in range(8)]


class Buf:
    __slots__ = ("name", "w", "r")

    def __init__(self, name):
        self.name = name
        self.w = {}
        self.r = {}


class Sched:
    ENG = ("pe", "act", "dve", "pool", "sp")

    def __init__(self, nc):
        self.nc = nc
        self.ops = []
        self.semh = {}
        self.semc = {}
        self.cur = {e: 0 for e in self.ENG}
        self.small = set()

    def _sem(self, key):
        if key not in self.semh:
            self.semh[key] = self.nc.alloc_semaphore(key)
            self.semc[key] = 0
        return key

    def op(self, eng, fn, reads=(), writes=(), dkey=None):
        waits = {}

        def add(evs):
            for s, v in evs.items():
                if waits.get(s, 0) < v:
                    waits[s] = v

        for b in reads:
            add(b.w)
        for b in writes:
            add(b.r)
            add(b.w)
        if dkey is not None:
            key = self._sem("d_" + dkey)
            inc = 16
        else:
            key = self._sem(f"{eng}{self.cur[eng]}")
            if self.semc[key] >= 2000:
                self.cur[eng] += 1
                key = self._sem(f"{eng}{self.cur[eng]}")
            inc = 1
        self.semc[key] += inc
        v = self.semc[key]
        if eng == "pe":
            waits = {s: x for s, x in waits.items() if not s.startswith("pe")}
        elif eng in ("act", "dve") and not any(b in self.small for b in list(reads) + list(writes)):
            waits = {s: x for s, x in waits.items() if not s.startswith(eng)}
        self.ops.append((eng, fn, waits, key, inc))
        for b in reads:
            if b not in writes:
                if b.r.get(key, 0) < v:
                    b.r[key] = v
        for b in writes:
            if b.r:
                b.w = {key: v}
                b.r = {}
            else:
                b.w[key] = v
        return (key, v)

    def emit(self, block):
        engs = {"pe": block.tensor, "act": block.scalar, "dve": block.vector, "pool": block.gpsimd, "sp": block.sync}
        for ename, deco in engs.items():
            myops = [o for o in self.ops if o[0] == ename]

            def body(eng, myops=myops):
                seen = {}
                for (_, fn, waits, key, inc) in myops:
                    for s, v in waits.items():
                        if seen.get(s, 0) >= v:
                            continue
                        seen[s] = v
                        eng.wait_ge(self.semh[s], v)
                    ins = fn(eng)
                    ins.then_inc(self.semh[key], inc)

            deco(body)


def build_nc(exchange=True, debug=False):
    nc = bass.Bass("TRN2", target_bir_lowering=False)
    S = Sched(nc)

    def din(name, shape, dt=F32):
        return nc.dram_tensor(name, list(shape), dt, kind="ExternalInput").ap()

    def dout(name, shape, dt=F32):
        return nc.dram_tensor(name, list(shape), dt, kind="ExternalOutput").ap()

    def dscr(name, shape, dt=F32):
        if debug and name in ("xs0", "xs1", "xs2", "qd", "kd", "vd", "gd", "zscr"):
            return nc.dram_tensor(name, list(shape), dt, kind="ExternalOutput").ap()
        return nc.dram_tensor(name, list(shape), dt).ap()

    xin = din("xin", [NT, D])
    s0in = din("s0in", [4 * 8 * 256, 512])
    w_a_in = din("w_a_in", [D, 4096])
    w_a_out = din("w_a_out", [D, D])
    w_b_in = din("w_b_in", [D, 12288])
    w_b_out = din("w_b_out", [4096, D])
    w_ffn_in = [din(f"w_ffn_in{l}", [D, 2 * FH]) for l in range(2)]
    w_ffn_out = [din(f"w_ffn_out{l}", [FH, D]) for l in range(2)]
    a_ws = din("a_ws", [8 * 128, 128])
    a_bs = din("a_bs", [8, 128])
    lnv = din("lnv", [10, D])
    c_ident = din("c_ident", [P, P])
    c_cs = din("c_cs", [P, NTL * 256])
    c_dqk = din("c_dqk", [P, 32])
    c_mask = din("c_mask", [P, 256])
    c_rowm = din("c_rowm", [P, 4])
    c_flag = din("c_flag", [P, 1])

    yout = dout("yout", [NT, D])
    sout_s = dout("sout_s", [4 * 8 * 256, 512])
    sout_p = dout("sout_p", [8 * 256, 512])
    vout = dout("vout", [P, D])

    zscr = dscr("zscr", [NT, 4096])
    xs = [dscr(f"xs{i}", [NT, D]) for i in range(3)]
    Rs = [dscr(f"R{i}", [NT, D]) for i in range(3)]
    qd = dscr("qd", [NT, D], BF16)
    kd = dscr("kd", [NT, D], BF16)
    vd = dscr("vd", [NT, 4096], BF16)
    gd = dscr("gd", [NT, 4096], BF16)
    sxh = [dscr(f"sx{h}", [256, 512]) for h in range(8)]
    gxh = [dscr(f"gx{h}", [512, 512]) for h in range(8)]

    off = [16512]

    def sb(name, shape, dt, at=None):
        nbytes = int(np.prod(shape[1:])) * (2 if dt == BF16 else 4)
        if at is None:
            o = off[0]
            off[0] += (nbytes + 31) // 32 * 32
        else:
            o = at
        return nc.alloc_sbuf_tensor_at(name, list(shape), dt, offset=o)

    ident = sb("ident", [P, P], BF16)
    identf = sb("identf", [P, P], F32)
    mask = sb("mask", [P, 2, P], F32)
    cs = sb("cs", [P, NTL, 256], F32)
    dqk = sb("dqk", [P, 32], F32)
    rowm = sb("rowm", [P, 4], F32)
    flag = sb("flag", [P, 1], F32)
    WgT = sb("WgT", [P, 2, 8, P], BF16)
    bcol = sb("bcol", [P, 2, 8], F32)
    st6 = sb("st6", [P, 4, 6], F32)
    mv = sb("mv", [P, 2], F32)
    rstd = sb("rstd", [P, 1], F32)
    nmr = sb("nmr", [P, 1], F32)
    sc = sb("sc", [P, P], BF16)
    xT = sb("xT", [P, 16, NT], BF16)
    XT_OFF = off[0] - 16 * NT * 2
    slots = [sb(f"slot{i}", [P, 16, 512], BF16) for i in range(NSLOT)]
    A = sb("A", [P, 16, NT], BF16)
    stg = [sb(f"stg{i}", [P, 512], F32) for i in range(4)]
    UG_OFF = off[0]
    ug = [sb(f"ug{i}", [P, 3, 384], F32) for i in range(2)]
    onr4 = sb("onr4", [P, 4, 512], F32, at=UG_OFF)
    sc4 = sb("sc4", [P, 4, P], BF16)
    mv4 = sb("mv4", [P, 4, 2], F32)
    rstd4 = sb("rstd4", [P, 4], F32)
    nmr4 = sb("nmr4", [P, 4], F32)
    xb2 = sb("xb2", [P, D], BF16)
    C_OFF = off[0]
    C_SIZE = 49152
    off[0] += C_SIZE
    assert off[0] <= 229344, off[0]
    xa = [sb(f"xa{i}", [P, D], F32, at=C_OFF + i * 8192) for i in range(2)]
    rb = sb("rb", [P, D], F32, at=C_OFF + 16384)
    xb = [sb("xb0", [P, D], BF16, at=C_OFF + 24576), sb("xb1", [P, D], BF16, at=C_OFF + 45056), xb2]
    gbc = sb("gbc", [P, D], F32, at=C_OFF + 28672)
    bbc = sb("bbc", [P, D], F32, at=C_OFF + 36864)
    gu = xa[0]
    gv = xa[1]
    gvb = sb("gvb", [P, D], BF16, at=C_OFF + 16384)
    gs = sb("gs", [P, D], BF16, at=C_OFF + 24576)
    rq = [sb(f"rq{i}", [P, 1024], BF16, at=C_OFF + i * 12288) for i in range(2)]
    rk = [sb(f"rk{i}", [P, 1024], BF16, at=C_OFF + i * 12288 + 2048) for i in range(2)]
    rv = [sb(f"rv{i}", [P, 2048], BF16, at=C_OFF + i * 12288 + 4096) for i in range(2)]
    rg = [sb(f"rg{i}", [P, 2048], BF16, at=C_OFF + i * 12288 + 8192) for i in range(2)]
    qT = sb("qT", [P, 8, P], BF16, at=C_OFF + 24576)
    kT = sb("kT", [P, 8, P], BF16, at=C_OFF + 26624)
    qTz = sb("qTz", [P, 4, 8, P], BF16, at=C_OFF + 28672)
    ktz = sb("ktz", [P, 1024], BF16, at=C_OFF + 36864)
    yin = sb("yin", [P, 2048], BF16, at=C_OFF + 38912)
    onr = sb("onr", [P, 512], F32, at=C_OFF + 43008)
    Sf = sb("Sf", [P, 8, 512], F32, at=XT_OFF)
    Sb = sb("Sb", [P, 8, 512], BF16, at=XT_OFF + 16384)
    osb = sb("osb", [P, 4, 512], F32, at=XT_OFF + 24576)

    pbank = [nc.alloc_psum_tensor(f"pb{i}", [P, 512], F32) for i in range(8)]
    bankB = [Buf(f"bank{i}") for i in range(8)]
    bank_i = [0]

    def next_bank():
        i = bank_i[0] % 8
        bank_i[0] += 1
        return pbank[i], bankB[i]

    slotB = [Buf(f"slot{i}") for i in range(NSLOT)]
    slot_i = [0]

    def next_slot():
        i = slot_i[0] % NSLOT
        slot_i[0] += 1
        return i

    xTB = [Buf(f"xT{t}") for t in range(NTL)]
    AB = [Buf(f"A{j}") for j in range(16)]
    stgB = [Buf(f"stg{i}") for i in range(4)]
    stg_i = [0]
    ugB = [Buf(f"ug{i}") for i in range(2)]
    ug_i = [0]
    B = {}

    def bf(name):
        if name not in B:
            B[name] = Buf(name)
        return B[name]

    cbufs = [bf(n) for n in ("xa0", "xa1", "rb", "xb0", "xb1", "gbc", "bbc")]
    const_b = bf("const")
    stat_b = bf("stat")
    S.small.add(stat_b)

    deferred = []

    DEPTH = 2

    def flush(n=1):
        while len(deferred) > DEPTH:
            deferred.pop(0)()

    def flush_all():
        while deferred:
            deferred.pop(0)()

    def wblock_src(w, r0, kc, c0):
        return w[r0:r0 + kc * P, c0:c0 + 512].rearrange("(kc p) n -> p kc n", p=P)

    def load_slot(w, r0, kc, c0):
        si = next_slot()
        src = wblock_src(w, r0, kc, c0)
        S.op("pool", lambda e: e.dma_start(out=slots[si][:, 0:kc, :], in_=src), writes=[slotB[si]], dkey=f"slot{si}")
        return si

    def ld(dst_ap, src_ap, b, key, eng="sp", **kw):
        S.op(eng, lambda e: e.dma_start(out=dst_ap, in_=src_ap, **kw), writes=[b], dkey=key)

    ld(identf[:], c_ident, const_b, "c0")
    ld(mask[:].rearrange("p a b -> p (a b)"), c_mask, const_b, "c1")
    ld(cs[:].rearrange("p a b -> p (a b)"), c_cs, const_b, "c2")
    ld(dqk[:], c_dqk, const_b, "c3")
    ld(rowm[:], c_rowm, const_b, "c4")
    ld(flag[:], c_flag, const_b, "c5")
    S.op("act", lambda e: e.copy(ident[:], identf[:]), reads=[const_b], writes=[bf("ident")])
    identB = bf("ident")

    def transposes(src_fn, n, dst_fn, src_bufs, dst_bufs, evac="dve", post=None):
        for g0 in range(0, n, 8):
            cnt = min(8, n - g0)
            bk, bb = next_bank()
            bkv = bk[:].bitcast(BF16)

            def pe(e, g0=g0, cnt=cnt, bkv=bkv):
                ins = None
                for j in range(cnt):
                    ins = e.transpose(bkv[:, j * P:(j + 1) * P], src_fn(g0 + j), ident[:])
                return ins

            S.op("pe", pe, reads=list(src_bufs) + [identB], writes=[bb])
            dst = dst_fn(g0, cnt) if dst_fn is not None else None
            src = bkv[:, 0:cnt * P].rearrange("p (a b) -> p a b", b=P)
            if post is not None:
                post(g0, cnt, src, bb)
            elif evac == "dve":
                S.op("dve", lambda e, dst=dst, src=src: e.tensor_copy(dst, src), reads=[bb], writes=list(dst_bufs))
            else:
                S.op("act", lambda e, dst=dst, src=src: e.copy(dst, src), reads=[bb], writes=list(dst_bufs))

    wtmp = gu
    wtb = gvb
    gB = [bf("xa0"), bf("rb")]
    for var in range(2):
        if var == 0:
            S.op("sp", lambda e: e.dma_start(out=wtmp[:, 0:1024].rearrange("p (g s) -> p g s", s=P),
                                             in_=a_ws.rearrange("(g t) s -> t g s", t=P)), writes=[gB[0]], dkey="xa0")
        else:
            S.op("dve", lambda e: e.memset(wtmp[:, 0:1024], 0.0), writes=[gB[0]])
            for i in range(4):
                S.op("sp", lambda e, i=i: e.dma_start(
                    out=wtmp[32 * i:32 * i + 32, 0:1024].rearrange("p (g s) -> p g s", s=P)[:, :, 32 * i:32 * i + 32],
                    in_=a_ws.rearrange("(g t) s -> t g s", t=P)[0:32, :, 0:32]), writes=[gB[0]], dkey=f"wd{i}")
        S.op("act", lambda e: e.copy(wtb[:, 0:1024], wtmp[:, 0:1024]), reads=[gB[0]], writes=[gB[1]])

        def post(g0, cnt, src, bb, var=var):
            S.op("dve", lambda e: e.tensor_tensor(WgT[:, var, g0:g0 + cnt, :], src,
                                                  mask[:, var:var + 1, :].to_broadcast([P, cnt, P]), ALU.mult),
                 reads=[bb, const_b], writes=[bf("WgT")])

        transposes(lambda j: wtb[:, j * P:(j + 1) * P], 8, None, [gB[1]], [], post=post)
    ld(bcol[:, 0, :], a_bs.rearrange("g p -> p g"), bf("bcol"), "c6", allow_slow_non_contiguous=True)
    for i in range(4):
        ld(bcol[32 * i:32 * i + 32, 1, :], a_bs.rearrange("g p -> p g")[0:32, :], bf("bcol"), f"c7{i}", allow_slow_non_contiguous=True)

    def matmul_group(bk, stat_fn, kc_n, si):
        def pe(e):
            ins = None
            for kc in range(kc_n):
                ins = e.matmul(bk[:], stat_fn(kc), slots[si][:, kc, :], start=(kc == 0), stop=(kc == kc_n - 1))
            return ins
        return pe

    def proj_tok(stat_fn, stat_bufs_fn, kc_n, w, r0, col0, nblocks, epi, last_cb=None):
        for b in range(nblocks):
            si = load_slot(w, r0, kc_n, col0 + 512 * b)
            for t in range(NTL):
                bk, bb = next_bank()
                S.op("pe", matmul_group(bk, lambda kc, t=t: stat_fn(kc, t), kc_n, si),
                     reads=[slotB[si]] + stat_bufs_fn(t), writes=[bb])
                epi(b, t, bk, bb)
                if last_cb is not None and b == nblocks - 1:
                    last_cb(t)
                flush(1)

    def next_stg():
        i = stg_i[0] % 4
        stg_i[0] += 1
        return i

    def epi_store(dst, col0, func=None, eng="act"):
        def epi(b, t, bk, bb):
            i = next_stg()
            if eng == "act":
                S.op("act", lambda e: e.activation(stg[i][:], bk[:], func if func is not None else AF.Copy),
                     reads=[bb], writes=[stgB[i]])
            else:
                S.op("dve", lambda e: e.tensor_copy(stg[i][:], bk[:]), reads=[bb], writes=[stgB[i]])
            c = col0 + 512 * b
            S.op("sp", lambda e: e.dma_start(out=dst[t * P:(t + 1) * P, c:c + 512], in_=stg[i][:]),
                 reads=[stgB[i]], writes=[bf(f"{dst.name}_{t}")], dkey=f"stg{i}")
        return epi

    def epi_store_bf(dst, col0, func=None):
        def epi(b, t, bk, bb):
            i = next_stg()
            sv = stg[i][:].bitcast(BF16)[:, 0:512]
            S.op("act", lambda e: e.activation(sv, bk[:], func if func is not None else AF.Copy),
                 reads=[bb], writes=[stgB[i]])
            c = col0 + 512 * b
            S.op("sp", lambda e: e.dma_start(out=dst[t * P:(t + 1) * P, c:c + 512], in_=sv),
                 reads=[stgB[i]], writes=[bf(f"{dst.name}_{t}")], dkey=f"stg{i}")
        return epi

    def load_bc(row_g, row_b):
        S.op("sp", lambda e: e.dma_start(out=gbc[:], in_=lnv[row_g:row_g + 1, :].partition_broadcast(P)),
             writes=[bf("gbc")], dkey="gbc")
        S.op("sp", lambda e: e.dma_start(out=bbc[:], in_=lnv[row_b:row_b + 1, :].partition_broadcast(P)),
             writes=[bf("bbc")], dkey="bbc")

    def stats(src_ap, nchunk, eps, src_bufs):
        def f(e):
            ins = None
            for c in range(nchunk):
                ins = e.bn_stats(st6[:, c, :], src_ap[:, c * 512:(c + 1) * 512])
            return ins
        S.op("dve", f, reads=src_bufs, writes=[stat_b])
        S.op("dve", lambda e: e.bn_aggr(mv[:], st6[:, 0:nchunk, :].rearrange("p a b -> p (a b)")), reads=[stat_b], writes=[stat_b])
        S.op("dve", lambda e: e.tensor_scalar_add(rstd[:], mv[:, 1:2], eps), reads=[stat_b], writes=[stat_b])
        S.op("act", lambda e: e.activation(rstd[:], rstd[:], AF.Sqrt), reads=[stat_b], writes=[stat_b])
        S.op("dve", lambda e: e.reciprocal(rstd[:], rstd[:]), reads=[stat_b], writes=[stat_b])
        S.op("dve", lambda e: e.scalar_tensor_tensor(nmr[:], mv[:, 0:1], -1.0, rstd[:], ALU.mult, ALU.mult),
             reads=[stat_b], writes=[stat_b])

    def make_xT(t, src_bf_ap, src_buf):
        def go():
            transposes(lambda j: src_bf_ap[:, j * P:(j + 1) * P], 16,
                       lambda g0, cnt: xT[:, g0:g0 + cnt, t * P:(t + 1) * P], [src_buf], [xTB[t]])
        deferred.append(go)

    ln_i = [0]
    xb_i = [0]

    def ln_tile(t, x_src, R_list, x_dst, want_xT=True):
        i = ln_i[0] % 2
        ln_i[0] += 1
        xab = bf(f"xa{i}")
        rows = slice(t * P, (t + 1) * P)
        S.op("sp", lambda e: e.dma_start(out=xa[i][:], in_=x_src[rows, :]), reads=[bf(f"{x_src.name}_{t}")], writes=[xab], dkey=f"xa{i}")
        for p, R in enumerate(R_list):
            S.op("sp", lambda e, R=R: e.dma_start(out=rb[:], in_=R[rows, :]), reads=[bf(f"{R.name}_{t}")],
                 writes=[bf("rb")], dkey="rb")
            if p == 0:
                S.op("dve", lambda e: e.scalar_tensor_tensor(xa[i][:], xa[i][:], ALPHA, rb[:], ALU.mult, ALU.add),
                     reads=[bf("rb")], writes=[xab])
            else:
                S.op("dve", lambda e: e.tensor_tensor(xa[i][:], xa[i][:], rb[:], ALU.add), reads=[bf("rb")], writes=[xab])
        stats(xa[i], 4, LN_EPS, [xab])
        S.op("act", lambda e: e.activation(xa[i][:], xa[i][:], AF.Identity, bias=nmr[:], scale=rstd[:]),
             reads=[stat_b], writes=[xab])
        S.op("dve", lambda e: e.tensor_tensor(xa[i][:], xa[i][:], gbc[:], ALU.mult), reads=[bf("gbc")], writes=[xab])
        S.op("dve", lambda e: e.tensor_tensor(xa[i][:], xa[i][:], bbc[:], ALU.add), reads=[bf("bbc")], writes=[xab])
        S.op("sp", lambda e: e.dma_start(out=x_dst[rows, :], in_=xa[i][:]), reads=[xab],
             writes=[bf(f"{x_dst.name}_{t}")], dkey=f"xas{i}")
        if want_xT:
            xi = xb_i[0] % 3
            xb_i[0] += 1
            S.op("act", lambda e: e.copy(xb[xi][:], xa[i][:]), reads=[xab], writes=[bf(f"xb{xi}")])
            make_xT(t, xb[xi], bf(f"xb{xi}"))

    def region_c_barrier(from_names, to_names):
        S.op("dve", lambda e: e.memset(sc[:, 0:2], 0.0),
             writes=[bf(n) for n in from_names] + [bf(n) for n in to_names] + [bf("sc")])

    LN_NAMES = ["xa0", "xa1", "rb", "xb0", "xb1", "gbc", "bbc"]
    RET_NAMES = ["rq0", "rk0", "rv0", "rg0", "rq1", "rk1", "rv1", "rg1", "qT", "kT", "qTz", "ktz", "yin", "onr"]

    def ffn(l, x_src, g_row, x_dst, final):
        parts = [(0, 3), (3, 7), (7, 11)]
        for pi, (b0, b1) in enumerate(parts):
            for b in range(b0, b1):
                sg_ = load_slot(w_ffn_in[l], 0, 16, 512 * b)
                su_ = load_slot(w_ffn_in[l], 0, 16, FH + 512 * b)
                for cc in range(4):
                    jl = (b - b0) * 4 + cc
                    banks = [next_bank() for _ in range(6)]

                    def pe(e, si, bks, cc=cc):
                        ins = None
                        for kc in range(16):
                            for tg in range(3):
                                ins = e.matmul(bks[tg][0][:, 0:384], slots[si][:, kc, cc * P:(cc + 1) * P],
                                               xT[:, kc, tg * 384:(tg + 1) * 384], start=(kc == 0), stop=(kc == 15))
                        return ins

                    S.op("pe", lambda e, pe=pe, si=sg_, bks=banks[0:3]: pe(e, si, bks), reads=[slotB[sg_]] + xTB,
                         writes=[b_[1] for b_ in banks[0:3]])
                    S.op("pe", lambda e, pe=pe, si=su_, bks=banks[3:6]: pe(e, si, bks), reads=[slotB[su_]] + xTB,
                         writes=[b_[1] for b_ in banks[3:6]])
                    ui = ug_i[0] % 2
                    ug_i[0] += 1
                    for tg in range(3):
                        S.op("act", lambda e, tg=tg, ui=ui, bks=banks: e.activation(ug[ui][:, tg, :], bks[tg][0][:, 0:384], AF.Silu),
                             reads=[banks[tg][1]], writes=[ugB[ui]])
                    for tg in range(3):
                        S.op("dve", lambda e, tg=tg, ui=ui, bks=banks, jl=jl: e.tensor_tensor(
                            A[:, jl, tg * 384:(tg + 1) * 384], ug[ui][:, tg, :], bks[3 + tg][0][:, 0:384], ALU.mult),
                            reads=[ugB[ui], banks[3 + tg][1]], writes=[AB[jl]])
                    flush(1)
            kc_n = 4 * (b1 - b0)
            last = (pi == len(parts) - 1)
            if last:
                load_bc(g_row, g_row + 1)
            proj_tok(lambda kc, t: A[:, kc, t * P:(t + 1) * P], lambda t, kc_n=kc_n: AB[0:kc_n], kc_n,
                     w_ffn_out[l], b0 * 512, 0, 4, epi_store(Rs[pi], 0, eng="dve"),
                     last_cb=(lambda t: ln_tile(t, x_src, Rs, x_dst, want_xT=not final)) if last else None)
        flush_all()

    for t in range(NTL):
        i = t % 2
        S.op("sp", lambda e, t=t, i=i: e.dma_start(out=xa[i][:], in_=xin[t * P:(t + 1) * P, :]), writes=[bf(f"xa{i}")], dkey=f"xa{i}")
        S.op("act", lambda e, i=i: e.copy(xb[0][:], xa[i][:]), reads=[bf(f"xa{i}")], writes=[bf("xb0")])
        make_xT(t, xb[0], bf("xb0"))
        flush_all()

    load_bc(0, 1)

    def gmlp_tile(t):
        var = 1 if t == NTL - 1 else 0
        rows = slice(t * P, (t + 1) * P)
        S.op("sp", lambda e: e.dma_start(out=gu[:], in_=zscr[rows, 0:2048]), reads=[bf(f"zscr_{t}")], writes=[bf("xa0")], dkey="xa0")
        S.op("sp", lambda e: e.dma_start(out=gv[:], in_=zscr[rows, 2048:4096]), reads=[bf(f"zscr_{t}")], writes=[bf("xa1")], dkey="xa1")
        stats(gv, 4, LN_EPS, [bf("xa1")])
        S.op("act", lambda e: e.activation(gv[:], gv[:], AF.Identity, bias=nmr[:], scale=rstd[:]), reads=[stat_b], writes=[bf("xa1")])
        S.op("dve", lambda e: e.tensor_tensor(gv[:], gv[:], gbc[:], ALU.mult), reads=[bf("gbc")], writes=[bf("xa1")])
        S.op("dve", lambda e: e.tensor_tensor(gv[:], gv[:], bbc[:], ALU.add), reads=[bf("bbc")], writes=[bf("xa1")])
        if var == 1:
            S.op("sp", lambda e: e.dma_start(out=vout, in_=gv[:]), reads=[bf("xa1")], writes=[bf("vout")], dkey="vout")
        S.op("act", lambda e: e.copy(gvb[:], gv[:]), reads=[bf("xa1")], writes=[bf("rb")])
        gmlp_tile_pe(t, var)

    def gmlp_tile_pe(t, var):
        for q4 in range(4):
            bk, bb = next_bank()

            def pe(e, q4=q4, bk=bk):
                ins = None
                for gg in range(2):
                    g = q4 * 2 + gg
                    ins = e.matmul(bk[:, gg * 256:(gg + 1) * 256], WgT[:, var, g, :], gvb[:, g * 256:(g + 1) * 256],
                                   start=True, stop=True)
                return ins

            S.op("pe", pe, reads=[bf("WgT"), bf("rb")], writes=[bb])
            for gg in range(2):
                g = q4 * 2 + gg
                S.op("dve", lambda e, g=g, gg=gg, bk=bk: e.scalar_tensor_tensor(
                    gs[:, g * 256:(g + 1) * 256], bk[:, gg * 256:(gg + 1) * 256], bcol[:, var, g:g + 1],
                    gu[:, g * 256:(g + 1) * 256], ALU.add, ALU.mult), reads=[bb, bf("xa0"), bf("bcol")], writes=[bf("xb0")])

        transposes(lambda j: gs[:, j * P:(j + 1) * P], 16,
                   lambda g0, cnt: A[:, g0:g0 + cnt, t * P:(t + 1) * P], [bf("xb0")], [bf(f"sT{t}")])

    proj_tok(lambda kc, t: xT[:, kc, t * P:(t + 1) * P], lambda t: [xTB[t]], 16, w_a_in, 0, 0, 8,
             epi_store(zscr, 0, AF.Gelu), last_cb=gmlp_tile)
    flush_all()
    load_bc(2, 3)
    proj_tok(lambda kc, t: A[:, kc, t * P:(t + 1) * P], lambda t: [bf(f"sT{t}")], 16, w_a_out, 0, 0, 4,
             epi_store(Rs[0], 0, eng="dve"), last_cb=lambda t: ln_tile(t, xin, [Rs[0]], xs[0]))
    flush_all()
    ffn(0, xs[0], 4, xs[1], final=False)

    def epi_rope(dst, col0, dcol0):
        def epi(b, t, bk, bb):
            i = next_stg()
            i2 = next_stg()
            typ = 1 if t == NTL - 1 else 0
            for hh in range(2):
                h = 2 * b + hh
                dc = typ * 16 + dcol0 + h
                S.op("act", lambda e, hh=hh, dc=dc: e.activation(stg[i][:, hh * 256:(hh + 1) * 256], bk[:, hh * 256:(hh + 1) * 256],
                                                                  AF.Identity, scale=dqk[:, dc:dc + 1]),
                     reads=[bb, const_b], writes=[stgB[i]])
            cosb = cs[:, t:t + 1, 0:128].to_broadcast([P, 2, P])
            sinb = cs[:, t:t + 1, 128:256].to_broadcast([P, 2, P])
            xv = stg[i][:].rearrange("p (h two f) -> p h two f", two=2, f=P)
            tv = stg[i2][:].rearrange("p (h two f) -> p h two f", two=2, f=P)
            ov = stg[i2][:].bitcast(BF16)[:, 0:512].rearrange("p (h two f) -> p h two f", two=2, f=P)
            x1, x2 = xv[:, :, 0, :], xv[:, :, 1, :]
            t1, t2 = tv[:, :, 0, :], tv[:, :, 1, :]

            def f(e):
                e.tensor_tensor(t1, x1, cosb, ALU.mult)
                e.tensor_tensor(t2, x2, sinb, ALU.mult)
                return e.tensor_tensor(t1, t1, t2, ALU.subtract)

            def f2(e):
                e.tensor_tensor(t2, x1, sinb, ALU.mult)
                e.tensor_tensor(x1, x2, cosb, ALU.mult)
                return e.tensor_tensor(t2, t2, x1, ALU.add)

            S.op("dve", lambda e: e.tensor_tensor(t1, x1, cosb, ALU.mult), reads=[stgB[i], const_b], writes=[stgB[i2]])
            S.op("dve", lambda e: e.tensor_tensor(t2, x2, sinb, ALU.mult), reads=[stgB[i], const_b], writes=[stgB[i2]])
            S.op("dve", lambda e: e.tensor_tensor(t1, t1, t2, ALU.subtract), reads=[stgB[i2]], writes=[stgB[i2]])
            S.op("dve", lambda e: e.tensor_tensor(t2, x1, sinb, ALU.mult), reads=[stgB[i], const_b], writes=[stgB[i2]])
            S.op("dve", lambda e: e.tensor_tensor(x1, x2, cosb, ALU.mult), reads=[stgB[i], const_b], writes=[stgB[i]])
            S.op("dve", lambda e: e.tensor_tensor(t2, t2, x1, ALU.add), reads=[stgB[i], stgB[i2]], writes=[stgB[i2]])
            sv = stg[i][:].bitcast(BF16)[:, 0:512]
            S.op("act", lambda e: e.copy(sv, stg[i2][:]), reads=[stgB[i2]], writes=[stgB[i]])
            c = col0 + 512 * b
            S.op("sp", lambda e: e.dma_start(out=dst[t * P:(t + 1) * P, c:c + 512], in_=sv),
                 reads=[stgB[i]], writes=[bf(f"{dst.name}_{t}")], dkey=f"stg{i}")
        return epi

    xTf = lambda kc, t: xT[:, kc, t * P:(t + 1) * P]
    xTb = lambda t: [xTB[t]]
    proj_tok(xTf, xTb, 16, w_b_in, 0, 2048, 4, epi_rope(kd, 0, 8))
    proj_tok(xTf, xTb, 16, w_b_in, 0, 4096, 8, epi_store_bf(vd, 0))
    proj_tok(xTf, xTb, 16, w_b_in, 0, 0, 4, epi_rope(qd, 0, 0))
    proj_tok(xTf, xTb, 16, w_b_in, 0, 8192, 8, epi_store_bf(gd, 0, AF.Silu))
    flush_all()

    SBh = [bf(f"S{hl}") for hl in range(4)]

    def ret_tile(t, hg, ri, mode):
        rows = slice(t * P, (t + 1) * P)
        typ = 1 if t == NTL - 1 else 0
        full = (mode == "full")
        S.op("sp", lambda e: e.dma_start(out=rk[ri][:], in_=kd[rows, hg * 1024:(hg + 1) * 1024]), reads=[bf(f"kd_{t}")], writes=[bf(f"rk{ri}")], dkey=f"rk{ri}")
        S.op("sp", lambda e: e.dma_start(out=rv[ri][:], in_=vd[rows, hg * 2048:(hg + 1) * 2048]), reads=[bf(f"vd_{t}")], writes=[bf(f"rv{ri}")], dkey=f"rv{ri}")
        if full:
            S.op("sp", lambda e: e.dma_start(out=rq[ri][:], in_=qd[rows, hg * 1024:(hg + 1) * 1024]), reads=[bf(f"qd_{t}")], writes=[bf(f"rq{ri}")], dkey=f"rq{ri}")
            S.op("sp", lambda e: e.dma_start(out=rg[ri][:], in_=gd[rows, hg * 2048:(hg + 1) * 2048]), reads=[bf(f"gd_{t}")], writes=[bf(f"rg{ri}")], dkey=f"rg{ri}")
            transposes(lambda j: rq[ri][:, j * P:(j + 1) * P], 8, lambda g0, cnt: qT[:, g0:g0 + cnt, :], [bf(f"rq{ri}")], [bf("qT")], evac="act")
            transposes(lambda j: rk[ri][:, j * P:(j + 1) * P], 8, lambda g0, cnt: kT[:, g0:g0 + cnt, :], [bf(f"rk{ri}")], [bf("kT")], evac="act")
        if typ == 0:
            if full:
                bs, bsb = next_bank()

                def pe_sc(e, bs=bs):
                    ins = None
                    for hl in range(4):
                        e.matmul(bs[:, hl * P:(hl + 1) * P], kT[:, 2 * hl, :], qT[:, 2 * hl, :], start=True, stop=False)
                        ins = e.matmul(bs[:, hl * P:(hl + 1) * P], kT[:, 2 * hl + 1, :], qT[:, 2 * hl + 1, :], start=False, stop=True)
                    return ins

                S.op("pe", pe_sc, reads=[bf("qT"), bf("kT")], writes=[bsb])
                S.op("dve", lambda e, bs=bs: e.tensor_tensor(sc4[:], bs[:].rearrange("p (h t) -> p h t", t=P),
                                                             mask[:, 0:1, :].to_broadcast([P, 4, P]), ALU.mult),
                     reads=[bsb, const_b], writes=[bf("sc4")])
                obanks = [next_bank() for _ in range(4)]
                for hl in range(4):
                    bo = obanks[hl][0]

                    def pe_o(e, hl=hl, bo=bo):
                        e.matmul(bo[:], sc4[:, hl, :], rv[ri][:, hl * 512:(hl + 1) * 512], start=True, stop=False)
                        e.matmul(bo[:], qT[:, 2 * hl, :], Sb[:, 2 * hl, :], start=False, stop=False)
                        return e.matmul(bo[:], qT[:, 2 * hl + 1, :], Sb[:, 2 * hl + 1, :], start=False, stop=True)

                    S.op("pe", pe_o, reads=[bf("sc4"), bf(f"rv{ri}"), bf("qT"), bf("Sb")], writes=[obanks[hl][1]])

                def fstats(e):
                    ins = None
                    for hl in range(4):
                        ins = e.bn_stats(st6[:, hl, :], obanks[hl][0][:])
                    return ins

                def faggr(e):
                    ins = None
                    for hl in range(4):
                        ins = e.bn_aggr(mv4[:, hl, :], st6[:, hl, :])
                    return ins

                S.op("dve", fstats, reads=[ob[1] for ob in obanks], writes=[stat_b])
                S.op("dve", faggr, reads=[stat_b], writes=[stat_b])
                S.op("dve", lambda e: e.tensor_scalar_add(rstd4[:], mv4[:, :, 1], GN_EPS), reads=[stat_b], writes=[stat_b])
                S.op("act", lambda e: e.activation(rstd4[:], rstd4[:], AF.Sqrt), reads=[stat_b], writes=[stat_b])
                S.op("dve", lambda e: e.reciprocal(rstd4[:], rstd4[:]), reads=[stat_b], writes=[stat_b])
                S.op("dve", lambda e: e.scalar_tensor_tensor(nmr4[:], mv4[:, :, 0], -1.0, rstd4[:], ALU.mult, ALU.mult),
                     reads=[stat_b], writes=[stat_b])
                for hl in range(4):
                    S.op("act", lambda e, hl=hl: e.activation(onr4[:, hl, :], obanks[hl][0][:], AF.Identity,
                                                              bias=nmr4[:, hl:hl + 1], scale=rstd4[:, hl:hl + 1]),
                         reads=[obanks[hl][1], stat_b], writes=[ugB[0], ugB[1]])
                S.op("dve", lambda e: e.tensor_tensor(yin[:], onr4[:].rearrange("p a b -> p (a b)"), rg[ri][:], ALU.mult),
                     reads=[ugB[0], ugB[1], bf(f"rg{ri}")], writes=[bf("yin")])
            for hl in range(4):
                for dc in range(2):
                    bd, bdb = next_bank()
                    S.op("pe", lambda e, hl=hl, dc=dc, bd=bd: e.matmul(bd[:], rk[ri][:, hl * 256 + dc * P: hl * 256 + (dc + 1) * P],
                                                                     rv[ri][:, hl * 512:(hl + 1) * 512], start=True, stop=True),
                         reads=[bf(f"rk{ri}"), bf(f"rv{ri}")], writes=[bdb])
                    j = 2 * hl + dc
                    S.op("dve", lambda e, j=j, bd=bd: e.tensor_tensor(Sf[:, j, :], Sf[:, j, :], bd[:], ALU.add),
                         reads=[bdb], writes=[SBh[hl]])
                gL = GAM[hg * 4 + hl] ** 128
                S.op("dve", lambda e, hl=hl, gL=gL: e.tensor_scalar_mul(Sf[:, 2 * hl:2 * hl + 2, :], Sf[:, 2 * hl:2 * hl + 2, :], float(gL)),
                     writes=[SBh[hl]])
            if full:
                S.op("act", lambda e: e.copy(Sb[:].rearrange("p a b -> p (a b)"), Sf[:].rearrange("p a b -> p (a b)")),
                     reads=SBh, writes=[bf("Sb")])
        else:
            assert full
            S.op("dve", lambda e: e.memset(qTz[:].rearrange("p a b c -> p (a b c)"), 0.0), writes=[bf("qTz")])
            for i in range(4):
                S.op("dve", lambda e, i=i: e.tensor_copy(qTz[:, i, :, 32 * i:32 * i + 32], qT[:, :, 32 * i:32 * i + 32]),
                     reads=[bf("qT")], writes=[bf("qTz")])
            for hl in range(4):
                bs, bsb = next_bank()

                def pe_sc(e, hl=hl, bs=bs):
                    e.matmul(bs[:, 0:P], kT[:, 2 * hl, :], qT[:, 2 * hl, :], start=True, stop=False)
                    return e.matmul(bs[:, 0:P], kT[:, 2 * hl + 1, :], qT[:, 2 * hl + 1, :], start=False, stop=True)

                S.op("pe", pe_sc, reads=[bf("qT"), bf("kT")], writes=[bsb])
                S.op("dve", lambda e, bs=bs: e.tensor_tensor(sc[:], bs[:, 0:P], mask[:, 1, :], ALU.mult), reads=[bsb, const_b], writes=[bf("sc")])
                bo, bob = next_bank()
                S.op("pe", lambda e, hl=hl, bo=bo: e.matmul(bo[:], sc[:], rv[ri][:, hl * 512:(hl + 1) * 512], start=True, stop=True),
                     reads=[bf("sc"), bf(f"rv{ri}")], writes=[bob])
                S.op("act", lambda e, hl=hl, bo=bo: e.copy(osb[:, hl, :], bo[:]), reads=[bob], writes=[bf("osb")])
            for i in range(4):
                r0 = (i * 8 + hg * 4) * 256
                S.op("sp", lambda e, r0=r0: e.dma_start(out=Sf[:], in_=s0in[r0:r0 + 1024, :].rearrange("(j p) v -> p j v", p=P)),
                     writes=SBh, dkey="Sf")
                S.op("act", lambda e: e.copy(Sb[:], Sf[:]), reads=SBh, writes=[bf("Sb")])
                S.op("dve", lambda e, i=i: e.tensor_scalar_mul(ktz[:], rk[ri][:], rowm[:, i:i + 1]), reads=[bf(f"rk{ri}"), const_b], writes=[bf("ktz")])
                for hl in range(4):
                    h = hg * 4 + hl
                    gL = GAM[h] ** 32
                    bo, bob = next_bank()

                    def pe_c(e, hl=hl, bo=bo, i=i):
                        e.matmul(bo[:], qTz[:, i, 2 * hl, :], Sb[:, 2 * hl, :], start=True, stop=False)
                        return e.matmul(bo[:], qTz[:, i, 2 * hl + 1, :], Sb[:, 2 * hl + 1, :], start=False, stop=True)

                    S.op("pe", pe_c, reads=[bf("qTz"), bf("Sb")], writes=[bob])
                    S.op("dve", lambda e, hl=hl, bo=bo: e.tensor_tensor(osb[:, hl, :], osb[:, hl, :], bo[:], ALU.add), reads=[bob], writes=[bf("osb")])
                    for dc in range(2):
                        bd, bdb = next_bank()
                        S.op("pe", lambda e, hl=hl, dc=dc, bd=bd: e.matmul(bd[:], ktz[:, hl * 256 + dc * P: hl * 256 + (dc + 1) * P],
                                                                         rv[ri][:, hl * 512:(hl + 1) * 512], start=True, stop=True),
                             reads=[bf("ktz"), bf(f"rv{ri}")], writes=[bdb])
                        j = 2 * hl + dc
                        S.op("dve", lambda e, j=j, bd=bd: e.tensor_tensor(Sf[:, j, :], Sf[:, j, :], bd[:], ALU.add), reads=[bdb], writes=SBh)
                        S.op("dve", lambda e, j=j, gL=gL: e.tensor_scalar_mul(Sf[:, j, :], Sf[:, j, :], float(gL)), reads=SBh, writes=SBh)
                S.op("sp", lambda e, r0=r0: e.dma_start(out=sout_s[r0:r0 + 1024, :].rearrange("(j p) v -> p j v", p=P), in_=Sf[:]),
                     reads=SBh, writes=[bf("sout_s")], dkey=f"Sfo{i}")
            for hl in range(4):
                gn_head(None, None, hl, ri)
        if full:
            def go():
                transposes(lambda j: yin[:, j * P:(j + 1) * P], 16,
                           lambda g0, cnt: A[:, g0:g0 + cnt, t * P:(t + 1) * P], [bf("yin")], [bf(f"sT{t}")])
            go()

    def gn_head(bo, bob, hl, ri):
        if bo is not None:
            src, sb_ = bo, [bob]
        else:
            src, sb_ = osb[:, hl, :], [bf("osb")]
        stats(src, 1, GN_EPS, sb_)
        S.op("act", lambda e: e.activation(onr[:], src[:] if bo is not None else src, AF.Identity, bias=nmr[:], scale=rstd[:]),
             reads=sb_ + [stat_b], writes=[bf("onr")])
        S.op("dve", lambda e: e.tensor_tensor(yin[:, hl * 512:(hl + 1) * 512], onr[:], rg[ri][:, hl * 512:(hl + 1) * 512], ALU.mult),
             reads=[bf("onr"), bf(f"rg{ri}")], writes=[bf("yin")])

    region_c_barrier(LN_NAMES, RET_NAMES)
    S.op("dve", lambda e: e.memset(sc[:, 0:2], 0.0), writes=xTB + SBh + [bf("Sb"), bf("osb"), bf("sc")])

    if exchange:
        for hg in range(2):
            S.op("dve", lambda e: e.memset(Sf[:].rearrange("p a b -> p (a b)"), 0.0), writes=SBh)
            for t in range(NPT):
                ret_tile(t, hg, t % 2, "state")
            for hl in range(4):
                h = hg * 4 + hl
                S.op("sp", lambda e, h=h, hl=hl: e.dma_start(out=sxh[h].rearrange("(j p) v -> p j v", p=P), in_=Sf[:, 2 * hl:2 * hl + 2, :]),
                     reads=SBh, writes=[bf(f"sx{h}")], dkey=f"sx{h}")
                S.op("pool", lambda e, h=h: e.collective_compute("AllGather", ALU.bypass, replica_groups=[[0, 1], [2, 3], [4, 5], [6, 7]],
                                                                 ins=[sxh[h]], outs=[gxh[h]]), reads=[bf(f"sx{h}")], writes=[bf(f"gx{h}")])

    for hg in range(2):
        if exchange:
            for hl in range(4):
                h = hg * 4 + hl
                S.op("sp", lambda e, h=h, hl=hl: e.dma_start(out=Sf[:, 2 * hl:2 * hl + 2, :], in_=gxh[h][0:256, :].rearrange("(j p) v -> p j v", p=P)),
                     reads=[bf(f"gx{h}")], writes=SBh, dkey="Sf")
            S.op("dve", lambda e: e.tensor_scalar_mul(Sf[:].rearrange("p a b -> p (a b)"), Sf[:].rearrange("p a b -> p (a b)"), flag[:, 0:1]),
                 reads=[const_b], writes=SBh)
        else:
            S.op("dve", lambda e: e.memset(Sf[:].rearrange("p a b -> p (a b)"), 0.0), writes=SBh)
        S.op("act", lambda e: e.copy(Sb[:].rearrange("p a b -> p (a b)"), Sf[:].rearrange("p a b -> p (a b)")), reads=SBh, writes=[bf("Sb")])
        for t in range(NTL):
            if t == NPT:
                S.op("sp", lambda e, hg=hg: e.dma_start(out=sout_p[hg * 1024:(hg + 1) * 1024, :].rearrange("(j p) v -> p j v", p=P), in_=Sf[:]),
                     reads=SBh, writes=[bf("sout_p")], dkey=f"sop{hg}")
            ret_tile(t, hg, t % 2, "full")
        last = (hg == 1)
        if last:
            region_c_barrier(RET_NAMES, LN_NAMES)
            S.op("dve", lambda e: e.memset(sc[:, 0:2], 0.0), writes=SBh + [bf("Sb"), bf("osb")] + xTB + [bf("sc")])
            load_bc(6, 7)
        proj_tok(lambda kc, t: A[:, kc, t * P:(t + 1) * P], lambda t: [bf(f"sT{t}")], 16, w_b_out, hg * 2048, 0, 4,
                 epi_store(Rs[hg], 0, eng="dve"),
                 last_cb=(lambda t: ln_tile(t, xs[1], [Rs[0], Rs[1]], xs[2])) if last else None)
    flush_all()
    ffn(1, xs[2], 8, yout, final=True)

    outs = [bf(f"yout_{t}") for t in range(NTL)] + [bf("sout_s"), bf("sout_p"), bf("vout")]
    S.op("sp", lambda e: e.nop(), reads=outs)
    with nc.Block() as block:
        S.emit(block)
    return nc


def _consts(half):
    c = {}
    c["c_ident"] = np.eye(P, dtype=np.float32)
    half_d = 128
    inv = np.power(np.float32(10000.0), -np.arange(half_d, dtype=np.float32) / np.float32(half_d)).astype(np.float32)
    cs = np.zeros((P, NTL, 256), np.float32)
    for t in range(NTL):
        if t < NPT:
            pos = (half * 1024 + t * P + np.arange(P)).astype(np.float32)
        else:
            pos = (4096 + (np.arange(P) % 32)).astype(np.float32)
        ang = (pos[:, None] * inv[None, :]).astype(np.float32)
        cs[:, t, 0:128] = np.cos(ang)
        cs[:, t, 128:256] = np.sin(ang)
    c["c_cs"] = cs.reshape(P, NTL * 256)
    dqk = np.zeros((P, 32), np.float64)
    for typ in range(2):
        tl = np.arange(P) if typ == 0 else (np.arange(P) % 32)
        for h in range(8):
            lg = np.log1p(-2.0 ** (-5.0 - h))
            dqk[:, typ * 16 + h] = np.exp(lg * (tl + 1.0)) * (256.0 ** -0.5)
            dqk[:, typ * 16 + 8 + h] = np.exp(-lg * (tl + 1.0))
    c["c_dqk"] = dqk.astype(np.float32)
    s = np.arange(P)[:, None]
    t = np.arange(P)[None, :]
    m0 = (s <= t).astype(np.float32)
    m1 = ((s <= t) & ((s // 32) == (t // 32))).astype(np.float32)
    c["c_mask"] = np.concatenate([m0, m1], axis=1)
    rowm = np.zeros((P, 4), np.float32)
    for i in range(4):
        rowm[32 * i:32 * i + 32, i] = 1.0
    c["c_rowm"] = rowm
    c["c_flag"] = np.full((P, 1), float(half), np.float32)
    return c


_NC_CACHE = {}


def kernel(x_prompt, x_sample, state_ret, w_a_in, a_ln_g, a_ln_b, a_ws, a_bs, w_a_out,
           w_b_in, w_b_out, w_ffn_in, w_ffn_out, ln_mix_g, ln_mix_b, ln_ffn_g, ln_ffn_b):
    f = lambda a: np.ascontiguousarray(np.asarray(a, dtype=np.float32))
    x_prompt, x_sample, state_ret = f(x_prompt), f(x_sample), f(state_ret)
    lnv = np.stack([f(a_ln_g)[0], f(a_ln_b)[0], f(ln_mix_g)[0], f(ln_mix_b)[0], f(ln_ffn_g)[0], f(ln_ffn_b)[0],
                    f(ln_mix_g)[1], f(ln_mix_b)[1], f(ln_ffn_g)[1], f(ln_ffn_b)[1]], axis=0)
    shared = {
        "w_a_in": f(w_a_in)[0], "w_a_out": f(w_a_out)[0], "w_b_in": f(w_b_in)[0], "w_b_out": f(w_b_out)[0],
        "w_ffn_in0": f(w_ffn_in)[0], "w_ffn_in1": f(w_ffn_in)[1], "w_ffn_out0": f(w_ffn_out)[0], "w_ffn_out1": f(w_ffn_out)[1],
        "a_ws": f(a_ws)[0].reshape(8 * 128, 128), "a_bs": f(a_bs)[0], "lnv": lnv,
    }
    in_maps = []
    for c in range(8):
        b, half = c // 2, c % 2
        m = dict(shared)
        m["xin"] = np.concatenate([x_prompt[b, half * 1024:(half + 1) * 1024], x_sample[4 * c:4 * c + 4].reshape(128, D)], axis=0)
        m["s0in"] = state_ret[0, 4 * c:4 * c + 4].reshape(4 * 8 * 256, 512)
        m.update(_consts(half))
        in_maps.append(m)
    if "nc" not in _NC_CACHE:
        _NC_CACHE["nc"] = build_nc()
    res = run_bass_kernel_spmd(_NC_CACHE["nc"], in_maps, core_ids=list(range(8)))
    r = res.results
    y_prompt = np.zeros((4, 2048, D), np.float32)
    y_sample = np.zeros((32, 32, D), np.float32)
    rsp = np.zeros((1, 4, 8, 256, 512), np.float32)
    rss = np.zeros((1, 32, 8, 256, 512), np.float32)
    gv = np.zeros((1, 32, 32, D), np.float32)
    for c in range(8):
        b, half = c // 2, c % 2
        y = np.asarray(r[c]["yout"])
        y_prompt[b, half * 1024:(half + 1) * 1024] = y[:1024]
        y_sample[4 * c:4 * c + 4] = y[1024:].reshape(4, 32, D)
        if half == 1:
            rsp[0, b] = np.asarray(r[c]["sout_p"]).reshape(8, 256, 512)
        rss[0, 4 * c:4 * c + 4] = np.asarray(r[c]["sout_s"]).reshape(4, 8, 256, 512)
        gv[0, 4 * c:4 * c + 4] = np.asarray(r[c]["vout"]).reshape(4, 32, D)
    return (y_prompt, y_sample, rsp, rss, gv)
```

```python
import numpy as np
import concourse.bass as bass
import concourse.mybir as mybir
from concourse.bass_utils import run_bass_kernel_spmd

F32 = mybir.dt.float32
BF16 = mybir.dt.bfloat16
AF = mybir.ActivationFunctionType
ALU = mybir.AluOpType
P = 128
D = 2048
NTL = 9
NPT = 8
NT = NTL * P
FH = 5632
ALPHA = 4.0 ** 0.25
LN_EPS = 1e-5
GN_EPS = 1e-6
NSLOT = 3
GAM = [1.0 - 2.0 ** (-5.0 - h) for h in range(8)]


class Buf:
    __slots__ = ("name", "w", "r")

    def __init__(self, name):
        self.name = name
        self.w = {}
        self.r = {}


class Sched:
    ENG = ("pe", "act", "dve", "pool", "sp")

    def __init__(self, nc):
        self.nc = nc
        self.ops = []
        self.semh = {}
        self.semc = {}
        self.cur = {e: 0 for e in self.ENG}
        self.small = set()

    def _sem(self, key):
        if key not in self.semh:
            self.semh[key] = self.nc.alloc_semaphore(key)
            self.semc[key] = 0
        return key

    def op(self, eng, fn, reads=(), writes=(), dkey=None):
        waits = {}

        def add(evs):
            for s, v in evs.items():
                if waits.get(s, 0) < v:
                    waits[s] = v

        for b in reads:
            add(b.w)
        for b in writes:
            add(b.r)
            add(b.w)
        if dkey is not None:
            key = self._sem("d_" + dkey)
            inc = 16
        else:
            key = self._sem(f"{eng}{self.cur[eng]}")
            if self.semc[key] >= 2000:
                self.cur[eng] += 1
                key = self._sem(f"{eng}{self.cur[eng]}")
            inc = 1
        self.semc[key] += inc
        v = self.semc[key]
        if eng == "pe":
            waits = {s: x for s, x in waits.items() if not s.startswith("pe")}
        elif eng in ("act", "dve") and not any(b in self.small for b in list(reads) + list(writes)):
            waits = {s: x for s, x in waits.items() if not s.startswith(eng)}
        self.ops.append((eng, fn, waits, key, inc))
        for b in reads:
            if b not in writes:
                if b.r.get(key, 0) < v:
                    b.r[key] = v
        for b in writes:
            if b.r:
                b.w = {key: v}
                b.r = {}
            else:
                b.w[key] = v
        return (key, v)

    def emit(self, block):
        engs = {"pe": block.tensor, "act": block.scalar, "dve": block.vector, "pool": block.gpsimd, "sp": block.sync}
        for ename, deco in engs.items():
            myops = [o for o in self.ops if o[0] == ename]

            def body(eng, myops=myops):
                seen = {}
                for (_, fn, waits, key, inc) in myops:
                    for s, v in waits.items():
                        if seen.get(s, 0) >= v:
                            continue
                        seen[s] = v
                        eng.wait_ge(self.semh[s], v)
                    ins = fn(eng)
                    ins.then_inc(self.semh[key], inc)

            deco(body)


def build_nc(exchange=True, debug=False):
    nc = bass.Bass("TRN2", target_bir_lowering=False)
    S = Sched(nc)

    def din(name, shape, dt=F32):
        return nc.dram_tensor(name, list(shape), dt, kind="ExternalInput").ap()

    def dout(name, shape, dt=F32):
        return nc.dram_tensor(name, list(shape), dt, kind="ExternalOutput").ap()

    def dscr(name, shape, dt=F32):
        if debug and name in ("xs0", "xs1", "xs2", "qd", "kd", "vd", "gd", "zscr"):
            return nc.dram_tensor(name, list(shape), dt, kind="ExternalOutput").ap()
        return nc.dram_tensor(name, list(shape), dt).ap()

    xin = din("xin", [NT, D])
    s0in = din("s0in", [4 * 8 * 256, 512])
    w_a_in = din("w_a_in", [D, 4096])
    w_a_out = din("w_a_out", [D, D])
    w_b_in = din("w_b_in", [D, 12288])
    w_b_out = din("w_b_out", [4096, D])
    w_ffn_in = [din(f"w_ffn_in{l}", [D, 2 * FH]) for l in range(2)]
    w_ffn_out = [din(f"w_ffn_out{l}", [FH, D]) for l in range(2)]
    a_ws = din("a_ws", [8 * 128, 128])
    a_bs = din("a_bs", [8, 128])
    lnv = din("lnv", [10, D])
    c_ident = din("c_ident", [P, P])
    c_cs = din("c_cs", [P, NTL * 256])
    c_dqk = din("c_dqk", [P, 32])
    c_mask = din("c_mask", [P, 256])
    c_rowm = din("c_rowm", [P, 4])
    c_flag = din("c_flag", [P, 1])

    yout = dout("yout", [NT, D])
    sout_s = dout("sout_s", [4 * 8 * 256, 512])
    sout_p = dout("sout_p", [8 * 256, 512])
    vout = dout("vout", [P, D])

    zscr = dscr("zscr", [NT, 4096])
    xs = [dscr(f"xs{i}", [NT, D]) for i in range(3)]
    Rs = [dscr(f"R{i}", [NT, D]) for i in range(3)]
    qd = dscr("qd", [NT, D], BF16)
    kd = dscr("kd", [NT, D], BF16)
    vd = dscr("vd", [NT, 4096], BF16)
    gd = dscr("gd", [NT, 4096], BF16)
    sxh = [dscr(f"sx{h}", [256, 512]) for h in range(8)]
    gxh = [dscr(f"gx{h}", [512, 512]) for h in range(8)]

    off = [16512]

    def sb(name, shape, dt, at=None):
        nbytes = int(np.prod(shape[1:])) * (2 if dt == BF16 else 4)
        if at is None:
            o = off[0]
            off[0] += (nbytes + 31) // 32 * 32
        else:
            o = at
        return nc.alloc_sbuf_tensor_at(name, list(shape), dt, offset=o)

    ident = sb("ident", [P, P], BF16)
    identf = sb("identf", [P, P], F32)
    mask = sb("mask", [P, 2, P], F32)
    cs = sb("cs", [P, NTL, 256], F32)
    dqk = sb("dqk", [P, 32], F32)
    rowm = sb("rowm", [P, 4], F32)
    flag = sb("flag", [P, 1], F32)
    WgT = sb("WgT", [P, 2, 8, P], BF16)
    bcol = sb("bcol", [P, 2, 8], F32)
    st6 = sb("st6", [P, 4, 6], F32)
    mv = sb("mv", [P, 2], F32)
    rstd = sb("rstd", [P, 1], F32)
    nmr = sb("nmr", [P, 1], F32)
    sc = sb("sc", [P, P], BF16)
    xT = sb("xT", [P, 16, NT], BF16)
    XT_OFF = off[0] - 16 * NT * 2
    slots = [sb(f"slot{i}", [P, 16, 512], BF16) for i in range(NSLOT)]
    A = sb("A", [P, 16, NT], BF16)
    stg = [sb(f"stg{i}", [P, 512], F32) for i in range(4)]
    UG_OFF = off[0]
    ug = [sb(f"ug{i}", [P, 3, 384], F32) for i in range(2)]
    onr4 = sb("onr4", [P, 4, 512], F32, at=UG_OFF)
    sc4 = sb("sc4", [P, 4, P], BF16)
    mv4 = sb("mv4", [P, 4, 2], F32)
    rstd4 = sb("rstd4", [P, 4], F32)
    nmr4 = sb("nmr4", [P, 4], F32)
    xb2 = sb("xb2", [P, D], BF16)
    C_OFF = off[0]
    C_SIZE = 49152
    off[0] += C_SIZE
    assert off[0] <= 229344, off[0]
    xa = [sb(f"xa{i}", [P, D], F32, at=C_OFF + i * 8192) for i in range(2)]
    rb = sb("rb", [P, D], F32, at=C_OFF + 16384)
    xb = [sb("xb0", [P, D], BF16, at=C_OFF + 24576), sb("xb1", [P, D], BF16, at=C_OFF + 45056), xb2]
    gbc = sb("gbc", [P, D], F32, at=C_OFF + 28672)
    bbc = sb("bbc", [P, D], F32, at=C_OFF + 36864)
    gu = xa[0]
    gv = xa[1]
    gvb = sb("gvb", [P, D], BF16, at=C_OFF + 16384)
    gs = sb("gs", [P, D], BF16, at=C_OFF + 24576)
    rq = [sb(f"rq{i}", [P, 1024], BF16, at=C_OFF + i * 12288) for i in range(2)]
    rk = [sb(f"rk{i}", [P, 1024], BF16, at=C_OFF + i * 12288 + 2048) for i in range(2)]
    rv = [sb(f"rv{i}", [P, 2048], BF16, at=C_OFF + i * 12288 + 4096) for i in range(2)]
    rg = [sb(f"rg{i}", [P, 2048], BF16, at=C_OFF + i * 12288 + 8192) for i in range(2)]
    qT = sb("qT", [P, 8, P], BF16, at=C_OFF + 24576)
    kT = sb("kT", [P, 8, P], BF16, at=C_OFF + 26624)
    qTz = sb("qTz", [P, 4, 8, P], BF16, at=C_OFF + 28672)
    ktz = sb("ktz", [P, 1024], BF16, at=C_OFF + 36864)
    yin2 = [sb("yin0", [P, 2048], BF16, at=C_OFF + 38912), sb("yin1", [P, 2048], BF16, at=C_OFF + 45056)]
    onr = sb("onr", [P, 512], F32, at=C_OFF + 43008)
    Sf = sb("Sf", [P, 8, 512], F32, at=XT_OFF)
    Sb = sb("Sb", [P, 8, 512], BF16, at=XT_OFF + 16384)
    osb = sb("osb", [P, 4, 512], F32, at=XT_OFF + 24576)

    pbank = [nc.alloc_psum_tensor(f"pb{i}", [P, 512], F32) for i in range(8)]
    bankB = [Buf(f"bank{i}") for i in range(8)]
    bank_i = [0]

    def next_bank():
        i = bank_i[0] % 8
        bank_i[0] += 1
        return pbank[i], bankB[i]

    slotB = [Buf(f"slot{i}") for i in range(NSLOT)]
    slot_i = [0]

    def next_slot():
        i = slot_i[0] % NSLOT
        slot_i[0] += 1
        return i

    xTB = [Buf(f"xT{t}") for t in range(NTL)]
    AB = [Buf(f"A{j}") for j in range(16)]
    stgB = [Buf(f"stg{i}") for i in range(4)]
    stg_i = [0]
    ugB = [Buf(f"ug{i}") for i in range(2)]
    ug_i = [0]
    B = {}

    def bf(name):
        if name not in B:
            B[name] = Buf(name)
        return B[name]

    cbufs = [bf(n) for n in ("xa0", "xa1", "rb", "xb0", "xb1", "gbc", "bbc")]
    const_b = bf("const")
    stat_b = bf("stat")
    S.small.add(stat_b)

    deferred = []

    DEPTH = 2

    def flush(n=1):
        while len(deferred) > DEPTH:
            deferred.pop(0)()

    def flush_all():
        while deferred:
            deferred.pop(0)()

    def wblock_src(w, r0, kc, c0):
        return w[r0:r0 + kc * P, c0:c0 + 512].rearrange("(kc p) n -> p kc n", p=P)

    def load_slot(w, r0, kc, c0):
        si = next_slot()
        src = wblock_src(w, r0, kc, c0)
        S.op("pool", lambda e: e.dma_start(out=slots[si][:, 0:kc, :], in_=src), writes=[slotB[si]], dkey=f"slot{si}")
        return si

    def ld(dst_ap, src_ap, b, key, eng="sp", **kw):
        S.op(eng, lambda e: e.dma_start(out=dst_ap, in_=src_ap, **kw), writes=[b], dkey=key)

    ld(identf[:], c_ident, const_b, "c0")
    ld(mask[:].rearrange("p a b -> p (a b)"), c_mask, const_b, "c1")
    ld(cs[:].rearrange("p a b -> p (a b)"), c_cs, const_b, "c2")
    ld(dqk[:], c_dqk, const_b, "c3")
    ld(rowm[:], c_rowm, const_b, "c4")
    ld(flag[:], c_flag, const_b, "c5")
    S.op("act", lambda e: e.copy(ident[:], identf[:]), reads=[const_b], writes=[bf("ident")])
    identB = bf("ident")

    def transposes(src_fn, n, dst_fn, src_bufs, dst_bufs, evac="dve", post=None):
        for g0 in range(0, n, 8):
            cnt = min(8, n - g0)
            bk, bb = next_bank()
            bkv = bk[:].bitcast(BF16)

            def pe(e, g0=g0, cnt=cnt, bkv=bkv):
                ins = None
                for j in range(cnt):
                    ins = e.transpose(bkv[:, j * P:(j + 1) * P], src_fn(g0 + j), ident[:])
                return ins

            S.op("pe", pe, reads=list(src_bufs) + [identB], writes=[bb])
            dst = dst_fn(g0, cnt) if dst_fn is not None else None
            src = bkv[:, 0:cnt * P].rearrange("p (a b) -> p a b", b=P)
            if post is not None:
                post(g0, cnt, src, bb)
            elif evac == "dve":
                S.op("dve", lambda e, dst=dst, src=src: e.tensor_copy(dst, src), reads=[bb], writes=list(dst_bufs))
            else:
                S.op("act", lambda e, dst=dst, src=src: e.copy(dst, src), reads=[bb], writes=list(dst_bufs))

    wtmp = gu
    wtb = gvb
    gB = [bf("xa0"), bf("rb")]
    for var in range(2):
        if var == 0:
            S.op("sp", lambda e: e.dma_start(out=wtmp[:, 0:1024].rearrange("p (g s) -> p g s", s=P),
                                             in_=a_ws.rearrange("(g t) s -> t g s", t=P)), writes=[gB[0]], dkey="xa0")
        else:
            S.op("dve", lambda e: e.memset(wtmp[:, 0:1024], 0.0), writes=[gB[0]])
            for i in range(4):
                S.op("sp", lambda e, i=i: e.dma_start(
                    out=wtmp[32 * i:32 * i + 32, 0:1024].rearrange("p (g s) -> p g s", s=P)[:, :, 32 * i:32 * i + 32],
                    in_=a_ws.rearrange("(g t) s -> t g s", t=P)[0:32, :, 0:32]), writes=[gB[0]], dkey=f"wd{i}")
        S.op("act", lambda e: e.copy(wtb[:, 0:1024], wtmp[:, 0:1024]), reads=[gB[0]], writes=[gB[1]])

        def post(g0, cnt, src, bb, var=var):
            S.op("dve", lambda e: e.tensor_tensor(WgT[:, var, g0:g0 + cnt, :], src,
                                                  mask[:, var:var + 1, :].to_broadcast([P, cnt, P]), ALU.mult),
                 reads=[bb, const_b], writes=[bf("WgT")])

        transposes(lambda j: wtb[:, j * P:(j + 1) * P], 8, None, [gB[1]], [], post=post)
    ld(bcol[:, 0, :], a_bs.rearrange("g p -> p g"), bf("bcol"), "c6", allow_slow_non_contiguous=True)
    for i in range(4):
        ld(bcol[32 * i:32 * i + 32, 1, :], a_bs.rearrange("g p -> p g")[0:32, :], bf("bcol"), f"c7{i}", allow_slow_non_contiguous=True)

    def matmul_group(bk, stat_fn, kc_n, si):
        def pe(e):
            ins = None
            for kc in range(kc_n):
                ins = e.matmul(bk[:], stat_fn(kc), slots[si][:, kc, :], start=(kc == 0), stop=(kc == kc_n - 1))
            return ins
        return pe

    def proj_tok(stat_fn, stat_bufs_fn, kc_n, w, r0, col0, nblocks, epi, last_cb=None):
        for b in range(nblocks):
            si = load_slot(w, r0, kc_n, col0 + 512 * b)
            for t in range(NTL):
                bk, bb = next_bank()
                S.op("pe", matmul_group(bk, lambda kc, t=t: stat_fn(kc, t), kc_n, si),
                     reads=[slotB[si]] + stat_bufs_fn(t), writes=[bb])
                epi(b, t, bk, bb)
                if last_cb is not None and b == nblocks - 1:
                    last_cb(t)
                flush(1)

    def next_stg():
        i = stg_i[0] % 4
        stg_i[0] += 1
        return i

    def epi_store(dst, col0, func=None, eng="act"):
        def epi(b, t, bk, bb):
            i = next_stg()
            if eng == "act":
                S.op("act", lambda e: e.activation(stg[i][:], bk[:], func if func is not None else AF.Copy),
                     reads=[bb], writes=[stgB[i]])
            else:
                S.op("dve", lambda e: e.tensor_copy(stg[i][:], bk[:]), reads=[bb], writes=[stgB[i]])
            c = col0 + 512 * b
            S.op("sp", lambda e: e.dma_start(out=dst[t * P:(t + 1) * P, c:c + 512], in_=stg[i][:]),
                 reads=[stgB[i]], writes=[bf(f"{dst.name}_{t}")], dkey=f"stg{i}")
        return epi

    def epi_store_bf(dst, col0, func=None):
        def epi(b, t, bk, bb):
            i = next_stg()
            sv = stg[i][:].bitcast(BF16)[:, 0:512]
            S.op("act", lambda e: e.activation(sv, bk[:], func if func is not None else AF.Copy),
                 reads=[bb], writes=[stgB[i]])
            c = col0 + 512 * b
            S.op("sp", lambda e: e.dma_start(out=dst[t * P:(t + 1) * P, c:c + 512], in_=sv),
                 reads=[stgB[i]], writes=[bf(f"{dst.name}_{t}")], dkey=f"stg{i}")
        return epi

    def load_bc(row_g, row_b):
        S.op("sp", lambda e: e.dma_start(out=gbc[:], in_=lnv[row_g:row_g + 1, :].partition_broadcast(P)),
             writes=[bf("gbc")], dkey="gbc")
        S.op("sp", lambda e: e.dma_start(out=bbc[:], in_=lnv[row_b:row_b + 1, :].partition_broadcast(P)),
             writes=[bf("bbc")], dkey="bbc")

    def stats(src_ap, nchunk, eps, src_bufs):
        def f(e):
            ins = None
            for c in range(nchunk):
                ins = e.bn_stats(st6[:, c, :], src_ap[:, c * 512:(c + 1) * 512])
            return ins
        S.op("dve", f, reads=src_bufs, writes=[stat_b])
        S.op("dve", lambda e: e.bn_aggr(mv[:], st6[:, 0:nchunk, :].rearrange("p a b -> p (a b)")), reads=[stat_b], writes=[stat_b])
        S.op("dve", lambda e: e.tensor_scalar_add(rstd[:], mv[:, 1:2], eps), reads=[stat_b], writes=[stat_b])
        S.op("act", lambda e: e.activation(rstd[:], rstd[:], AF.Sqrt), reads=[stat_b], writes=[stat_b])
        S.op("dve", lambda e: e.reciprocal(rstd[:], rstd[:]), reads=[stat_b], writes=[stat_b])
        S.op("dve", lambda e: e.scalar_tensor_tensor(nmr[:], mv[:, 0:1], -1.0, rstd[:], ALU.mult, ALU.mult),
             reads=[stat_b], writes=[stat_b])

    def make_xT(t, src_bf_ap, src_buf):
        def go():
            transposes(lambda j: src_bf_ap[:, j * P:(j + 1) * P], 16,
                       lambda g0, cnt: xT[:, g0:g0 + cnt, t * P:(t + 1) * P], [src_buf], [xTB[t]])
        deferred.append(go)

    ln_i = [0]
    xb_i = [0]

    def ln_tile(t, x_src, R_list, x_dst, want_xT=True):
        i = ln_i[0] % 2
        ln_i[0] += 1
        xab = bf(f"xa{i}")
        rows = slice(t * P, (t + 1) * P)
        S.op("sp", lambda e: e.dma_start(out=xa[i][:], in_=x_src[rows, :]), reads=[bf(f"{x_src.name}_{t}")], writes=[xab], dkey=f"xa{i}")
        for p, R in enumerate(R_list):
            S.op("sp", lambda e, R=R: e.dma_start(out=rb[:], in_=R[rows, :]), reads=[bf(f"{R.name}_{t}")],
                 writes=[bf("rb")], dkey="rb")
            if p == 0:
                S.op("dve", lambda e: e.scalar_tensor_tensor(xa[i][:], xa[i][:], ALPHA, rb[:], ALU.mult, ALU.add),
                     reads=[bf("rb")], writes=[xab])
            else:
                S.op("dve", lambda e: e.tensor_tensor(xa[i][:], xa[i][:], rb[:], ALU.add), reads=[bf("rb")], writes=[xab])
        stats(xa[i], 4, LN_EPS, [xab])
        S.op("act", lambda e: e.activation(xa[i][:], xa[i][:], AF.Identity, bias=nmr[:], scale=rstd[:]),
             reads=[stat_b], writes=[xab])
        S.op("dve", lambda e: e.tensor_tensor(xa[i][:], xa[i][:], gbc[:], ALU.mult), reads=[bf("gbc")], writes=[xab])
        S.op("dve", lambda e: e.tensor_tensor(xa[i][:], xa[i][:], bbc[:], ALU.add), reads=[bf("bbc")], writes=[xab])
        S.op("sp", lambda e: e.dma_start(out=x_dst[rows, :], in_=xa[i][:]), reads=[xab],
             writes=[bf(f"{x_dst.name}_{t}")], dkey=f"xas{i}")
        if want_xT:
            xi = xb_i[0] % 3
            xb_i[0] += 1
            S.op("act", lambda e: e.copy(xb[xi][:], xa[i][:]), reads=[xab], writes=[bf(f"xb{xi}")])
            make_xT(t, xb[xi], bf(f"xb{xi}"))

    def region_c_barrier(from_names, to_names):
        S.op("dve", lambda e: e.memset(sc[:, 0:2], 0.0),
             writes=[bf(n) for n in from_names] + [bf(n) for n in to_names] + [bf("sc")])

    LN_NAMES = ["xa0", "xa1", "rb", "xb0", "xb1", "gbc", "bbc"]
    RET_NAMES = ["rq0", "rk0", "rv0", "rg0", "rq1", "rk1", "rv1", "rg1", "qT", "kT", "qTz", "ktz", "yin0", "yin1", "onr"]

    def ffn(l, x_src, g_row, x_dst, final):
        parts = [(0, 3), (3, 7), (7, 11)]
        for pi, (b0, b1) in enumerate(parts):
            for b in range(b0, b1):
                sg_ = load_slot(w_ffn_in[l], 0, 16, 512 * b)
                su_ = load_slot(w_ffn_in[l], 0, 16, FH + 512 * b)
                for cc in range(4):
                    jl = (b - b0) * 4 + cc
                    banks = [next_bank() for _ in range(6)]

                    def pe(e, si, bks, cc=cc):
                        ins = None
                        for kc in range(16):
                            for tg in range(3):
                                ins = e.matmul(bks[tg][0][:, 0:384], slots[si][:, kc, cc * P:(cc + 1) * P],
                                               xT[:, kc, tg * 384:(tg + 1) * 384], start=(kc == 0), stop=(kc == 15))
                        return ins

                    S.op("pe", lambda e, pe=pe, si=sg_, bks=banks[0:3]: pe(e, si, bks), reads=[slotB[sg_]] + xTB,
                         writes=[b_[1] for b_ in banks[0:3]])
                    S.op("pe", lambda e, pe=pe, si=su_, bks=banks[3:6]: pe(e, si, bks), reads=[slotB[su_]] + xTB,
                         writes=[b_[1] for b_ in banks[3:6]])
                    ui = ug_i[0] % 2
                    ug_i[0] += 1
                    for tg in range(3):
                        S.op("act", lambda e, tg=tg, ui=ui, bks=banks: e.activation(ug[ui][:, tg, :], bks[tg][0][:, 0:384], AF.Silu),
                             reads=[banks[tg][1]], writes=[ugB[ui]])
                    for tg in range(3):
                        S.op("dve", lambda e, tg=tg, ui=ui, bks=banks, jl=jl: e.tensor_tensor(
                            A[:, jl, tg * 384:(tg + 1) * 384], ug[ui][:, tg, :], bks[3 + tg][0][:, 0:384], ALU.mult),
                            reads=[ugB[ui], banks[3 + tg][1]], writes=[AB[jl]])
                    flush(1)
            kc_n = 4 * (b1 - b0)
            last = (pi == len(parts) - 1)
            if last:
                load_bc(g_row, g_row + 1)
            proj_tok(lambda kc, t: A[:, kc, t * P:(t + 1) * P], lambda t, kc_n=kc_n: AB[0:kc_n], kc_n,
                     w_ffn_out[l], b0 * 512, 0, 4, epi_store(Rs[pi], 0, eng="dve"),
                     last_cb=(lambda t: ln_tile(t, x_src, Rs, x_dst, want_xT=not final)) if last else None)
        flush_all()

    for t in range(NTL):
        i = t % 2
        S.op("sp", lambda e, t=t, i=i: e.dma_start(out=xa[i][:], in_=xin[t * P:(t + 1) * P, :]), writes=[bf(f"xa{i}")], dkey=f"xa{i}")
        S.op("act", lambda e, i=i: e.copy(xb[0][:], xa[i][:]), reads=[bf(f"xa{i}")], writes=[bf("xb0")])
        make_xT(t, xb[0], bf("xb0"))
        flush_all()

    load_bc(0, 1)

    def gmlp_tile(t):
        var = 1 if t == NTL - 1 else 0
        rows = slice(t * P, (t + 1) * P)
        S.op("sp", lambda e: e.dma_start(out=gu[:], in_=zscr[rows, 0:2048]), reads=[bf(f"zscr_{t}")], writes=[bf("xa0")], dkey="xa0")
        S.op("sp", lambda e: e.dma_start(out=gv[:], in_=zscr[rows, 2048:4096]), reads=[bf(f"zscr_{t}")], writes=[bf("xa1")], dkey="xa1")
        stats(gv, 4, LN_EPS, [bf("xa1")])
        S.op("act", lambda e: e.activation(gv[:], gv[:], AF.Identity, bias=nmr[:], scale=rstd[:]), reads=[stat_b], writes=[bf("xa1")])
        S.op("dve", lambda e: e.tensor_tensor(gv[:], gv[:], gbc[:], ALU.mult), reads=[bf("gbc")], writes=[bf("xa1")])
        S.op("dve", lambda e: e.tensor_tensor(gv[:], gv[:], bbc[:], ALU.add), reads=[bf("bbc")], writes=[bf("xa1")])
        if var == 1:
            S.op("sp", lambda e: e.dma_start(out=vout, in_=gv[:]), reads=[bf("xa1")], writes=[bf("vout")], dkey="vout")
        S.op("act", lambda e: e.copy(gvb[:], gv[:]), reads=[bf("xa1")], writes=[bf("rb")])
        gmlp_tile_pe(t, var)

    def gmlp_tile_pe(t, var):
        for q4 in range(4):
            bk, bb = next_bank()

            def pe(e, q4=q4, bk=bk):
                ins = None
                for gg in range(2):
                    g = q4 * 2 + gg
                    ins = e.matmul(bk[:, gg * 256:(gg + 1) * 256], WgT[:, var, g, :], gvb[:, g * 256:(g + 1) * 256],
                                   start=True, stop=True)
                return ins

            S.op("pe", pe, reads=[bf("WgT"), bf("rb")], writes=[bb])
            for gg in range(2):
                g = q4 * 2 + gg
                S.op("dve", lambda e, g=g, gg=gg, bk=bk: e.scalar_tensor_tensor(
                    gs[:, g * 256:(g + 1) * 256], bk[:, gg * 256:(gg + 1) * 256], bcol[:, var, g:g + 1],
                    gu[:, g * 256:(g + 1) * 256], ALU.add, ALU.mult), reads=[bb, bf("xa0"), bf("bcol")], writes=[bf("xb0")])

        transposes(lambda j: gs[:, j * P:(j + 1) * P], 16,
                   lambda g0, cnt: A[:, g0:g0 + cnt, t * P:(t + 1) * P], [bf("xb0")], [bf(f"sT{t}")])

    proj_tok(lambda kc, t: xT[:, kc, t * P:(t + 1) * P], lambda t: [xTB[t]], 16, w_a_in, 0, 0, 8,
             epi_store(zscr, 0, AF.Gelu), last_cb=gmlp_tile)
    flush_all()
    load_bc(2, 3)
    proj_tok(lambda kc, t: A[:, kc, t * P:(t + 1) * P], lambda t: [bf(f"sT{t}")], 16, w_a_out, 0, 0, 4,
             epi_store(Rs[0], 0, eng="dve"), last_cb=lambda t: ln_tile(t, xin, [Rs[0]], xs[0]))
    flush_all()
    ffn(0, xs[0], 4, xs[1], final=False)

    def epi_rope(dst, col0, dcol0):
        def epi(b, t, bk, bb):
            i = next_stg()
            i2 = next_stg()
            typ = 1 if t == NTL - 1 else 0
            for hh in range(2):
                h = 2 * b + hh
                dc = typ * 16 + dcol0 + h
                S.op("act", lambda e, hh=hh, dc=dc: e.activation(stg[i][:, hh * 256:(hh + 1) * 256], bk[:, hh * 256:(hh + 1) * 256],
                                                                  AF.Identity, scale=dqk[:, dc:dc + 1]),
                     reads=[bb, const_b], writes=[stgB[i]])
            cosb = cs[:, t:t + 1, 0:128].to_broadcast([P, 2, P])
            sinb = cs[:, t:t + 1, 128:256].to_broadcast([P, 2, P])
            xv = stg[i][:].rearrange("p (h two f) -> p h two f", two=2, f=P)
            tv = stg[i2][:].rearrange("p (h two f) -> p h two f", two=2, f=P)
            ov = stg[i2][:].bitcast(BF16)[:, 0:512].rearrange("p (h two f) -> p h two f", two=2, f=P)
            x1, x2 = xv[:, :, 0, :], xv[:, :, 1, :]
            t1, t2 = tv[:, :, 0, :], tv[:, :, 1, :]

            def f(e):
                e.tensor_tensor(t1, x1, cosb, ALU.mult)
                e.tensor_tensor(t2, x2, sinb, ALU.mult)
                return e.tensor_tensor(t1, t1, t2, ALU.subtract)

            def f2(e):
                e.tensor_tensor(t2, x1, sinb, ALU.mult)
                e.tensor_tensor(x1, x2, cosb, ALU.mult)
                return e.tensor_tensor(t2, t2, x1, ALU.add)

            S.op("dve", lambda e: e.tensor_tensor(t1, x1, cosb, ALU.mult), reads=[stgB[i], const_b], writes=[stgB[i2]])
            S.op("dve", lambda e: e.tensor_tensor(t2, x2, sinb, ALU.mult), reads=[stgB[i], const_b], writes=[stgB[i2]])
            S.op("dve", lambda e: e.tensor_tensor(t1, t1, t2, ALU.subtract), reads=[stgB[i2]], writes=[stgB[i2]])
            S.op("dve", lambda e: e.tensor_tensor(t2, x1, sinb, ALU.mult), reads=[stgB[i], const_b], writes=[stgB[i2]])
            S.op("dve", lambda e: e.tensor_tensor(x1, x2, cosb, ALU.mult), reads=[stgB[i], const_b], writes=[stgB[i]])
            S.op("dve", lambda e: e.tensor_tensor(t2, t2, x1, ALU.add), reads=[stgB[i], stgB[i2]], writes=[stgB[i2]])
            sv = stg[i][:].bitcast(BF16)[:, 0:512]
            S.op("act", lambda e: e.copy(sv, stg[i2][:]), reads=[stgB[i2]], writes=[stgB[i]])
            c = col0 + 512 * b
            S.op("sp", lambda e: e.dma_start(out=dst[t * P:(t + 1) * P, c:c + 512], in_=sv),
                 reads=[stgB[i]], writes=[bf(f"{dst.name}_{t}")], dkey=f"stg{i}")
        return epi

    xTf = lambda kc, t: xT[:, kc, t * P:(t + 1) * P]
    xTb = lambda t: [xTB[t]]
    proj_tok(xTf, xTb, 16, w_b_in, 0, 2048, 4, epi_rope(kd, 0, 8))
    proj_tok(xTf, xTb, 16, w_b_in, 0, 4096, 8, epi_store_bf(vd, 0))
    proj_tok(xTf, xTb, 16, w_b_in, 0, 0, 4, epi_rope(qd, 0, 0))
    proj_tok(xTf, xTb, 16, w_b_in, 0, 8192, 8, epi_store_bf(gd, 0, AF.Silu))
    flush_all()

    SBh = [bf(f"S{hl}") for hl in range(4)]

    def ret_tile(t, hg, ri, mode):
        rows = slice(t * P, (t + 1) * P)
        typ = 1 if t == NTL - 1 else 0
        full = (mode == "full")
        yi = t % 2
        yin = yin2[yi]
        yinB = bf(f"yin{yi}")
        S.op("sp", lambda e: e.dma_start(out=rk[ri][:], in_=kd[rows, hg * 1024:(hg + 1) * 1024]), reads=[bf(f"kd_{t}")], writes=[bf(f"rk{ri}")], dkey=f"rk{ri}")
        S.op("sp", lambda e: e.dma_start(out=rv[ri][:], in_=vd[rows, hg * 2048:(hg + 1) * 2048]), reads=[bf(f"vd_{t}")], writes=[bf(f"rv{ri}")], dkey=f"rv{ri}")
        if full:
            S.op("sp", lambda e: e.dma_start(out=rq[ri][:], in_=qd[rows, hg * 1024:(hg + 1) * 1024]), reads=[bf(f"qd_{t}")], writes=[bf(f"rq{ri}")], dkey=f"rq{ri}")
            S.op("sp", lambda e: e.dma_start(out=rg[ri][:], in_=gd[rows, hg * 2048:(hg + 1) * 2048]), reads=[bf(f"gd_{t}")], writes=[bf(f"rg{ri}")], dkey=f"rg{ri}")
            transposes(lambda j: rq[ri][:, j * P:(j + 1) * P], 8, lambda g0, cnt: qT[:, g0:g0 + cnt, :], [bf(f"rq{ri}")], [bf("qT")], evac="act")
            transposes(lambda j: rk[ri][:, j * P:(j + 1) * P], 8, lambda g0, cnt: kT[:, g0:g0 + cnt, :], [bf(f"rk{ri}")], [bf("kT")], evac="act")
        if typ == 0:
            def state_update(dsb=None):
                for hl in range(4):
                    for dc in range(2):
                        bd, bdb = next_bank() if dsb is None else dsb[(2 * hl + dc) % len(dsb)]
                        S.op("pe", lambda e, hl=hl, dc=dc, bd=bd: e.matmul(bd[:], rk[ri][:, hl * 256 + dc * P: hl * 256 + (dc + 1) * P],
                                                                         rv[ri][:, hl * 512:(hl + 1) * 512], start=True, stop=True),
                             reads=[bf(f"rk{ri}"), bf(f"rv{ri}")], writes=[bdb])
                        j = 2 * hl + dc
                        S.op("dve", lambda e, j=j, bd=bd: e.tensor_tensor(Sf[:, j, :], Sf[:, j, :], bd[:], ALU.add),
                             reads=[bdb], writes=[SBh[hl]])
                    gL = GAM[hg * 4 + hl] ** 128
                    S.op("dve", lambda e, hl=hl, gL=gL: e.tensor_scalar_mul(Sf[:, 2 * hl:2 * hl + 2, :], Sf[:, 2 * hl:2 * hl + 2, :], float(gL)),
                         writes=[SBh[hl]])
                if full:
                    S.op("act", lambda e: e.copy(Sb[:].rearrange("p a b -> p (a b)"), Sf[:].rearrange("p a b -> p (a b)")),
                         reads=SBh, writes=[bf("Sb")])
            if full:
                ring = [next_bank() for _ in range(8)]
                bs, bsb = ring[0]

                def pe_sc(e, bs=bs):
                    ins = None
                    for hl in range(4):
                        e.matmul(bs[:, hl * P:(hl + 1) * P], kT[:, 2 * hl, :], qT[:, 2 * hl, :], start=True, stop=False)
                        ins = e.matmul(bs[:, hl * P:(hl + 1) * P], kT[:, 2 * hl + 1, :], qT[:, 2 * hl + 1, :], start=False, stop=True)
                    return ins

                S.op("pe", pe_sc, reads=[bf("qT"), bf("kT")], writes=[bsb])
                S.op("dve", lambda e, bs=bs: e.tensor_tensor(sc4[:], bs[:].rearrange("p (h t) -> p h t", t=P),
                                                             mask[:, 0:1, :].to_broadcast([P, 4, P]), ALU.mult),
                     reads=[bsb, const_b], writes=[bf("sc4")])
                obanks = ring[4:8]
                for hl in range(4):
                    bo = obanks[hl][0]

                    def pe_o(e, hl=hl, bo=bo):
                        e.matmul(bo[:], sc4[:, hl, :], rv[ri][:, hl * 512:(hl + 1) * 512], start=True, stop=False)
                        e.matmul(bo[:], qT[:, 2 * hl, :], Sb[:, 2 * hl, :], start=False, stop=False)
                        return e.matmul(bo[:], qT[:, 2 * hl + 1, :], Sb[:, 2 * hl + 1, :], start=False, stop=True)

                    S.op("pe", pe_o, reads=[bf("sc4"), bf(f"rv{ri}"), bf("qT"), bf("Sb")], writes=[obanks[hl][1]])

                state_update(ring[1:4])

                def fstats(e):
                    ins = None
                    for hl in range(4):
                        ins = e.bn_stats(st6[:, hl, :], obanks[hl][0][:])
                    return ins

                def faggr(e):
                    ins = None
                    for hl in range(4):
                        ins = e.bn_aggr(mv4[:, hl, :], st6[:, hl, :])
                    return ins

                S.op("dve", fstats, reads=[ob[1] for ob in obanks], writes=[stat_b])
                S.op("dve", faggr, reads=[stat_b], writes=[stat_b])
                S.op("dve", lambda e: e.tensor_scalar_add(rstd4[:], mv4[:, :, 1], GN_EPS), reads=[stat_b], writes=[stat_b])
                S.op("act", lambda e: e.activation(rstd4[:], rstd4[:], AF.Sqrt), reads=[stat_b], writes=[stat_b])
                S.op("dve", lambda e: e.reciprocal(rstd4[:], rstd4[:]), reads=[stat_b], writes=[stat_b])
                S.op("dve", lambda e: e.scalar_tensor_tensor(nmr4[:], mv4[:, :, 0], -1.0, rstd4[:], ALU.mult, ALU.mult),
                     reads=[stat_b], writes=[stat_b])
                for hl in range(4):
                    S.op("act", lambda e, hl=hl: e.activation(onr4[:, hl, :], obanks[hl][0][:], AF.Identity,
                                                              bias=nmr4[:, hl:hl + 1], scale=rstd4[:, hl:hl + 1]),
                         reads=[obanks[hl][1], stat_b], writes=[ugB[0], ugB[1]])
                S.op("dve", lambda e: e.tensor_tensor(yin[:], onr4[:].rearrange("p a b -> p (a b)"), rg[ri][:], ALU.mult),
                     reads=[ugB[0], ugB[1], bf(f"rg{ri}")], writes=[yinB])
            else:
                state_update()
        else:
            assert full
            S.op("dve", lambda e: e.memset(qTz[:].rearrange("p a b c -> p (a b c)"), 0.0), writes=[bf("qTz")])
            for i in range(4):
                S.op("dve", lambda e, i=i: e.tensor_copy(qTz[:, i, :, 32 * i:32 * i + 32], qT[:, :, 32 * i:32 * i + 32]),
                     reads=[bf("qT")], writes=[bf("qTz")])
            for hl in range(4):
                bs, bsb = next_bank()

                def pe_sc(e, hl=hl, bs=bs):
                    e.matmul(bs[:, 0:P], kT[:, 2 * hl, :], qT[:, 2 * hl, :], start=True, stop=False)
                    return e.matmul(bs[:, 0:P], kT[:, 2 * hl + 1, :], qT[:, 2 * hl + 1, :], start=False, stop=True)

                S.op("pe", pe_sc, reads=[bf("qT"), bf("kT")], writes=[bsb])
                S.op("dve", lambda e, bs=bs: e.tensor_tensor(sc[:], bs[:, 0:P], mask[:, 1, :], ALU.mult), reads=[bsb, const_b], writes=[bf("sc")])
                bo, bob = next_bank()
                S.op("pe", lambda e, hl=hl, bo=bo: e.matmul(bo[:], sc[:], rv[ri][:, hl * 512:(hl + 1) * 512], start=True, stop=True),
                     reads=[bf("sc"), bf(f"rv{ri}")], writes=[bob])
                S.op("act", lambda e, hl=hl, bo=bo: e.copy(osb[:, hl, :], bo[:]), reads=[bob], writes=[bf("osb")])
            for i in range(4):
                r0 = (i * 8 + hg * 4) * 256
                S.op("sp", lambda e, r0=r0: e.dma_start(out=Sf[:], in_=s0in[r0:r0 + 1024, :].rearrange("(j p) v -> p j v", p=P)),
                     writes=SBh, dkey="Sf")
                S.op("act", lambda e: e.copy(Sb[:], Sf[:]), reads=SBh, writes=[bf("Sb")])
                S.op("dve", lambda e, i=i: e.tensor_scalar_mul(ktz[:], rk[ri][:], rowm[:, i:i + 1]), reads=[bf(f"rk{ri}"), const_b], writes=[bf("ktz")])
                for hl in range(4):
                    h = hg * 4 + hl
                    gL = GAM[h] ** 32
                    bo, bob = next_bank()

                    def pe_c(e, hl=hl, bo=bo, i=i):
                        e.matmul(bo[:], qTz[:, i, 2 * hl, :], Sb[:, 2 * hl, :], start=True, stop=False)
                        return e.matmul(bo[:], qTz[:, i, 2 * hl + 1, :], Sb[:, 2 * hl + 1, :], start=False, stop=True)

                    S.op("pe", pe_c, reads=[bf("qTz"), bf("Sb")], writes=[bob])
                    S.op("dve", lambda e, hl=hl, bo=bo: e.tensor_tensor(osb[:, hl, :], osb[:, hl, :], bo[:], ALU.add), reads=[bob], writes=[bf("osb")])
                    for dc in range(2):
                        bd, bdb = next_bank()
                        S.op("pe", lambda e, hl=hl, dc=dc, bd=bd: e.matmul(bd[:], ktz[:, hl * 256 + dc * P: hl * 256 + (dc + 1) * P],
                                                                         rv[ri][:, hl * 512:(hl + 1) * 512], start=True, stop=True),
                             reads=[bf("ktz"), bf(f"rv{ri}")], writes=[bdb])
                        j = 2 * hl + dc
                        S.op("dve", lambda e, j=j, bd=bd: e.tensor_tensor(Sf[:, j, :], Sf[:, j, :], bd[:], ALU.add), reads=[bdb], writes=SBh)
                        S.op("dve", lambda e, j=j, gL=gL: e.tensor_scalar_mul(Sf[:, j, :], Sf[:, j, :], float(gL)), reads=SBh, writes=SBh)
                S.op("sp", lambda e, r0=r0: e.dma_start(out=sout_s[r0:r0 + 1024, :].rearrange("(j p) v -> p j v", p=P), in_=Sf[:]),
                     reads=SBh, writes=[bf("sout_s")], dkey=f"Sfo{i}")
            for hl in range(4):
                gn_head(None, None, hl, ri, yin, yinB)
        if full:
            def go():
                transposes(lambda j: yin[:, j * P:(j + 1) * P], 16,
                           lambda g0, cnt: A[:, g0:g0 + cnt, t * P:(t + 1) * P], [yinB], [bf(f"sT{t}")])
            ret_deferred.append(go)

    ret_deferred = []

    def gn_head(bo, bob, hl, ri, yin, yinB):
        if bo is not None:
            src, sb_ = bo, [bob]
        else:
            src, sb_ = osb[:, hl, :], [bf("osb")]
        stats(src, 1, GN_EPS, sb_)
        S.op("act", lambda e: e.activation(onr[:], src[:] if bo is not None else src, AF.Identity, bias=nmr[:], scale=rstd[:]),
             reads=sb_ + [stat_b], writes=[bf("onr")])
        S.op("dve", lambda e: e.tensor_tensor(yin[:, hl * 512:(hl + 1) * 512], onr[:], rg[ri][:, hl * 512:(hl + 1) * 512], ALU.mult),
             reads=[bf("onr"), bf(f"rg{ri}")], writes=[yinB])

    region_c_barrier(LN_NAMES, RET_NAMES)
    S.op("dve", lambda e: e.memset(sc[:, 0:2], 0.0), writes=xTB + SBh + [bf("Sb"), bf("osb"), bf("sc")])

    if exchange:
        for hg in range(2):
            S.op("dve", lambda e: e.memset(Sf[:].rearrange("p a b -> p (a b)"), 0.0), writes=SBh)
            for t in range(NPT):
                ret_tile(t, hg, t % 2, "state")
            for hl in range(4):
                h = hg * 4 + hl
                S.op("sp", lambda e, h=h, hl=hl: e.dma_start(out=sxh[h].rearrange("(j p) v -> p j v", p=P), in_=Sf[:, 2 * hl:2 * hl + 2, :]),
                     reads=SBh, writes=[bf(f"sx{h}")], dkey=f"sx{h}")
                S.op("pool", lambda e, h=h: e.collective_compute("AllGather", ALU.bypass, replica_groups=[[0, 1], [2, 3], [4, 5], [6, 7]],
                                                                 ins=[sxh[h]], outs=[gxh[h]]), reads=[bf(f"sx{h}")], writes=[bf(f"gx{h}")])

    for hg in range(2):
        if exchange:
            for hl in range(4):
                h = hg * 4 + hl
                S.op("sp", lambda e, h=h, hl=hl: e.dma_start(out=Sf[:, 2 * hl:2 * hl + 2, :], in_=gxh[h][0:256, :].rearrange("(j p) v -> p j v", p=P)),
                     reads=[bf(f"gx{h}")], writes=SBh, dkey="Sf")
            S.op("dve", lambda e: e.tensor_scalar_mul(Sf[:].rearrange("p a b -> p (a b)"), Sf[:].rearrange("p a b -> p (a b)"), flag[:, 0:1]),
                 reads=[const_b], writes=SBh)
        else:
            S.op("dve", lambda e: e.memset(Sf[:].rearrange("p a b -> p (a b)"), 0.0), writes=SBh)
        S.op("act", lambda e: e.copy(Sb[:].rearrange("p a b -> p (a b)"), Sf[:].rearrange("p a b -> p (a b)")), reads=SBh, writes=[bf("Sb")])
        for t in range(NTL):
            if t == NPT:
                S.op("sp", lambda e, hg=hg: e.dma_start(out=sout_p[hg * 1024:(hg + 1) * 1024, :].rearrange("(j p) v -> p j v", p=P), in_=Sf[:]),
                     reads=SBh, writes=[bf("sout_p")], dkey=f"sop{hg}")
            ret_tile(t, hg, t % 2, "full")
            while len(ret_deferred) > 1:
                ret_deferred.pop(0)()
        while ret_deferred:
            ret_deferred.pop(0)()
        last = (hg == 1)
        if last:
            region_c_barrier(RET_NAMES, LN_NAMES)
            S.op("dve", lambda e: e.memset(sc[:, 0:2], 0.0), writes=SBh + [bf("Sb"), bf("osb")] + xTB + [bf("sc")])
            load_bc(6, 7)
        proj_tok(lambda kc, t: A[:, kc, t * P:(t + 1) * P], lambda t: [bf(f"sT{t}")], 16, w_b_out, hg * 2048, 0, 4,
                 epi_store(Rs[hg], 0, eng="dve"),
                 last_cb=(lambda t: ln_tile(t, xs[1], [Rs[0], Rs[1]], xs[2])) if last else None)
    flush_all()
    ffn(1, xs[2], 8, yout, final=True)

    outs = [bf(f"yout_{t}") for t in range(NTL)] + [bf("sout_s"), bf("sout_p"), bf("vout")]
    S.op("sp", lambda e: e.nop(), reads=outs)
    with nc.Block() as block:
        S.emit(block)
    return nc


def _consts(half):
    c = {}
    c["c_ident"] = np.eye(P, dtype=np.float32)
    half_d = 128
    inv = np.power(np.float32(10000.0), -np.arange(half_d, dtype=np.float32) / np.float32(half_d)).astype(np.float32)
    cs = np.zeros((P, NTL, 256), np.float32)
    for t in range(NTL):
        if t < NPT:
            pos = (half * 1024 + t * P + np.arange(P)).astype(np.float32)
        else:
            pos = (4096 + (np.arange(P) % 32)).astype(np.float32)
        ang = (pos[:, None] * inv[None, :]).astype(np.float32)
        cs[:, t, 0:128] = np.cos(ang)
        cs[:, t, 128:256] = np.sin(ang)
    c["c_cs"] = cs.reshape(P, NTL * 256)
    dqk = np.zeros((P, 32), np.float64)
    for typ in range(2):
        tl = np.arange(P) if typ == 0 else (np.arange(P) % 32)
        for h in range(8):
            lg = np.log1p(-2.0 ** (-5.0 - h))
            dqk[:, typ * 16 + h] = np.exp(lg * (tl + 1.0)) * (256.0 ** -0.5)
            dqk[:, typ * 16 + 8 + h] = np.exp(-lg * (tl + 1.0))
    c["c_dqk"] = dqk.astype(np.float32)
    s = np.arange(P)[:, None]
    t = np.arange(P)[None, :]
    m0 = (s <= t).astype(np.float32)
    m1 = ((s <= t) & ((s // 32) == (t // 32))).astype(np.float32)
    c["c_mask"] = np.concatenate([m0, m1], axis=1)
    rowm = np.zeros((P, 4), np.float32)
    for i in range(4):
        rowm[32 * i:32 * i + 32, i] = 1.0
    c["c_rowm"] = rowm
    c["c_flag"] = np.full((P, 1), float(half), np.float32)
    return c


_NC_CACHE = {}


def kernel(x_prompt, x_sample, state_ret, w_a_in, a_ln_g, a_ln_b, a_ws, a_bs, w_a_out,
           w_b_in, w_b_out, w_ffn_in, w_ffn_out, ln_mix_g, ln_mix_b, ln_ffn_g, ln_ffn_b):
    f = lambda a: np.ascontiguousarray(np.asarray(a, dtype=np.float32))
    x_prompt, x_sample, state_ret = f(x_prompt), f(x_sample), f(state_ret)
    lnv = np.stack([f(a_ln_g)[0], f(a_ln_b)[0], f(ln_mix_g)[0], f(ln_mix_b)[0], f(ln_ffn_g)[0], f(ln_ffn_b)[0],
                    f(ln_mix_g)[1], f(ln_mix_b)[1], f(ln_ffn_g)[1], f(ln_ffn_b)[1]], axis=0)
    shared = {
        "w_a_in": f(w_a_in)[0], "w_a_out": f(w_a_out)[0], "w_b_in": f(w_b_in)[0], "w_b_out": f(w_b_out)[0],
        "w_ffn_in0": f(w_ffn_in)[0], "w_ffn_in1": f(w_ffn_in)[1], "w_ffn_out0": f(w_ffn_out)[0], "w_ffn_out1": f(w_ffn_out)[1],
        "a_ws": f(a_ws)[0].reshape(8 * 128, 128), "a_bs": f(a_bs)[0], "lnv": lnv,
    }
    in_maps = []
    for c in range(8):
        b, half = c // 2, c % 2
        m = dict(shared)
        m["xin"] = np.concatenate([x_prompt[b, half * 1024:(half + 1) * 1024], x_sample[4 * c:4 * c + 4].reshape(128, D)], axis=0)
        m["s0in"] = state_ret[0, 4 * c:4 * c + 4].reshape(4 * 8 * 256, 512)
        m.update(_consts(half))
        in_maps.append(m)
    if "nc" not in _NC_CACHE:
        _NC_CACHE["nc"] = build_nc()
    res = run_bass_kernel_spmd(_NC_CACHE["nc"], in_maps, core_ids=list(range(8)))
    r = res.results
    y_prompt = np.zeros((4, 2048, D), np.float32)
    y_sample = np.zeros((32, 32, D), np.float32)
    rsp = np.zeros((1, 4, 8, 256, 512), np.float32)
    rss = np.zeros((1, 32, 8, 256, 512), np.float32)
    gv = np.zeros((1, 32, 32, D), np.float32)
    for c in range(8):
        b, half = c // 2, c % 2
        y = np.asarray(r[c]["yout"])
        y_prompt[b, half * 1024:(half + 1) * 1024] = y[:1024]
        y_sample[4 * c:4 * c + 4] = y[1024:].reshape(4, 32, D)
        if half == 1:
            rsp[0, b] = np.asarray(r[c]["sout_p"]).reshape(8, 256, 512)
        rss[0, 4 * c:4 * c + 4] = np.asarray(r[c]["sout_s"]).reshape(4, 8, 256, 512)
        gv[0, 4 * c:4 * c + 4] = np.asarray(r[c]["vout"]).reshape(4, 32, D)
    return (y_prompt, y_sample, rsp, rss, gv)
```
